# Optimizing a Trainium2 kernel written in Bass

```python
import math
import jax, jax.numpy as jnp
from jax import lax
import numpy as np

D_MODEL = 1024
BATCH = 8
SEQ = 4096
DEPTH = 2

CHUNK = 64
Q_BLOCK = 128
N_EVEN = (DEPTH + 1) // 2
N_ODD = DEPTH // 2
DN_ALPHA = (2.0 * DEPTH) ** 0.25
DN_BETA = (8.0 * DEPTH) ** -0.25
LN_EPS = 1e-5
RMS_EPS = 1e-6

GLA_HEADS = 4
GLA_DK = 64
GLA_DV = 128
GLA_GATE_RANK = 16
GLA_TAU = 16.0
GLA_QK = GLA_HEADS * GLA_DK
GLA_V = GLA_HEADS * GLA_DV
GLA_SPLITS = (GLA_QK, GLA_QK, GLA_V, GLA_V, GLA_GATE_RANK)
GLA_IN = sum(GLA_SPLITS)
GLA_OFFSETS = tuple(int(o) for o in np.cumsum(GLA_SPLITS)[:-1])

RW_HEADS = 8
RW_N = 64
RW_W = RW_HEADS * RW_N
RW_DECAY_RANK = 64
RW_A_RANK = 64
RW_GATE_RANK = 160
RW_GN_EPS = 64e-5
RW_SPLITS = (RW_W, RW_W, RW_W, RW_DECAY_RANK, RW_A_RANK, RW_GATE_RANK)
RW_IN = sum(RW_SPLITS)
RW_OFFSETS = tuple(int(o) for o in np.cumsum(RW_SPLITS)[:-1])

EVEN_IN = GLA_IN + RW_IN
EVEN_MIX = GLA_V + RW_W

MLA_HEADS = 16
MLA_NOPE = 64
MLA_ROPE = 32
MLA_V = 64
MLA_Q_RANK = 768
MLA_KV_RANK = 256
ODD_IN = MLA_Q_RANK + MLA_KV_RANK + MLA_ROPE
MLA_MIX = MLA_HEADS * MLA_V
ROPE_THETA = 10000.0

FFN_HIDDEN = math.ceil(8 * D_MODEL / 3 / 256) * 256

kernel_name = "hybrid_gla_rwkv7_mla_deepnorm_trunk"


def layer_norm(x, g, b):
    xf = x.astype(jnp.float32)
    mu = xf.mean(-1, keepdims=True)
    var = jnp.square(xf - mu).mean(-1, keepdims=True)
    return ((xf - mu) * lax.rsqrt(var + LN_EPS) * g + b).astype(x.dtype)


def rms_norm(x, g):
    xf = x.astype(jnp.float32)
    return (xf * lax.rsqrt(jnp.mean(xf * xf, -1, keepdims=True) + RMS_EPS) * g).astype(x.dtype)


def token_shift(t):
    return jnp.pad(t[:, :-1], ((0, 0), (1, 0), (0, 0)))


def gla_chunked(q, k, v, log_a):
    B, S, H, DK = q.shape
    DV = v.shape[-1]
    n = S // CHUNK

    def to_chunks(t):
        return t.astype(jnp.float32).reshape(B, n, CHUNK, H, t.shape[-1]).transpose(1, 0, 3, 2, 4)

    qc, kc, vc, gc = to_chunks(q), to_chunks(k), to_chunks(v), to_chunks(log_a)
    causal = jnp.tril(jnp.ones((CHUNK, CHUNK), bool))[None, None, :, :, None]

    def step(state, inp):
        qi, ki, vi, gi = inp
        b = jnp.cumsum(gi, axis=2)
        o_inter = jnp.einsum('bhck,bhkv->bhcv', qi * jnp.exp(b), state)
        rel = jnp.where(causal, b[:, :, :, None, :] - b[:, :, None, :, :], -jnp.inf)
        attn = jnp.einsum('bhik,bhjk,bhijk->bhij', qi, ki, jnp.exp(rel))
        o_intra = jnp.einsum('bhij,bhjv->bhiv', attn, vi)
        b_last = b[:, :, -1:, :]
        new_state = (jnp.exp(b_last[:, :, 0, :])[..., None] * state
                     + jnp.einsum('bhck,bhcv->bhkv', ki * jnp.exp(b_last - b), vi))
        return new_state, o_inter + o_intra

    s0 = jnp.zeros((B, H, DK, DV), jnp.float32)
    _, o = lax.scan(step, s0, (qc, kc, vc, gc))
    return o.transpose(1, 0, 3, 2, 4).reshape(B, S, H, DV)


def rwkv7_scan(r, w, k, v, a_vec, b_vec):
    B, S, H, N = r.shape

    def tm(t):
        return t.astype(jnp.float32).transpose(1, 0, 2, 3)

    def step(state, inp):
        rt, wt, kt, vt, at, bt = inp
        sa = jnp.einsum('bhvk,bhk->bhv', state, at)
        state = state * wt[:, :, None, :] + sa[..., None] * bt[:, :, None, :] + vt[..., None] * kt[:, :, None, :]
        return state, jnp.einsum('bhvk,bhk->bhv', state, rt)

    s0 = jnp.zeros((B, H, N, N), jnp.float32)
    _, y = lax.scan(step, s0, (tm(r), tm(w), tm(k), tm(v), tm(a_vec), tm(b_vec)))
    return y.transpose(1, 0, 2, 3)


def even_mixer(x, w_in, gla_gate_w2, gla_gate_b, gla_norm_g, rw_mu, rw_w0, rw_w2, rw_a0, rw_a2,
               rw_g2, rw_k_k, rw_k_a, rw_r_k, rw_ln_g, rw_ln_b, w_out):
    B, S, _ = x.shape
    p = x @ w_in
    p_gla, p_rw = p[..., :GLA_IN], p[..., GLA_IN:]

    def heads(t, d):
        return t.reshape(B, S, -1, d)

    gq, gk, gv, gg, glr = jnp.split(p_gla, GLA_OFFSETS, axis=-1)
    log_a = jax.nn.log_sigmoid((glr @ gla_gate_w2 + gla_gate_b).astype(jnp.float32)) / GLA_TAU
    o_a = gla_chunked(heads(gq, GLA_DK) * GLA_DK ** -0.5, heads(gk, GLA_DK),
                      heads(gv, GLA_DV), heads(log_a, GLA_DK))
    o_a = (rms_norm(o_a, gla_norm_g) * jax.nn.silu(heads(gg, GLA_DV).astype(jnp.float32)))
    o_a = o_a.reshape(B, S, GLA_V).astype(x.dtype)

    p_rw = p_rw + rw_mu * (token_shift(p_rw) - p_rw)
    r, k, v, wl, al, gl = jnp.split(p_rw, RW_OFFSETS, axis=-1)
    w = -jax.nn.softplus(-(rw_w0 + jnp.tanh(wl) @ rw_w2)) - 0.5
    decay = jnp.exp(-jnp.exp(w.astype(jnp.float32)))
    a = jax.nn.sigmoid(rw_a0 + al @ rw_a2)
    g = jax.nn.sigmoid(gl) @ rw_g2
    kk = heads(k * rw_k_k, RW_N).astype(jnp.float32)
    kk = kk * lax.rsqrt(jnp.maximum(jnp.sum(kk * kk, -1, keepdims=True), 1e-24))
    k = k * (1 + (a - 1) * rw_k_a)
    r_h, k_h, v_h, a_h = heads(r, RW_N), heads(k, RW_N), heads(v, RW_N), heads(a, RW_N)
    y = rwkv7_scan(r_h, heads(decay, RW_N), k_h, v_h, -kk, kk * a_h)
    mu = y.mean(-1, keepdims=True)
    var = jnp.square(y - mu).mean(-1, keepdims=True)
    y = ((y - mu) * lax.rsqrt(var + RW_GN_EPS)).reshape(B, S, RW_W) * rw_ln_g + rw_ln_b
    bonus = jnp.sum(r_h * k_h * rw_r_k, -1, keepdims=True) * v_h
    o_b = ((heads(y, RW_N) + bonus).reshape(B, S, RW_W) * g).astype(x.dtype)

    return jnp.concatenate([o_a, o_b], axis=-1) @ w_out


def apply_rope(t, cos, sin):
    t2 = t.astype(jnp.float32).reshape(t.shape[:-1] + (-1, 2))
    x1, x2 = t2[..., 0], t2[..., 1]
    out = jnp.stack([x1 * cos - x2 * sin, x1 * sin + x2 * cos], axis=-1)
    return out.reshape(t.shape).astype(t.dtype)


def block_causal_mla_attention(q_nope, q_pe, k_nope, k_pe, v):
    B, S, H, _ = q_nope.shape
    scale = (MLA_NOPE + MLA_ROPE) ** -0.5
    key_chunk = jnp.arange(S) // CHUNK

    def block(i):
        start = i * Q_BLOCK
        qn = lax.dynamic_slice_in_dim(q_nope, start, Q_BLOCK, axis=1)
        qp = lax.dynamic_slice_in_dim(q_pe, start, Q_BLOCK, axis=1)
        s = (jnp.einsum('bqhd,bkhd->bhqk', qn, k_nope)
             + jnp.einsum('bqhr,bkr->bhqk', qp, k_pe)).astype(jnp.float32) * scale
        q_chunk = (start + jnp.arange(Q_BLOCK)) // CHUNK
        s = jnp.where(key_chunk[None, :] <= q_chunk[:, None], s, -jnp.inf)
        pr = jax.nn.softmax(s, axis=-1).astype(v.dtype)
        return jnp.einsum('bhqk,bkhd->bqhd', pr, v)

    o = lax.map(block, jnp.arange(S // Q_BLOCK))
    return o.transpose(1, 0, 2, 3, 4).reshape(B, S, H, MLA_V)


def odd_mixer(x, positions, w_in, q_norm_g, w_q_b, kv_norm_g, w_kv_b, w_out):
    B, S, _ = x.shape
    p = x @ w_in
    cq = p[..., :MLA_Q_RANK]
    ckv = p[..., MLA_Q_RANK:MLA_Q_RANK + MLA_KV_RANK]
    k_pe = p[..., MLA_Q_RANK + MLA_KV_RANK:]
    q = (rms_norm(cq, q_norm_g) @ w_q_b).reshape(B, S, MLA_HEADS, MLA_NOPE + MLA_ROPE)
    kv = (rms_norm(ckv, kv_norm_g) @ w_kv_b).reshape(B, S, MLA_HEADS, MLA_NOPE + MLA_V)
    q_nope, q_pe = q[..., :MLA_NOPE], q[..., MLA_NOPE:]
    k_nope, v = kv[..., :MLA_NOPE], kv[..., MLA_NOPE:]
    inv_freq = ROPE_THETA ** (-jnp.arange(0, MLA_ROPE, 2, dtype=jnp.float32) / MLA_ROPE)
    ang = positions.astype(jnp.float32)[..., None] * inv_freq
    cos, sin = jnp.cos(ang), jnp.sin(ang)
    q_pe = apply_rope(q_pe, cos[:, :, None, :], sin[:, :, None, :])
    k_pe = apply_rope(k_pe, cos, sin)
    o = block_causal_mla_attention(q_nope, q_pe, k_nope, k_pe, v)
    return o.reshape(B, S, MLA_MIX) @ w_out


def swiglu_ffn(x, w_gate_up, w_down):
    gu = x @ w_gate_up
    gate, up = gu[..., :FFN_HIDDEN], gu[..., FFN_HIDDEN:]
    return (jax.nn.silu(gate) * up) @ w_down


def setup_inputs(seed: int = 0) -> dict:
    key = jax.random.key(seed)
    ks = iter(jax.random.split(key, 64))

    def dense(lead, fan_in, fan_out, scale=1.0):
        return jax.random.normal(next(ks), lead + (fan_in, fan_out), jnp.float32) * (scale * fan_in ** -0.5)

    def vec(shape, base, noise):
        return base + noise * jax.random.normal(next(ks), shape, jnp.float32)

    E, O = (N_EVEN,), (N_ODD,)
    x = jax.random.normal(next(ks), (BATCH, SEQ, D_MODEL), jnp.float32)
    positions = (jax.random.randint(next(ks), (BATCH, 1), 0, 4096, jnp.int32)
                 + jnp.arange(SEQ, dtype=jnp.int32)[None, :])

    even_scales = (1.0, 1.0, DN_BETA, 1.0, 1.0) + (1.0, 1.0, DN_BETA, 1.0, 1.0, 1.0)
    even_w_in = jnp.concatenate([dense(E, D_MODEL, c, s) for c, s in zip(GLA_SPLITS + RW_SPLITS, even_scales)], -1)

    mla_w_kv_b = jnp.concatenate([
        dense(O, MLA_KV_RANK, MLA_NOPE, 1.0).reshape(O + (MLA_KV_RANK, 1, MLA_NOPE)) * jnp.ones((1, MLA_HEADS, 1)),
        dense(O, MLA_KV_RANK, MLA_HEADS * MLA_V, DN_BETA).reshape(O + (MLA_KV_RANK, MLA_HEADS, MLA_V)),
    ], -1)
    mla_w_kv_b = mla_w_kv_b + 0.3 * dense(O, MLA_KV_RANK, MLA_HEADS * (MLA_NOPE + MLA_V)).reshape(mla_w_kv_b.shape) * jnp.concatenate([jnp.ones((MLA_NOPE,)), jnp.zeros((MLA_V,))])

    return {
        "x": x,
        "positions": positions,
        "even_w_in": even_w_in,
        "gla_gate_w2": dense(E, GLA_GATE_RANK, GLA_QK),
        "gla_gate_b": vec(E + (GLA_QK,), 0.0, 0.1),
        "gla_norm_g": vec(E + (GLA_DV,), 1.0, 0.02),
        "rwkv_mu": jax.random.uniform(next(ks), E + (RW_IN,), jnp.float32),
        "rwkv_w0": vec(E + (RW_W,), 0.0, 0.5),
        "rwkv_w2": dense(E, RW_DECAY_RANK, RW_W),
        "rwkv_a0": vec(E + (RW_W,), 0.0, 0.1),
        "rwkv_a2": dense(E, RW_A_RANK, RW_W),
        "rwkv_g2": dense(E, RW_GATE_RANK, RW_W),
        "rwkv_k_k": vec(E + (RW_W,), 0.85, 0.02),
        "rwkv_k_a": vec(E + (RW_W,), 1.0, 0.02),
        "rwkv_r_k": vec(E + (RW_HEADS, RW_N), 0.0, 0.1),
        "rwkv_ln_g": vec(E + (RW_W,), 1.0, 0.02),
        "rwkv_ln_b": vec(E + (RW_W,), 0.0, 0.02),
        "even_w_out": dense(E, EVEN_MIX, D_MODEL, DN_BETA),
        "mla_w_in": dense(O, D_MODEL, ODD_IN),
        "mla_q_norm_g": vec(O + (MLA_Q_RANK,), 1.0, 0.02),
        "mla_w_q_b": dense(O, MLA_Q_RANK, MLA_HEADS * (MLA_NOPE + MLA_ROPE)),
        "mla_kv_norm_g": vec(O + (MLA_KV_RANK,), 1.0, 0.02),
        "mla_w_kv_b": mla_w_kv_b.reshape(O + (MLA_KV_RANK, MLA_HEADS * (MLA_NOPE + MLA_V))),
        "mla_w_out": dense(O, MLA_MIX, D_MODEL, DN_BETA),
        "ffn_w_gate_up": dense((DEPTH,), D_MODEL, 2 * FFN_HIDDEN, DN_BETA),
        "ffn_w_down": dense((DEPTH,), FFN_HIDDEN, D_MODEL, DN_BETA),
        "ln_g": vec((DEPTH, 2, D_MODEL), 1.0, 0.02),
        "ln_b": vec((DEPTH, 2, D_MODEL), 0.0, 0.02),
    }


def reference(x, positions, even_w_in, gla_gate_w2, gla_gate_b, gla_norm_g, rwkv_mu, rwkv_w0, rwkv_w2,
              rwkv_a0, rwkv_a2, rwkv_g2, rwkv_k_k, rwkv_k_a, rwkv_r_k, rwkv_ln_g, rwkv_ln_b, even_w_out,
              mla_w_in, mla_q_norm_g, mla_w_q_b, mla_kv_norm_g, mla_w_kv_b, mla_w_out,
              ffn_w_gate_up, ffn_w_down, ln_g, ln_b):
    for i in range(DEPTH):
        j = i // 2
        if i % 2 == 0:
            h = even_mixer(x, even_w_in[j], gla_gate_w2[j], gla_gate_b[j], gla_norm_g[j], rwkv_mu[j],
                           rwkv_w0[j], rwkv_w2[j], rwkv_a0[j], rwkv_a2[j], rwkv_g2[j], rwkv_k_k[j],
                           rwkv_k_a[j], rwkv_r_k[j], rwkv_ln_g[j], rwkv_ln_b[j], even_w_out[j])
        else:
            h = odd_mixer(x, positions, mla_w_in[j], mla_q_norm_g[j], mla_w_q_b[j], mla_kv_norm_g[j],
                          mla_w_kv_b[j], mla_w_out[j])
        x = layer_norm(DN_ALPHA * x + h, ln_g[i, 0], ln_b[i, 0])
        x = layer_norm(DN_ALPHA * x + swiglu_ffn(x, ffn_w_gate_up[i], ffn_w_down[i]), ln_g[i, 1], ln_b[i, 1])
    return x
```

```python
import math
import threading
from contextlib import ExitStack
import threading
import numpy as np
import concourse.bass as bass
import concourse.mybir as mybir
from concourse.bass_utils import run_bass_kernel_spmd

F32 = mybir.dt.float32
BF16 = mybir.dt.bfloat16
I32 = mybir.dt.int32
ALU = mybir.AluOpType
AF = mybir.ActivationFunctionType
AX = mybir.AxisListType

ENGS = ("pe", "act", "dve", "pool", "sp")


class Buf:
    __slots__ = ("name", "writes", "reads")

    def __init__(self, name=""):
        self.name = name
        self.writes = []
        self.reads = []


class V:
    __slots__ = ("ap", "buf")

    def __init__(self, ap, buf):
        self.ap = ap
        self.buf = buf

    def __getitem__(self, idx):
        return V(self.ap[idx], self.buf)

    def re(self, pattern, **kw):
        return V(self.ap.rearrange(pattern, **kw), self.buf)

    def sub(self, name=""):
        return V(self.ap, Buf(name))

    def bc(self, shape):
        return V(self.ap.to_broadcast(list(shape)), self.buf)

    def unsq(self, axis):
        return V(self.ap.unsqueeze(axis), self.buf)

    def pbc(self, n):
        return V(self.ap.partition_broadcast(n), self.buf)

    def bitcast(self, dt):
        return V(self.ap.bitcast(dt), self.buf)


class DmaPart:
    def __init__(self, sem, name):
        self.sem = sem
        self.total = 0
        self.name = name


class DmaSem:
    def __init__(self, prog, name):
        self.prog = prog
        self.name = name
        self.parts = {}

    def part(self, eng):
        k = "sw" if eng == "pool" else "hw"
        if k not in self.parts:
            p = self.prog
            sem = p.stack.enter_context(p.nc.semaphore(f"ds{len(p.dsems)}_{k}_{self.name}"))
            self.parts[k] = DmaPart(sem, self.name + k)
            p.dsems.append(self.parts[k])
        return self.parts[k]

    @property
    def total(self):
        return sum(x.total for x in self.parts.values())


class Prog:
    def __init__(self, nc, stack):
        self.nc = nc
        self.stack = stack
        self.ins = {e: [] for e in ENGS}
        self.waited = {e: {} for e in ENGS}
        self.esem = {e: stack.enter_context(nc.semaphore("es_" + e)) for e in ENGS}
        self.dsems = []
        self.ntile = 0
        self._il_step = None

    ARENA = 212800

    def init_mem(self):
        nc = self.nc
        self.arena = self.stack.enter_context(nc.sbuf_tensor("arena", [128, self.ARENA], mybir.dt.uint8))
        self.banks = [self.stack.enter_context(nc.psum_tensor(f"bank{i}", [128, 512], F32)) for i in range(8)]
        self.off = 0

    def stage_begin(self):
        self.barrier()
        self.off = 0

    def sb(self, shape, dtype, name=""):
        esz = {F32: 4, BF16: 2, I32: 4}[dtype]
        n = 1
        for d in shape[1:]:
            n *= d
        nb = (n * esz + 31) // 32 * 32
        assert self.off + nb <= self.ARENA, f"SBUF arena overflow: {self.off}+{nb} ({name})"
        ap = self.arena[0:shape[0], self.off:self.off + n * esz].bitcast(dtype)
        self.off += nb
        if len(shape) == 3:
            ap = ap.rearrange("p (a b) -> p a b", a=shape[1])
        elif len(shape) == 4:
            ap = ap.rearrange("p (a b c) -> p a b c", a=shape[1], b=shape[2])
        return V(ap, Buf(name))

    def bank(self, i, dtype=F32, name=""):
        ap = self.banks[i][:, :]
        if dtype != F32:
            ap = ap.bitcast(dtype)
        return V(ap, Buf(name or f"bank{i}"))

    def dram(self, name, shape, dtype, kind="Internal"):
        t = self.nc.dram_tensor(name, list(shape), dtype, kind=kind)
        return V(t.ap(), None)

    def dsem(self, name):
        return DmaSem(self, name)

    def _deps(self, eng, reads, writes):
        toks = []
        for v in reads:
            b = v.buf if isinstance(v, V) else v
            if b is None:
                continue
            toks.extend((t, True) for t in b.writes)
        for v in writes:
            b = v.buf if isinstance(v, V) else v
            if b is None:
                continue
            toks.extend((t, False) for t in b.writes)
            toks.extend((t, False) for t in b.reads)
        waits = []
        wd = self.waited[eng]
        for tk, raw in toks:
            if tk[0] == "e":
                _, f, idx = tk
                if f == eng and (eng == "pe" or not raw):
                    continue
                if wd.get(("e", f), -1) >= idx:
                    continue
                wd[("e", f)] = idx
                self.ins[f][idx]["signal"] = True
                waits.append(tk)
            else:
                _, ds = tk
                if wd.get(("d", id(ds)), -1) >= ds.total:
                    continue
                wd[("d", id(ds))] = ds.total
                waits.append(("d", ds, ds.total))
        return waits

    def _commit(self, tok, reads, writes):
        for v in reads:
            b = v.buf if isinstance(v, V) else v
            if b is None:
                continue
            if tok[0] == "e":
                b.reads = [t for t in b.reads if not (t[0] == "e" and t[1] == tok[1])]
            elif tok in b.reads:
                continue
            b.reads.append(tok)
        for v in writes:
            b = v.buf if isinstance(v, V) else v
            if b is None:
                continue
            b.writes = [tok]
            b.reads = []

    def op(self, eng, fn, reads=(), writes=()):
        waits = self._deps(eng, reads, writes)
        idx = len(self.ins[eng])
        self.ins[eng].append(dict(fn=fn, waits=waits, signal=False, dsem=None))
        self._commit(("e", eng, idx), reads, writes)
        if self._il_step is not None:
            self._il_step()
        return idx

    def dma(self, eng, out, in_, ds, serial=False, **kw):
        ds = ds.part(eng)
        waits = self._deps(eng, [in_], [out])
        if serial and ds.total and self.waited[eng].get(("d", id(ds)), -1) < ds.total:
            self.waited[eng][("d", id(ds))] = ds.total
            waits.append(("d", ds, ds.total))
        idx = len(self.ins[eng])
        ds.total += 16
        oa, ia = out.ap, in_.ap
        self.ins[eng].append(dict(fn=lambda e: e.dma_start(out=oa, in_=ia, **kw), waits=waits,
                                  signal=False, dsem=ds))
        self._commit(("d", ds), [in_], [out])
        if self._il_step is not None:
            self._il_step()

    def interleave(self, fns):
        n = len(fns)
        import os
        if n == 1 or os.environ.get('NO_IL'):
            for f in fns:
                f()
            return
        cv = threading.Condition()
        st = {"turn": 0, "alive": [True] * n, "err": None}
        loc = threading.local()

        def nxt(i):
            for d in range(1, n + 1):
                j = (i + d) % n
                if st["alive"][j]:
                    return j
            return None

        def step():
            i = loc.idx
            with cv:
                j = nxt(i)
                if j is None or j == i:
                    return
                st["turn"] = j
                cv.notify_all()
                while st["turn"] != i:
                    cv.wait()

        def worker(i):
            loc.idx = i
            with cv:
                while st["turn"] != i:
                    cv.wait()
            try:
                fns[i]()
            except BaseException as e:
                st["err"] = e
            with cv:
                st["alive"][i] = False
                st["turn"] = nxt(i)
                cv.notify_all()

        self._il_step = step
        ths = [threading.Thread(target=worker, args=(i,)) for i in range(n)]
        for t in ths:
            t.start()
        for t in ths:
            t.join()
        self._il_step = None
        if st["err"] is not None:
            raise st["err"]

    def wdma(self, out, in_, eng="pool"):
        if not hasattr(self, "wsems"):
            self.wsems = [self.dsem(f"w{i}") for i in range(6)]
            self.wcnt = 0
        ds = self.wsems[self.wcnt % len(self.wsems)]
        self.wcnt += 1
        self.dma(eng, out, in_, ds, serial=True)

    def barrier(self):
        for e in ENGS:
            waits = []
            wd = self.waited[e]
            for f in ENGS:
                if f == e or not self.ins[f]:
                    continue
                idx = None
                for k in range(len(self.ins[f]) - 1, -1, -1):
                    if self.ins[f][k]["dsem"] is None and self.ins[f][k]["fn"] is not None:
                        idx = k
                        break
                if idx is None or wd.get(("e", f), -1) >= idx:
                    continue
                wd[("e", f)] = idx
                self.ins[f][idx]["signal"] = True
                waits.append(("e", f, idx))
            for ds in self.dsems:
                if ds.total and wd.get(("d", id(ds)), -1) < ds.total:
                    wd[("d", id(ds))] = ds.total
                    waits.append(("d", ds, ds.total))
            if waits:
                self.ins[e].append(dict(fn=None, waits=waits, signal=False, dsem=None))

    def emit(self):
        nc = self.nc
        rank = {}
        for e in ENGS:
            c = 0
            r = []
            for rec in self.ins[e]:
                if rec["signal"]:
                    c += 1
                r.append(c)
            rank[e] = r
        print("signals", {e: (rank[e][-1] if rank[e] else 0) for e in ENGS}, "dma_max", max([d.total for d in self.dsems] + [0]), flush=True)
        handles = {"pe": "tensor", "act": "scalar", "dve": "vector", "pool": "gpsimd", "sp": "sync"}

        def run(e, eng):
            for rec in self.ins[e]:
                for w in rec["waits"]:
                    if w[0] == "e":
                        eng.wait_ge(self.esem[w[1]], rank[w[1]][w[2]])
                    else:
                        eng.wait_ge(w[1].sem, w[2])
                if rec["fn"] is None:
                    continue
                ins = rec["fn"](eng)
                if rec["dsem"] is not None:
                    ins.then_inc(rec["dsem"].sem, 16)
                elif rec["signal"]:
                    ins.then_inc(self.esem[e], 1)

        with nc.Block() as block:
            for e in ENGS:
                if not self.ins[e]:
                    continue
                getattr(block, handles[e])(lambda eng, e=e: run(e, eng))

    def mm(self, out, lhsT, rhs, start=True, stop=True):
        o, l, r = out.ap, lhsT.ap, rhs.ap
        return self.op("pe", lambda e: e.matmul(o, l, r, start=start, stop=stop), [lhsT, rhs], [out])

    def tr(self, out, in_, ident):
        o, i, d = out.ap, in_.ap, ident.ap
        return self.op("pe", lambda e: e.transpose(o, i, d), [in_, ident], [out])

    def act(self, out, in_, func, bias=None, scale=None, accum=None, eng="act"):
        o, i = out.ap, in_.ap
        kw = {}
        rd = [in_]
        wr = [out]
        if bias is not None:
            if isinstance(bias, V):
                kw["bias"] = bias.ap
                rd.append(bias)
            else:
                kw["bias"] = bias
        if scale is not None:
            if isinstance(scale, V):
                kw["scale"] = scale.ap
                rd.append(scale)
            else:
                kw["scale"] = scale
        if accum is not None:
            kw["accum_out"] = accum.ap
            wr.append(accum)
        return self.op("act", lambda e: e.activation(o, i, func, **kw), rd, wr)

    def tt(self, out, a, b, op, eng="dve"):
        o, x, y = out.ap, a.ap, b.ap
        return self.op(eng, lambda e: e.tensor_tensor(o, x, y, op), [a, b], [out])

    def ts(self, out, a, s1, op0, s2=None, op1=None, eng="dve", accum=None):
        o, x = out.ap, a.ap
        rd = [a]
        wr = [out]
        a1 = s1
        if isinstance(s1, V):
            rd.append(s1)
            a1 = s1.ap
        a2 = s2
        if isinstance(s2, V):
            rd.append(s2)
            a2 = s2.ap
        kw = {}
        if op1 is not None:
            kw["op1"] = op1
        if accum is not None:
            kw["accum_out"] = accum.ap
            wr.append(accum)
        return self.op(eng, lambda e: e.tensor_scalar(o, x, a1, a2, op0, **kw), rd, wr)

    def stt(self, out, a, s, b, op0, op1, eng="dve"):
        o, x, y = out.ap, a.ap, b.ap
        rd = [a, b]
        sc = s
        if isinstance(s, V):
            rd.append(s)
            sc = s.ap
        return self.op(eng, lambda e: e.scalar_tensor_tensor(o, x, sc, y, op0, op1), rd, [out])

    def copy(self, out, in_, eng="dve"):
        o, i = out.ap, in_.ap
        if eng == "act":
            return self.op("act", lambda e: e.copy(o, i), [in_], [out])
        return self.op(eng, lambda e: e.tensor_copy(o, i), [in_], [out])

    def memset(self, out, val, eng="dve"):
        o = out.ap
        return self.op(eng, lambda e: e.memset(o, val), [], [out])

    def reduce(self, out, in_, op=None, axis=None, eng="dve"):
        o, i = out.ap, in_.ap
        op = op or ALU.add
        axis = axis or AX.X
        return self.op(eng, lambda e: e.tensor_reduce(o, i, axis, op), [in_], [out])


DN_ALPHA = 4.0 ** 0.25
LN_EPS = 1e-5
D = 1024
FH = 2816
NF = 22


def ln_epilogue(P, z, xs, ods, g_bc, b_bc, ys, tmp):
    for n in range(2):
        sl = slice(n * 512, (n + 1) * 512)
        P.stt(z[:, sl], xs[:, sl], DN_ALPHA, ods[n], ALU.mult, ALU.add)
    ln_core(P, z, g_bc, b_bc, ys, tmp)


def ln_core(P, z, g_bc, b_bc, ys, tmp):
    stats, mv, rstd = tmp["stats"], tmp["mv"], tmp["rstd"]
    for n in range(2):
        sl = slice(n * 512, (n + 1) * 512)
        so, zi = stats[:, n, :].ap, z[:, sl].ap
        P.op("dve", lambda e, so=so, zi=zi: e.bn_stats(so, zi), [z], [stats])
    mo, si = mv.ap, stats.re("p a b -> p (a b)").ap
    P.op("dve", lambda e: e.bn_aggr(mo, si), [stats], [mv])
    P.ts(rstd, mv[:, 1:2], LN_EPS, ALU.add)
    P.act(rstd, rstd, AF.Sqrt)
    ro = rstd.ap
    P.op("dve", lambda e: e.reciprocal(ro, ro), [rstd], [rstd])
    P.ts(z, z, mv[:, 0:1], ALU.subtract, rstd[:, 0:1], ALU.mult)
    P.tt(z, z, g_bc, ALU.mult, eng="pool")
    P.tt(ys, z, b_bc, ALU.add, eng="pool")


def stage_ffn(P, S, xin, wgu_d, wd_d, lng_d, lnb_d, yout, ident_d):
    P.stage_begin()
    TT = 256
    NT = S // TT
    ident = P.sb([128, 128], BF16, "ident")
    wgu = P.sb([128, 8, NF * 256], BF16, "wgu")
    wd = P.sb([128, NF, D], BF16, "wd")
    g_bc = P.sb([128, D], F32, "g_bc")
    b_bc = P.sb([128, D], F32, "b_bc")
    ds_c = P.dsem("ffn_c")
    P.dma("pool", ident, ident_d, ds_c)
    P.dma("sp", g_bc, lng_d.pbc(128), ds_c)
    P.dma("sp", b_bc, lnb_d.pbc(128), ds_c)
    wgu_chunks = [wgu[:, :, f * 512:(f + 1) * 512].sub(f"wgu{f}") for f in range(NF // 2)]
    wd_chunks = [wd[:, f * 2:(f + 1) * 2, :].sub(f"wd{f}") for f in range(NF // 2)]
    wgu_src = wgu_d.re("(k p) n -> p k n", p=128)
    wd_src = wd_d.re("(f p) n -> p f n", p=128)

    xs = [[P.sb([128, D], F32, f"xs{s}{u}") for u in range(2)] for s in range(2)]
    xb = [[P.sb([128, D], BF16, f"xb{s}{u}") for u in range(2)] for s in range(2)]
    ds_x = [P.dsem(f"ffn_x{s}") for s in range(2)]
    xT = [P.sb([128, 8, TT], BF16, f"xT{s}") for s in range(2)]
    hT = P.sb([128, NF, TT], BF16, "hT")
    sg = [P.sb([128, TT], F32, f"sg{s}") for s in range(2)]
    z = [P.sb([128, D], F32, f"z{s}") for s in range(2)]
    ys = [P.sb([128, D], F32, f"ys{s}") for s in range(2)]
    ds_y = [P.dsem(f"ffn_y{s}") for s in range(2)]
    tmp = [dict(stats=P.sb([128, 2, 6], F32), mv=P.sb([128, 2], F32), rstd=P.sb([128, 1], F32)) for _ in range(2)]
    tp = P.bank(0, BF16, "tp")
    gu = [P.bank(1 + i, F32, f"gu{i}").re("p (a b) -> p a b", a=2) for i in range(2)]
    od = [[P.bank(3 + 2 * u + n, F32, f"od{u}{n}") for n in range(2)] for u in range(2)]

    def load(t):
        s = t % 2
        for u in range(2):
            rows = slice(t * TT + u * 128, t * TT + (u + 1) * 128)
            P.dma("sp", xs[s][u], xin[rows, :], ds_x[s])
            P.dma("pool", xb[s][u], xin[rows, :], ds_x[s])

    wl = {"gu": 0, "d": 0}

    def compute(t):
        s = t % 2
        for u in range(2):
            for k in range(8):
                P.tr(tp[:, k * 128:(k + 1) * 128], xb[s][u][:, k * 128:(k + 1) * 128], ident)
            P.copy(xT[s][:, :, u * 128:(u + 1) * 128], tp.re("p (k t) -> p k t", k=8), eng="act")
        for f in range(NF):
            if t == 0 and f % 2 == 0:
                c = f // 2
                P.wdma(wgu_chunks[c], wgu_src[:, :, c * 512:(c + 1) * 512])
            wch = wgu_chunks[f // 2]
            g = gu[f % 2]
            for half in range(2):
                for k in range(8):
                    c0 = (f % 2) * 256 + half * 128
                    P.mm(g[:, half, :], wch[:, k, c0:c0 + 128], xT[s][:, k, :], start=(k == 0), stop=(k == 7))
            sgt = sg[f % 2]
            P.act(sgt, g[:, 0, :], AF.Silu)
            P.tt(hT[:, f, :], sgt, g[:, 1, :], ALU.mult)
        for u in range(2):
            for n in range(2):
                for f in range(NF):
                    if t == 0 and u == 0 and n == 0 and f % 2 == 0:
                        c = f // 2
                        P.wdma(wd_chunks[c], wd_src[:, c * 2:(c + 1) * 2, :])
                    P.mm(od[u][n], hT[:, f, u * 128:(u + 1) * 128], wd_chunks[f // 2][:, f % 2, n * 512:(n + 1) * 512],
                         start=(f == 0), stop=(f == NF - 1))
        for u in range(2):
            ln_epilogue(P, z[u], xs[s][u], od[u], g_bc, b_bc, ys[u], tmp[u])
            rows = slice(t * TT + u * 128, t * TT + (u + 1) * 128)
            P.dma("sp", yout[rows, :], ys[u], ds_y[u])

    load(0)
    for t in range(NT):
        if t + 1 < NT:
            load(t + 1)
        compute(t)


RMS_EPS = 1e-6
NH = 16
QR = 768
KVR = 256
MAGIC = 12582912.0
TWO_PI = 2.0 * math.pi


def rms_stats(P, srcs, n, eps, stats, mv, rstd):
    k = len(srcs)
    for i, v in enumerate(srcs):
        so, zi = stats[:, i, :].ap, v.ap
        P.op("dve", lambda e, so=so, zi=zi: e.bn_stats(so, zi), [v], [stats])
    mo, si = mv.ap, stats[:, 0:k, :].re("p a b -> p (a b)").ap
    P.op("dve", lambda e: e.bn_aggr(mo, si), [stats], [mv])
    P.stt(rstd, mv[:, 0:1], mv[:, 0:1], mv[:, 1:2], ALU.mult, ALU.add)
    P.ts(rstd, rstd, eps, ALU.add)
    P.act(rstd, rstd, AF.Sqrt)
    ro = rstd.ap
    P.op("dve", lambda e: e.reciprocal(ro, ro), [rstd], [rstd])


def rope(P, dst1, dst2, x1, x2, cos, sin, t):
    P.tt(t[0], x1, cos, ALU.mult)
    P.tt(t[1], x2, sin, ALU.mult)
    P.tt(dst1, t[0], t[1], ALU.subtract)
    P.tt(t[2], x1, sin, ALU.mult)
    P.tt(t[3], x2, cos, ALU.mult)
    P.tt(dst2, t[2], t[3], ALU.add)


def stage_mla_proj(P, S, xin, pos_d, invf_d, win_d, qng_d, wqb_d, kvng_d, wkvb_d, ident_d, qT_d, kT_d, vA_d):
    P.stage_begin()
    NT = S // 128
    ident = P.sb([128, 128], BF16, "ident")
    win = P.sb([128, 8, 1056], BF16, "win")
    wqb = P.sb([128, 6, 1536], BF16, "wqb")
    wkvb = P.sb([128, 2, 2048], BF16, "wkvb")
    qng = P.sb([128, QR], F32, "qng")
    kvng = P.sb([128, KVR], F32, "kvng")
    invf = P.sb([128, 16], F32, "invf")
    posi = P.sb([128, NT], I32, "posi")
    posf = P.sb([128, NT], F32, "posf")
    ang = P.sb([128, NT, 16], F32, "ang")
    tn = P.sb([128, NT, 16], F32, "tn")
    cosT = P.sb([128, NT, 16], F32, "cos")
    sinT = P.sb([128, NT, 16], F32, "sin")
    ds_c = P.dsem("mp_c")
    P.dma("pool", ident, ident_d, ds_c)
    P.dma("sp", qng, qng_d.pbc(128), ds_c)
    P.dma("sp", kvng, kvng_d.pbc(128), ds_c)
    P.dma("sp", invf, invf_d.pbc(128), ds_c)
    P.dma("sp", posi, pos_d, ds_c)
    win_src = win_d.re("(k p) n -> p k n", p=128)
    for k in range(0, 8, 2):
        P.wdma(win[:, k:k + 2, :].sub(f"win{k}"), win_src[:, k:k + 2, :])
    wqb_src = wqb_d.re("(k p) n -> p k n", p=128)
    for k in range(0, 6, 2):
        P.wdma(wqb[:, k:k + 2, :].sub(f"wqb{k}"), wqb_src[:, k:k + 2, :])
    P.wdma(wkvb.sub("wkvb"), wkvb_d.re("(k p) n -> p k n", p=128))
    P.barrier()
    P.copy(posf, posi)
    P.tt(ang, posf.unsq(2).bc([128, NT, 16]), invf.unsq(1).bc([128, NT, 16]), ALU.mult)
    for (dst, shift) in ((sinT, 0.0), (cosT, math.pi / 2)):
        a2 = ang
        if shift:
            P.ts(dst, ang, shift, ALU.add)
            a2 = dst
        P.ts(tn, a2, 1.0 / TWO_PI, ALU.mult, MAGIC, ALU.add)
        P.ts(tn, tn, MAGIC, ALU.subtract)
        P.stt(dst, tn, -TWO_PI, a2, ALU.mult, ALU.add)
        P.ts(dst, dst, 3.14159, ALU.min, -3.14159, ALU.max)
        P.act(dst, dst, AF.Sin)

    xb = [P.sb([128, D], BF16, f"xb{s}") for s in range(2)]
    ds_x = [P.dsem(f"mp_x{s}") for s in range(2)]
    xT = P.sb([128, 8, 128], BF16, "xT")
    cqn = P.sb([128, QR], BF16, "cqn")
    ckvn = P.sb([128, KVR], BF16, "ckvn")
    cqT = P.sb([128, 6, 128], BF16, "cqT")
    ckvT = P.sb([128, 2, 128], BF16, "ckvT")
    Qf = P.sb([128, NH, 96], BF16, "Qf")
    Kf = P.sb([128, NH, 96], BF16, "Kf")
    Va = [P.sb([128, NH, 65], BF16, f"Va{s}") for s in range(2)]
    ds_v = [P.dsem(f"mp_v{s}") for s in range(2)]
    kr = P.sb([128, 32], BF16, "kr")
    rt = [P.sb([128, NH, 16], F32, f"rt{i}") for i in range(4)]
    QTs = [P.sb([96, NH, 512], BF16, f"QTs{s}") for s in range(2)]
    KTs = [P.sb([96, NH, 512], BF16, f"KTs{s}") for s in range(2)]
    ds_q = [P.dsem(f"mp_q{s}") for s in range(2)]
    stats = P.sb([128, 2, 6], F32, "stats")
    mv = P.sb([128, 2], F32, "mv")
    rstd = P.sb([128, 1], F32, "rstd")
    for s in range(2):
        P.memset(Va[s][:, :, 64:65], 1.0)
    tp = P.bank(0, BF16, "tp")
    bk = [None] + [P.bank(i, F32, f"b{i}") for i in range(1, 8)]
    qT_v = qT_d.re("h d s -> d h s")
    kT_v = kT_d.re("h d s -> d h s")

    def load(t):
        P.dma("pool", xb[t % 2], xin[t * 128:(t + 1) * 128, :], ds_x[t % 2])

    def compute(t):
        s = t % 2
        for k in range(8):
            P.tr(tp[:, k * 128:(k + 1) * 128], xb[s][:, k * 128:(k + 1) * 128], ident)
        P.copy(xT, tp.re("p (k t) -> p k t", k=8), eng="act")
        for n, (c0, c1) in enumerate(((0, 512), (512, 1024), (1024, 1056))):
            for k in range(8):
                P.mm(bk[1 + n][:, 0:c1 - c0], xT[:, k, :], win[:, k, c0:c1], start=(k == 0), stop=(k == 7))
        rms_stats(P, [bk[1], bk[2][:, 0:256]], QR, RMS_EPS, stats, mv, rstd)
        P.stt(cqn[:, 0:512], bk[1], rstd[:, 0:1], qng[:, 0:512], ALU.mult, ALU.mult)
        P.stt(cqn[:, 512:768], bk[2][:, 0:256], rstd[:, 0:1], qng[:, 512:768], ALU.mult, ALU.mult)
        rms_stats(P, [bk[2][:, 256:512]], KVR, RMS_EPS, stats, mv, rstd)
        P.stt(ckvn, bk[2][:, 256:512], rstd[:, 0:1], kvng, ALU.mult, ALU.mult)
        cs = cosT[:, t, :]
        sn = sinT[:, t, :]
        kp = bk[3][:, 0:32].re("p (i two) -> p i two", two=2)
        krv = kr.re("p (i two) -> p i two", two=2)
        rope(P, krv[:, :, 0], krv[:, :, 1], kp[:, :, 0], kp[:, :, 1], cs, sn, [r[:, 0, :] for r in rt])
        for k in range(6):
            P.tr(tp[:, k * 128:(k + 1) * 128], cqn[:, k * 128:(k + 1) * 128], ident)
        P.copy(cqT, tp[:, 0:768].re("p (k t) -> p k t", k=6), eng="act")
        for k in range(2):
            P.tr(tp[:, k * 128:(k + 1) * 128], ckvn[:, k * 128:(k + 1) * 128], ident)
        P.copy(ckvT, tp[:, 0:256].re("p (k t) -> p k t", k=2), eng="act")
        for n in range(3):
            for k in range(6):
                P.mm(bk[4 + n], cqT[:, k, :], wqb[:, k, n * 512:(n + 1) * 512], start=(k == 0), stop=(k == 5))
        kvb = [bk[1], bk[2], bk[3], bk[7]]
        for n in range(4):
            for k in range(2):
                P.mm(kvb[n], ckvT[:, k, :], wkvb[:, k, n * 512:(n + 1) * 512], start=(k == 0), stop=(k == 1))
        for n in range(2):
            P.copy(Qf[:, n * 8:(n + 1) * 8, 0:64], bk[4 + n].re("p (h d) -> p h d", h=8), eng="act")
        qp = bk[6].re("p (h i two) -> p h i two", h=NH, two=2)
        qd = Qf[:, :, 64:96].re("p h (i two) -> p h i two", two=2)
        csb = cs.unsq(1).bc([128, NH, 16])
        snb = sn.unsq(1).bc([128, NH, 16])
        rope(P, qd[:, :, :, 0], qd[:, :, :, 1], qp[:, :, :, 0], qp[:, :, :, 1], csb, snb, rt)
        for n in range(4):
            kv4 = kvb[n].re("p (h d) -> p h d", h=4)
            P.copy(Kf[:, n * 4:(n + 1) * 4, 0:64], kv4[:, :, 0:64], eng="act")
            P.copy(Va[s][:, n * 4:(n + 1) * 4, 0:64], kv4[:, :, 64:128])
        P.copy(Kf[:, :, 64:96], kr.unsq(1).bc([128, NH, 32]))
        w = (t % 4) * 128
        slot = (t // 4) % 2
        for (src, dst) in ((Qf, QTs[slot]), (Kf, KTs[slot])):
            for half in range(2):
                for hh in range(8):
                    h = half * 8 + hh
                    P.tr(tp[0:96, hh * 128:(hh + 1) * 128], src[:, h, :], ident)
                P.copy(dst[:, half * 8:(half + 1) * 8, w:w + 128], tp[0:96, :].re("p (h t) -> p h t", h=8), eng="act")
        P.dma("sp", vA_d[t * 128:(t + 1) * 128, :], Va[s].re("p h d -> p (h d)"), ds_v[s])
        if t % 4 == 3:
            win0 = (t // 4) * 512
            P.dma("sp", qT_v[:, :, win0:win0 + 512], QTs[slot], ds_q[slot])
            P.dma("sp", kT_v[:, :, win0:win0 + 512], KTs[slot], ds_q[slot])

    load(0)
    for t in range(NT):
        if t + 1 < NT:
            load(t + 1)
        compute(t)


def stage_attn(P, S, qT_d, kT_d, vA_d, oT_d, ones_d):
    P.stage_begin()
    NJ = S // 128
    NG = S // 512
    scale = 96.0 ** -0.5
    Vall = P.sb([128, NJ, NH * 65], BF16, "Vall")
    ones = P.sb([128, 64], F32, "ones")
    ds_c = P.dsem("at_c")
    P.dma("sp", ones, ones_d, ds_c)
    vsrc = vA_d.re("(j p) c -> p j c", p=128)
    step = max(1, NJ // 4)
    for j0 in range(0, NJ, step):
        P.dma("sp", Vall[:, j0:j0 + step, :], vsrc[:, j0:j0 + step, :], ds_c)
    KT = [P.sb([96, S], BF16, f"KT{s}") for s in range(2)]
    QT = [P.sb([96, S], BF16, f"QT{s}") for s in range(2)]
    ds_h = [P.dsem(f"at_h{s}") for s in range(2)]
    OT = [P.sb([64, S], BF16, f"OT{s}") for s in range(2)]
    ds_o = [P.dsem(f"at_o{s}") for s in range(2)]
    PT = [P.sb([128, 512], BF16, f"PT{i}") for i in range(4)]
    osb = [P.sb([65, 512], F32, f"osb{i}") for i in range(2)]
    sc = [P.bank(i, F32, f"sc{i}") for i in range(4)]
    oacc = [P.bank(4 + i, F32, f"oacc{i}") for i in range(2)]
    bcp = P.bank(6, F32, "bcp")

    def load(h):
        P.dma("sp", KT[h % 2], kT_d[h], ds_h[h % 2])
        P.dma("sp", QT[h % 2], qT_d[h], ds_h[h % 2])

    items = [(h, g, j) for h in range(NH) for g in range(NG) for j in range(4 * g + 4)]
    LOOK = 2
    pending = []

    def qk(idx):
        h, g, j = items[idx]
        if g == 0 and j == 0 and h + 1 < NH:
            load(h + 1)
        m = j - 4 * g
        c0 = max(0, m) * 128
        n = 512 - c0
        P.mm(sc[idx % 4][:, 0:n], KT[h % 2][:, j * 128:(j + 1) * 128], QT[h % 2][:, g * 512 + c0:(g + 1) * 512])

    def pv(idx):
        h, g, j = items[idx]
        nj = 4 * g + 4
        m = j - 4 * g
        c0 = max(0, m) * 128
        n = 512 - c0
        scb, pt, oa = sc[idx % 4], PT[idx % 4], oacc[g % 2]
        P.act(pt[:, 0:n], scb[:, 0:n], AF.Exp, scale=scale)
        if m >= 0:
            P.memset(pt[64:128, 0:64], 0.0, eng="pool")
        P.mm(oa[0:65, c0:512], Vall[:, j, h * 65:(h + 1) * 65], pt[:, 0:n], start=(j == 0), stop=(j == nj - 1))
        if j == nj - 1:
            ob = osb[g % 2]
            ot = OT[h % 2]
            P.copy(ob, oa[0:65, :], eng="act")
            ro = ob[64:65, :].ap
            P.op("dve", lambda e, ro=ro: e.reciprocal(ro, ro), [ob], [ob])

            def fin(h=h, g=g, ob=ob, ot=ot):
                P.mm(bcp[0:64, :], ones[64:65, 0:64], ob[64:65, :])
                P.tt(ot[:, g * 512:(g + 1) * 512], ob[0:64, :], bcp[0:64, :], ALU.mult)
                if g == NG - 1:
                    P.dma("sp", oT_d[h], ot, ds_o[h % 2])
            pending.append([idx + 3, fin])

    load(0)
    n_it = len(items)
    for idx in range(n_it + LOOK):
        if idx < n_it:
            qk(idx)
        if idx - LOOK >= 0:
            pv(idx - LOOK)
        while pending and pending[0][0] <= idx - LOOK:
            pending.pop(0)[1]()
    while pending:
        pending.pop(0)[1]()


def stage_mla_out(P, S, oT_d, xin, wout_d, lng_d, lnb_d, yout):
    P.stage_begin()
    NW = S // 512
    wout = P.sb([64, NH, D], BF16, "wout")
    g_bc = P.sb([128, D], F32, "g_bc")
    b_bc = P.sb([128, D], F32, "b_bc")
    ds_c = P.dsem("mo_c")
    P.dma("sp", g_bc, lng_d.pbc(128), ds_c)
    P.dma("sp", b_bc, lnb_d.pbc(128), ds_c)
    P.wdma(wout, wout_d.re("(h d) n -> d h n", d=64))
    OTw = [P.sb([64, NH, 512], BF16, f"OTw{s}") for s in range(2)]
    ds_w = [P.dsem(f"mo_w{s}") for s in range(2)]
    xs = [P.sb([128, D], F32, f"xs{s}") for s in range(2)]
    ds_x = [P.dsem(f"mo_x{s}") for s in range(2)]
    z = [P.sb([128, D], F32, f"z{s}") for s in range(2)]
    ys = [P.sb([128, D], F32, f"ys{s}") for s in range(2)]
    ds_y = [P.dsem(f"mo_y{s}") for s in range(2)]
    tmp = [dict(stats=P.sb([128, 2, 6], F32), mv=P.sb([128, 2], F32), rstd=P.sb([128, 1], F32)) for _ in range(2)]
    od = [[P.bank(2 * u + n, F32, f"od{u}{n}") for n in range(2)] for u in range(2)]
    oT_v = oT_d.re("h d s -> d h s")

    def loadw(w):
        P.dma("sp", OTw[w % 2], oT_v[:, :, w * 512:(w + 1) * 512], ds_w[w % 2])

    def loadx(t):
        P.dma("sp", xs[t % 2], xin[t * 128:(t + 1) * 128, :], ds_x[t % 2])

    loadw(0)
    loadx(0)
    for w in range(NW):
        if w + 1 < NW:
            loadw(w + 1)
        for u4 in range(4):
            t = w * 4 + u4
            if t + 1 < S // 128:
                loadx(t + 1)
            u = t % 2
            for n in range(2):
                for h in range(NH):
                    P.mm(od[u][n], OTw[w % 2][:, h, u4 * 128:(u4 + 1) * 128], wout[:, h, n * 512:(n + 1) * 512],
                         start=(h == 0), stop=(h == NH - 1))
            ln_epilogue(P, z[u], xs[t % 2], od[u], g_bc, b_bc, ys[u], tmp[u])
            P.dma("sp", yout[t * 128:(t + 1) * 128, :], ys[u], ds_y[u])


EC_STOP = 99
EC_VAR = ''

EC = math.exp(-0.5)
CB_R, CB_A, CB_B, CB_K, CB_BH, CB_KH, CB_V = [i * 512 for i in range(7)]
CB_GQ, CB_GK, CB_GKH, CB_GV = 3584, 3840, 4096, 4352
WB = 4864
CF_BONUS, CF_G, CF_GS = 0, 512, 1024
WF = 1536


def rsqrt_lnexp(P, out, in_, add=None, mx=None, scale_in=None):
    if scale_in is not None:
        P.ts(out, in_, scale_in, ALU.mult, add if add is not None else 0.0, ALU.add)
    elif mx is not None:
        P.ts(out, in_, mx, ALU.max)
    else:
        P.ts(out, in_, add, ALU.add)
    P.act(out, out, AF.Ln)
    P.act(out, out, AF.Exp, scale=-0.5)


def sigmoid_(P, out, in_, tmp=None):
    P.act(out, in_, AF.Tanh, scale=0.5)
    P.ts(out, out, 0.5, ALU.mult, 0.5, ALU.add)


def stage_even_prep(P, S, xin, c, prepB_d, prepF_d, gam_d):
    P.stage_begin()
    NT = S // 128
    ds_c = P.dsem("ep_c")
    ident = P.sb([128, 128], BF16, "ident")
    P.dma("pool", ident, c["ident"], ds_c)
    win = P.sb([128, 8, 3376], BF16, "win")
    src = c["even_w_in"].re("(k p) n -> p k n", p=128)
    for k in range(8):
        P.wdma(win[:, k:k + 1, :].sub(f"win{k}"), src[:, k:k + 1, :])
    Wwa = P.sb([128, 512], BF16, "Wwa")
    P.wdma(Wwa[0:64, :].sub("w2"), c["rwkv_w2"])
    P.wdma(Wwa[64:128, :].sub("a2"), c["rwkv_a2"])
    G2 = P.sb([128, 2, 512], BF16, "G2")
    P.wdma(G2[:, 0, :].sub("g2a"), c["rwkv_g2"][0:128, :])
    P.wdma(G2[0:32, 1, :].sub("g2b"), c["rwkv_g2"][128:160, :])
    GW = P.sb([48, 256], BF16, "GW")
    P.wdma(GW[32:48, :].sub("gw"), c["gla_gate_w2"])

    def bc(name, n):
        t = P.sb([128, n], F32, name)
        P.dma("sp", t, c[name].pbc(128), ds_c)
        return t
    mu = bc("rwkv_mu", 1824)
    w0 = bc("rwkv_w0", 512)
    a0 = bc("rwkv_a0", 512)
    k_k = bc("rwkv_k_k", 512)
    k_a = bc("rwkv_k_a", 512)
    r_k = bc("rwkv_r_k", 512)
    gate_b = bc("gla_gate_b", 256)

    def cst(name, shape):
        t = P.sb(shape, F32, name)
        P.dma("sp", t, c[name], ds_c)
        return t
    TRI_i = cst("tri_incl", [128, 128])
    TRI_x = cst("tri_excl", [128, 128])
    TRI_r = cst("tri_rev", [128, 128])
    TRIg_i = cst("trig_incl", [128, 128])
    TRIg_r = cst("trig_rev", [128, 128])
    CBk = cst("cb", [128, 2])
    CBg = cst("cbg", [128, 2])
    P.barrier()

    xb = [P.sb([128, 1024], BF16, f"xb{s}") for s in range(2)]
    ds_x = [P.dsem(f"ep_x{s}") for s in range(2)]
    xT = P.sb([128, 8, 128], BF16, "xT")
    Psb = P.sb([128, 3376], F32, "Psb")
    Sh = [P.sb([128, 1824], F32, f"Sh{s}") for s in range(2)]
    ds_sh = [P.dsem(f"ep_sh{s}") for s in range(2)]
    Lin = P.sb([128, 304], BF16, "Lin")
    LT0 = P.sb([128, 128], BF16, "LT0")
    LT1 = P.sb([128, 128], BF16, "LT1")
    LT2 = P.sb([48, 128], BF16, "LT2")
    sigw = P.sb([128, 512], F32, "sigw")
    av = P.sb([128, 512], F32, "a")
    kkn = P.sb([128, 512], F32, "kkn")
    sq = P.sb([128, 512], F32, "sq")
    k2 = P.sb([128, 512], F32, "k2")
    bvec = P.sb([128, 512], F32, "bvec")
    E = P.sb([128, 512], F32, "E")
    ss = P.sb([128, 8], F32, "ss")
    rks = P.sb([128, 8], F32, "rks")
    sp = P.sb([128, 256], F32, "sp")
    Eg = P.sb([128, 256], F32, "Eg")
    OB = [P.sb([128, WB], BF16, f"OB{s}") for s in range(2)]
    OF = [P.sb([128, WF], F32, f"OF{s}") for s in range(2)]
    GM = [P.sb([64, 24], F32, f"GM{s}") for s in range(2)]
    ds_o = [P.dsem(f"ep_o{s}") for s in range(2)]
    P.memset(Sh[0][0:1, :], 0.0)
    tp = P.bank(7, BF16, "tp")
    rb = [P.bank(i, F32, f"rb{i}") for i in range(7)]
    rbi = [0]

    def nb():
        b = rb[rbi[0] % 7]
        rbi[0] += 1
        return b

    def load(t):
        P.dma("pool", xb[t % 2], xin[t * 128:(t + 1) * 128, :], ds_x[t % 2])

    def compute(t):
        s = t % 2
        ob, of, gm = OB[s], OF[s], GM[s]
        for k in range(8):
            P.tr(tp[:, k * 128:(k + 1) * 128], xb[s][:, k * 128:(k + 1) * 128], ident)
        P.copy(xT, tp.re("p (k t) -> p k t", k=8), eng="act")
        for n in range(7):
            c0, c1 = n * 512, min(3376, (n + 1) * 512)
            b = nb()
            for k in range(8):
                P.mm(b[:, 0:c1 - c0], xT[:, k, :], win[:, k, c0:c1], start=(k == 0), stop=(k == 7))
            P.copy(Psb[:, c0:c1], b[:, 0:c1 - c0], eng=("act" if n % 2 == 0 else "dve"))
        sh = Sh[s]
        P.dma("sp", sh[1:128, :], Psb[0:127, 1536:3360], ds_sh[s])
        if t + 1 < NT:
            P.dma("sp", Sh[1 - s][0:1, :], Psb[127:128, 1536:3360], ds_sh[1 - s])
        prw = Psb[:, 1536:3360]
        P.tt(sh, sh, prw, ALU.subtract, eng="pool")
        P.tt(sh, sh, mu, ALU.mult, eng="pool")
        P.tt(sh, sh, prw, ALU.add, eng="pool")
        r, kx, vx = sh[:, 0:512], sh[:, 512:1024], sh[:, 1024:1536]
        wl, al, gl = sh[:, 1536:1600], sh[:, 1600:1664], sh[:, 1664:1824]
        P.act(Lin[:, 0:64], wl, AF.Tanh)
        P.copy(Lin[:, 64:128], al)
        P.act(sq[:, 0:160], gl, AF.Tanh, scale=0.5)
        P.ts(Lin[:, 128:288], sq[:, 0:160], 0.5, ALU.mult, 0.5, ALU.add)
        P.copy(Lin[:, 288:304], Psb[:, 3360:3376])
        for i, (lt, c0, c1) in enumerate(((LT0, 0, 128), (LT1, 128, 256), (LT2, 256, 304))):
            P.tr(tp[0:c1 - c0, i * 128:(i + 1) * 128], Lin[:, c0:c1], ident)
            P.copy(lt, tp[0:c1 - c0, i * 128:(i + 1) * 128], eng="act")
        wpre, apre, gpre, lapre = nb(), nb(), nb(), nb()
        P.mm(wpre, LT0[0:64, :], Wwa[0:64, :])
        P.mm(apre, LT0[64:128, :], Wwa[64:128, :])
        P.mm(gpre, LT1, G2[:, 0, :], start=True, stop=False)
        P.mm(gpre, LT2[0:32, :], G2[0:32, 1, :], start=False, stop=True)
        P.mm(lapre[:, 0:256], LT2[32:48, :], GW[32:48, :])
        P.copy(of[:, CF_G:CF_G + 512], gpre, eng="act")
        P.tt(sigw, wpre, w0, ALU.add)
        sigmoid_(P, sigw, sigw)
        P.tt(av, apre, a0, ALU.add)
        sigmoid_(P, av, av)
        P.tt(sp, lapre[:, 0:256], gate_b, ALU.add)
        P.act(sp, sp, AF.Exp, scale=-1.0)
        P.act(sp, sp, AF.Ln, bias=1.0)
        cum, cumx, rev = nb(), nb(), nb()
        P.mm(cum, TRI_i, sigw)
        P.mm(cumx, TRI_x, sigw)
        P.mm(rev, TRI_r, sigw)
        gmb = nb()
        for h in range(8):
            P.mm(gmb[0:64, h * 2:(h + 1) * 2], sigw[:, h * 64:(h + 1) * 64], CBk)
        for h in range(4):
            P.mm(gmb[0:64, 16 + h * 2:16 + (h + 1) * 2], sp[:, h * 64:(h + 1) * 64], CBg)
        P.act(gm, gmb[0:64, 0:24], AF.Exp)
        P.dma("sp", gam_d[t], gm, ds_o[s])
        P.tt(kkn, kx, k_k, ALU.mult)
        P.tt(sq, kkn, kkn, ALU.mult)
        P.reduce(ss, sq.re("p (h k) -> p h k", h=8))
        rsqrt_lnexp(P, ss, ss, mx=1e-24)
        P.tt(kkn.re("p (h k) -> p h k", h=8), kkn.re("p (h k) -> p h k", h=8), ss.unsq(2).bc([128, 8, 64]), ALU.mult)
        P.stt(k2, av, -1.0, k_a, ALU.add, ALU.mult)
        P.stt(k2, k2, 1.0, kx, ALU.add, ALU.mult)
        P.tt(bvec, kkn, av, ALU.mult)
        P.act(E, cum, AF.Exp)
        P.tt(ob[:, CB_R:CB_R + 512], r, E, ALU.mult)
        P.act(E, cumx, AF.Exp)
        P.stt(ob[:, CB_A:CB_A + 512], kkn, -1.0, E, ALU.mult, ALU.mult)
        P.act(E, cum, AF.Exp, scale=-1.0)
        P.tt(ob[:, CB_B:CB_B + 512], bvec, E, ALU.mult)
        P.tt(ob[:, CB_K:CB_K + 512], k2, E, ALU.mult)
        P.act(E, rev, AF.Exp)
        P.tt(ob[:, CB_BH:CB_BH + 512], bvec, E, ALU.mult)
        P.tt(ob[:, CB_KH:CB_KH + 512], k2, E, ALU.mult)
        P.copy(ob[:, CB_V:CB_V + 512], vx, eng="act")
        P.tt(sq, r, k2, ALU.mult)
        P.tt(sq, sq, r_k, ALU.mult)
        P.reduce(rks, sq.re("p (h k) -> p h k", h=8))
        P.tt(of[:, CF_BONUS:CF_BONUS + 512].re("p (h k) -> p h k", h=8), vx.re("p (h k) -> p h k", h=8),
             rks.unsq(2).bc([128, 8, 64]), ALU.mult)
        cumg, revg = nb(), nb()
        P.mm(cumg[:, 0:256], TRIg_i, sp)
        P.mm(revg[:, 0:256], TRIg_r, sp)
        gq, gk = Psb[:, 0:256], Psb[:, 256:512]
        gv, gg = Psb[:, 512:1024], Psb[:, 1024:1536]
        P.act(Eg, cumg[:, 0:256], AF.Exp)
        P.stt(ob[:, CB_GQ:CB_GQ + 256], gq, 0.125, Eg, ALU.mult, ALU.mult)
        P.act(Eg, cumg[:, 0:256], AF.Exp, scale=-1.0)
        P.tt(ob[:, CB_GK:CB_GK + 256], gk, Eg, ALU.mult)
        P.act(Eg, revg[:, 0:256], AF.Exp)
        P.tt(ob[:, CB_GKH:CB_GKH + 256], gk, Eg, ALU.mult)
        P.copy(ob[:, CB_GV:CB_GV + 512], gv, eng="act")
        gs = of[:, CF_GS:CF_GS + 512]
        sigmoid_(P, gs, gg)
        P.tt(gs, gs, gg, ALU.mult)
        rows = slice(t * 128, (t + 1) * 128)
        P.dma("sp", prepB_d[rows, :], ob, ds_o[s])
        P.dma("sp", prepF_d[rows, :], of, ds_o[s])

    load(0)
    for t in range(NT):
        if t + 1 < NT:
            load(t + 1)
        compute(t)


def stage_even_chunk(P, S, c, prepB_d, prepF_d, gam_d, oT_d):
    P.stage_begin()
    NT = S // 128
    ds_c = P.dsem("ec_c")
    ident = P.sb([128, 128], BF16, "ident")
    P.dma("pool", ident, c["ident"], ds_c)

    def cst(name, shape):
        t = P.sb(shape, F32, name)
        P.dma("sp", t, c[name], ds_c)
        return t
    ML_s = cst("ml_strict", [128, 128])
    MU_s = cst("mu_strict", [128, 128])
    MU_i = cst("mu_incl", [128, 128])

    def bc(name, n):
        t = P.sb([128, n], F32, name)
        P.dma("sp", t, c[name].pbc(128), ds_c)
        return t
    ln_g = bc("rwkv_ln_g", 512)
    ln_b = bc("rwkv_ln_b", 512)
    norm_g = bc("gla_norm_g", 128)

    IB = [P.sb([128, WB], BF16, f"IB{s}") for s in range(2)]
    IF = [P.sb([128, WF], F32, f"IF{s}") for s in range(2)]
    GM = [P.sb([64, 24], F32, f"GMi{s}") for s in range(2)]
    ds_i = [P.dsem(f"ec_i{s}") for s in range(2)]

    def fm(name, nh=8):
        return P.sb([64, nh, 128], BF16, name)
    RT, AT, BT, KT = fm("RT"), fm("AT"), fm("BT"), fm("KT")
    RTm = [fm("RTm0"), fm("RTm1")]
    qtT, ktT = fm("qtT", 4), fm("ktT", 4)
    qtTm = [fm("qtTm0", 4), fm("qtTm1", 4)]

    def mat(name, nh=8):
        return P.sb([128, nh, 128], BF16, name)
    Pm = [mat("Pm0"), mat("Pm1")]
    PTm = [mat("PTm0"), mat("PTm1")]
    Qm = [mat("Qm0"), mat("Qm1")]
    AakT, ArbT, ArkT = mat("AakT"), mat("ArbT"), mat("ArkT")
    attnT = mat("attnT", 4)
    WtT = fm("WtT")
    AV = P.sb([128, 8, 64], BF16, "AV")
    Ut = P.sb([128, 8, 64], F32, "Ut")
    Usb = P.sb([128, 8, 64], BF16, "Usb")
    H = P.sb([64, 8, 64], F32, "H")
    Hb = [P.sb([64, 8, 64], BF16, f"Hb{i}") for i in range(3)]
    Sg = P.sb([64, 4, 128], F32, "Sg")
    Sgb = [P.sb([64, 4, 128], BF16, f"Sgb{i}") for i in range(3)]
    Ysb = P.sb([128, 8, 64], F32, "Ysb")
    ysq = P.sb([128, 8, 64], F32, "ysq")
    Osb = P.sb([128, 4, 128], F32, "Osb")
    s1 = P.sb([128, 8], F32, "s1")
    s2 = P.sb([128, 8], F32, "s2")
    MIX = P.sb([128, 1024], BF16, "MIX")
    mixT = [P.sb([128, 8, 128], BF16, f"mixT{s}") for s in range(2)]
    ds_m = [P.dsem(f"ec_m{s}") for s in range(2)]
    BHm = [P.sb([128, 512], BF16, f"BHm{i}") for i in range(2)]
    KHm = [P.sb([128, 512], BF16, f"KHm{i}") for i in range(2)]
    GKHm = [P.sb([128, 256], BF16, f"GKHm{i}") for i in range(2)]
    for tl in RTm + qtTm + BHm + KHm + GKHm + [Usb]:
        P.memset(tl, 0.0)
    P.memset(H, 0.0)
    P.memset(Hb[0], 0.0)
    P.memset(Sg, 0.0)
    P.memset(Sgb[0], 0.0)
    tp = P.bank(7, BF16, "tp")
    Yb = P.bank(6, F32, "Yb")
    Ob = P.bank(5, F32, "Ob")
    rb = [P.bank(i, F32, f"rb{i}") for i in range(5)]
    rbi = [0]

    def nb():
        b = rb[rbi[0] % 5]
        rbi[0] += 1
        return b
    oT_v = oT_d.re("(k two) d s -> (two d) k s", two=2)

    def load(t):
        s = t % 2
        rows = slice(t * 128, (t + 1) * 128)
        P.dma("sp", IB[s], prepB_d[rows, :], ds_i[s])
        P.dma("sp", IF[s], prepF_d[rows, :], ds_i[s])
        P.dma("sp", GM[s], gam_d[t], ds_i[s])

    def transpose_heads(dst, src, c0, nh, dstm=None):
        for h in range(nh):
            P.tr(tp[0:64, h * 128:(h + 1) * 128], src[:, c0 + h * 64:c0 + (h + 1) * 64], ident)
        tv = tp[0:64, 0:nh * 128].re("p (h t) -> p h t", h=nh)
        P.copy(dst, tv, eng="act")
        if dstm is not None and EC_VAR != 'b':
            P.copy(dstm[0][:, :, 0:64], dst[:, :, 0:64], eng="pool")
            P.copy(dstm[1][:, :, 64:128], dst[:, :, 64:128], eng="pool")

    def headmm(dst, lhs, rhs, mask, nh=8):
        for g in range(nh // 4):
            b = nb()
            for hh in range(4):
                h = g * 4 + hh
                P.mm(b[:, hh * 128:(hh + 1) * 128], lhs[:, h, :], rhs[:, h, :])
            bv = b.re("p (h t) -> p h t", h=4)
            if mask is None:
                P.copy(dst[:, g * 4:(g + 1) * 4, :], bv, eng="act")
            else:
                P.tt(dst[:, g * 4:(g + 1) * 4, :], bv, mask.unsq(1).bc([128, 4, 128]), ALU.mult)

    def compute(t):
        s = t % 2
        ib, iff, gm = IB[s], IF[s], GM[s]
        if EC_STOP <= 0:
            return
        transpose_heads(RT, ib, CB_R, 8, RTm)
        transpose_heads(AT, ib, CB_A, 8)
        transpose_heads(BT, ib, CB_B, 8)
        transpose_heads(KT, ib, CB_K, 8)
        if EC_VAR == 'a':
            return
        L0, M0 = PTm[0], Pm[0]
        headmm(L0, AT, BT, ML_s)
        headmm(M0, BT, AT, MU_s)
        headmm(AakT, KT, AT, MU_s)
        headmm(ArbT, BT, RT, MU_i)
        headmm(ArkT, KT, RT, MU_i)
        if EC_STOP <= 1:
            return
        P.tt(Qm[0], M0, ident.unsq(1).bc([128, 8, 128]), ALU.add)
        cur = 0
        for lvl in range(5):
            nxt = 1 - cur
            if lvl < 4:
                headmm(Pm[nxt], PTm[cur], Pm[cur], None)
            headmm(PTm[nxt], Pm[cur], PTm[cur], None)
            for g in range(2):
                b = nb()
                for hh in range(4):
                    h = g * 4 + hh
                    P.mm(b[:, hh * 128:(hh + 1) * 128], PTm[nxt][:, h, :], Qm[cur][:, h, :])
                P.tt(Qm[nxt][:, g * 4:(g + 1) * 4, :], b.re("p (h t) -> p h t", h=4), Qm[cur][:, g * 4:(g + 1) * 4, :], ALU.add)
            cur = nxt
        TT = Qm[cur]
        if EC_STOP <= 2:
            return
        for g in range(2):
            b = nb()
            for hh in range(4):
                h = g * 4 + hh
                P.mm(b[0:64, hh * 128:(hh + 1) * 128], ib[:, CB_A + h * 64:CB_A + (h + 1) * 64], TT[:, h, :])
            P.copy(WtT[:, g * 4:(g + 1) * 4, :], b[0:64, :].re("p (h t) -> p h t", h=4), eng="act")
        b = nb()
        for h in range(8):
            P.mm(b[:, h * 64:(h + 1) * 64], AakT[:, h, :], ib[:, CB_V + h * 64:CB_V + (h + 1) * 64])
        P.copy(AV, b.re("p (h v) -> p h v", h=8), eng="act")
        b = nb()
        for h in range(8):
            P.mm(b[:, h * 64:(h + 1) * 64], TT[:, h, :], AV[:, h, :])
        P.copy(Ut, b.re("p (h v) -> p h v", h=8), eng="act")
        if EC_STOP <= 3:
            return
        transpose_heads(qtT, ib, CB_GQ, 4, qtTm)
        transpose_heads(ktT, ib, CB_GK, 4)
        headmm(attnT, ktT, qtT, MU_i, nh=4)
        if EC_STOP <= 4:
            return
        for cc in range(2):
            rs = slice(cc * 64, (cc + 1) * 64)
            P.copy(BHm[cc][rs, :], ib[rs, CB_BH:CB_BH + 512], eng="pool")
            P.copy(KHm[cc][rs, :], ib[rs, CB_KH:CB_KH + 512], eng="pool")
            P.copy(GKHm[cc][rs, :], ib[rs, CB_GKH:CB_GKH + 256], eng="pool")
        for cc in range(2):
            rs = slice(cc * 64, (cc + 1) * 64)
            hb_in, hb_out = Hb[(2 * t + cc) % 3], Hb[(2 * t + cc + 1) % 3]
            sb_in, sb_out = Sgb[(2 * t + cc) % 3], Sgb[(2 * t + cc + 1) % 3]
            up = nb()
            for h in range(8):
                P.mm(up[:, h * 64:(h + 1) * 64], WtT[:, h, :], hb_in[:, h, :])
            P.tt(Usb[rs, :, :], up[rs, :].re("p (h v) -> p h v", h=8), Ut[rs, :, :], ALU.add)
            dh = nb()
            for h in range(8):
                P.mm(dh[0:64, h * 64:(h + 1) * 64], BHm[cc][:, h * 64:(h + 1) * 64], Usb[:, h, :],
                     start=True, stop=False)
                P.mm(dh[0:64, h * 64:(h + 1) * 64], KHm[cc][:, h * 64:(h + 1) * 64],
                     ib[:, CB_V + h * 64:CB_V + (h + 1) * 64], start=False, stop=True)
            gsel = gm[:, 0:16].re("p (h c) -> p h c", c=2)[:, :, cc]
            P.tt(H, H, gsel.unsq(2).bc([64, 8, 64]), ALU.mult)
            P.tt(H, H, dh[0:64, :].re("p (h v) -> p h v", h=8), ALU.add)
            P.copy(hb_out, H, eng="act")
            ds_ = nb()
            for h in range(4):
                P.mm(ds_[0:64, h * 128:(h + 1) * 128], GKHm[cc][:, h * 64:(h + 1) * 64],
                     ib[:, CB_GV + h * 128:CB_GV + (h + 1) * 128])
            gsel2 = gm[:, 16:24].re("p (h c) -> p h c", c=2)[:, :, cc]
            P.tt(Sg, Sg, gsel2.unsq(2).bc([64, 4, 128]), ALU.mult)
            P.tt(Sg, Sg, ds_[0:64, :].re("p (h v) -> p h v", h=4), ALU.add)
            P.copy(sb_out, Sg, eng="act")
        for h in range(8):
            ys_ = Yb[:, h * 64:(h + 1) * 64]
            P.mm(ys_, RTm[0][:, h, :], Hb[(2 * t) % 3][:, h, :], start=True, stop=False)
            P.mm(ys_, RTm[1][:, h, :], Hb[(2 * t + 1) % 3][:, h, :], start=False, stop=False)
            P.mm(ys_, ArbT[:, h, :], Usb[:, h, :], start=False, stop=False)
            P.mm(ys_, ArkT[:, h, :], ib[:, CB_V + h * 64:CB_V + (h + 1) * 64], start=False, stop=True)
        for h in range(4):
            os_ = Ob[:, h * 128:(h + 1) * 128]
            P.mm(os_, qtTm[0][:, h, :], Sgb[(2 * t) % 3][:, h, :], start=True, stop=False)
            P.mm(os_, qtTm[1][:, h, :], Sgb[(2 * t + 1) % 3][:, h, :], start=False, stop=False)
            P.mm(os_, attnT[:, h, :], ib[:, CB_GV + h * 128:CB_GV + (h + 1) * 128], start=False, stop=True)
        if EC_STOP <= 5:
            return
        P.copy(Ysb, Yb.re("p (h v) -> p h v", h=8), eng="act")
        P.reduce(s1, Ysb)
        P.tt(ysq, Ysb, Ysb, ALU.mult)
        P.reduce(s2, ysq)
        P.ts(s1, s1, 1.0 / 64, ALU.mult)
        P.stt(ysq[:, :, 0], s1, -1.0, s1, ALU.mult, ALU.mult)
        P.stt(s2, s2, 1.0 / 64, ysq[:, :, 0], ALU.mult, ALU.add)
        _r = rsqrt_lnexp
        _r(P, s2, s2, add=64e-5)
        P.tt(Ysb, Ysb, s1.unsq(2).bc([128, 8, 64]), ALU.subtract)
        P.tt(Ysb, Ysb, s2.unsq(2).bc([128, 8, 64]), ALU.mult)
        yf = Ysb.re("p h v -> p (h v)")
        P.tt(yf, yf, ln_g, ALU.mult)
        P.tt(yf, yf, ln_b, ALU.add)
        P.tt(yf, yf, iff[:, CF_BONUS:CF_BONUS + 512], ALU.add)
        P.tt(MIX[:, 512:1024], yf, iff[:, CF_G:CF_G + 512], ALU.mult)
        if EC_STOP <= 6:
            return
        P.copy(Osb, Ob.re("p (h v) -> p h v", h=4), eng="act")
        osq = ysq.re("p h v -> p (h v)").re("p (h v) -> p h v", h=4)
        P.tt(osq, Osb, Osb, ALU.mult)
        P.reduce(s1[:, 0:4], osq)
        _r(P, s1[:, 0:4], s1[:, 0:4], scale_in=1.0 / 128, add=1e-6)
        P.tt(Osb, Osb, s1[:, 0:4].unsq(2).bc([128, 4, 128]), ALU.mult)
        P.tt(Osb, Osb, norm_g.unsq(1).bc([128, 4, 128]), ALU.mult)
        P.tt(MIX[:, 0:512], Osb.re("p h v -> p (h v)"), iff[:, CF_GS:CF_GS + 512], ALU.mult)
        if EC_STOP <= 7:
            return
        mt = mixT[s]
        for k in range(8):
            P.tr(tp[:, k * 128:(k + 1) * 128], MIX[:, k * 128:(k + 1) * 128], ident)
        P.copy(mt, tp.re("p (k t) -> p k t", k=8), eng="act")
        P.dma("sp", oT_v[:, :, t * 128:(t + 1) * 128], mt, ds_m[s])

    load(0)
    for t in range(NT):
        if t + 1 < NT:
            load(t + 1)
        compute(t)


NCORES = 8


def host_consts():
    c = {}
    c["ident"] = np.eye(128, dtype=np.float32)
    idx = np.arange(128)
    same = (idx[:, None] // 64) == (idx[None, :] // 64)
    lt = idx[:, None] < idx[None, :]
    le = idx[:, None] <= idx[None, :]
    gt = idx[:, None] > idx[None, :]
    EC = math.exp(-0.5)
    c["tri_incl"] = (-EC * (same & le)).astype(np.float32)
    c["tri_excl"] = (-EC * (same & lt)).astype(np.float32)
    c["tri_rev"] = (-EC * (same & gt)).astype(np.float32)
    c["trig_incl"] = (-(1.0 / 16) * (same & le)).astype(np.float32)
    c["trig_rev"] = (-(1.0 / 16) * (same & gt)).astype(np.float32)
    cb = np.zeros((128, 2), np.float32)
    cb[0:64, 0] = 1
    cb[64:128, 1] = 1
    c["cb"] = (-EC * cb).astype(np.float32)
    c["cbg"] = (-(1.0 / 16) * cb).astype(np.float32)
    c["ml_strict"] = (same & gt).astype(np.float32)
    c["mu_strict"] = (same & lt).astype(np.float32)
    c["mu_incl"] = (same & le).astype(np.float32)
    c["invf"] = (10000.0 ** (-np.arange(0, 32, 2, dtype=np.float32) / 32)).astype(np.float32)
    c["ones"] = np.ones((128, 64), np.float32)
    return c


CONST_SHAPES = dict(ident=[128, 128], tri_incl=[128, 128], tri_excl=[128, 128], tri_rev=[128, 128],
                    trig_incl=[128, 128], trig_rev=[128, 128], cb=[128, 2], cbg=[128, 2],
                    ml_strict=[128, 128], mu_strict=[128, 128], mu_incl=[128, 128], invf=[16], ones=[128, 64])

WEIGHT_SHAPES = dict(
    even_w_in=[1024, 3376], gla_gate_w2=[16, 256], gla_gate_b=[256], gla_norm_g=[128], rwkv_mu=[1824],
    rwkv_w0=[512], rwkv_w2=[64, 512], rwkv_a0=[512], rwkv_w2_=None, rwkv_a2=[64, 512], rwkv_g2=[160, 512],
    rwkv_k_k=[512], rwkv_k_a=[512], rwkv_r_k=[512], rwkv_ln_g=[512], rwkv_ln_b=[512], even_w_out=[1024, 1024],
    mla_w_in=[1024, 1056], mla_q_norm_g=[768], mla_w_q_b=[768, 1536], mla_kv_norm_g=[256], mla_w_kv_b=[256, 2048],
    mla_w_out=[1024, 1024], ffn_gu0=[1024, 5632], ffn_gu1=[1024, 5632], ffn_d0=[2816, 1024], ffn_d1=[2816, 1024],
    ln_g00=[1024], ln_g01=[1024], ln_g10=[1024], ln_g11=[1024], ln_b00=[1024], ln_b01=[1024], ln_b10=[1024], ln_b11=[1024])
del WEIGHT_SHAPES["rwkv_w2_"]


def host_weights(inp):
    w = {}
    f = lambda a: np.ascontiguousarray(np.asarray(a, dtype=np.float32))
    ew = np.asarray(inp["even_w_in"][0])
    o = 1552
    perm = np.concatenate([np.arange(0, 1536), o + np.arange(0, 1824), np.arange(1536, 1552)])
    w["even_w_in"] = f(ew[:, perm])
    for k in ("gla_gate_w2", "gla_gate_b", "gla_norm_g", "rwkv_mu", "rwkv_w0", "rwkv_w2", "rwkv_a0", "rwkv_a2",
              "rwkv_g2", "rwkv_k_k", "rwkv_k_a", "rwkv_ln_g", "rwkv_ln_b", "even_w_out", "mla_w_in",
              "mla_q_norm_g", "mla_kv_norm_g", "mla_w_kv_b", "mla_w_out"):
        w[k] = f(inp[k][0])
    w["rwkv_r_k"] = f(np.asarray(inp["rwkv_r_k"][0]).reshape(512))
    wq = np.asarray(inp["mla_w_q_b"][0]).reshape(768, 16, 96)
    w["mla_w_q_b"] = f(np.concatenate([wq[:, :, 0:64].reshape(768, 1024), wq[:, :, 64:96].reshape(768, 512)], 1))
    for i in range(2):
        gu = np.asarray(inp["ffn_w_gate_up"][i])
        w[f"ffn_gu{i}"] = f(gu.reshape(1024, 2, 22, 128).transpose(0, 2, 1, 3).reshape(1024, 5632))
        w[f"ffn_d{i}"] = f(inp["ffn_w_down"][i])
        for j in range(2):
            w[f"ln_g{i}{j}"] = f(inp["ln_g"][i, j])
            w[f"ln_b{i}{j}"] = f(inp["ln_b"][i, j])
    return w


def build(S, debug=False, stages=("ep", "ec", "eo", "f0", "mp", "at", "mo", "f1")):
    nc = bass.Bass("TRN2", target_bir_lowering=False)
    with ExitStack() as st:
        P = Prog(nc, st)
        P.init_mem()
        kind_dbg = "ExternalOutput" if debug else "Internal"
        x = P.dram("x", [S, 1024], F32, kind="ExternalInput")
        pos = P.dram("pos", [128, S // 128], I32, kind="ExternalInput")
        c = {k: P.dram(k, v, F32, kind="ExternalInput") for k, v in CONST_SHAPES.items()}
        w = {k: P.dram(k, v, F32, kind="ExternalInput") for k, v in WEIGHT_SHAPES.items()}
        c.update(w)
        prepB = P.dram("prepB", [S, WB], BF16)
        prepF = P.dram("prepF", [S, WF], F32)
        gam = P.dram("gam", [S // 128, 64, 24], F32)
        oT = P.dram("oT", [16, 64, S], BF16)
        oT2 = P.dram("oT2", [16, 64, S], BF16)
        qT = P.dram("qT", [16, 96, S], BF16)
        kT = P.dram("kT", [16, 96, S], BF16)
        vA = P.dram("vA", [S, 16 * 65], BF16)
        x1 = P.dram("x1", [S, 1024], F32, kind=kind_dbg if "eo" in stages else "ExternalInput")
        x2 = P.dram("x2", [S, 1024], F32, kind=kind_dbg if "f0" in stages else "ExternalInput")
        x3 = P.dram("x3", [S, 1024], F32, kind=kind_dbg if "mo" in stages else "ExternalInput")
        out = P.dram("out", [S, 1024], F32, kind="ExternalOutput")
        if "ep" in stages:
            stage_even_prep(P, S, x, c, prepB, prepF, gam)
        if "ec" in stages:
            stage_even_chunk(P, S, c, prepB, prepF, gam, oT)
        if "eo" in stages:
            stage_mla_out(P, S, oT, x, c["even_w_out"], c["ln_g00"], c["ln_b00"], x1)
        if "f0" in stages:
            stage_ffn(P, S, x1, c["ffn_gu0"], c["ffn_d0"], c["ln_g01"], c["ln_b01"], x2, c["ident"])
        if "mp" in stages:
            stage_mla_proj(P, S, x2, pos, c["invf"], c["mla_w_in"], c["mla_q_norm_g"], c["mla_w_q_b"],
                           c["mla_kv_norm_g"], c["mla_w_kv_b"], c["ident"], qT, kT, vA)
        if "at" in stages:
            stage_attn(P, S, qT, kT, vA, oT2, c["ones"])
        if "mo" in stages:
            stage_mla_out(P, S, oT2, x2, c["mla_w_out"], c["ln_g10"], c["ln_b10"], x3)
        if "f1" in stages:
            stage_ffn(P, S, x3, c["ffn_gu1"], c["ffn_d1"], c["ln_g11"], c["ln_b11"], out, c["ident"])
        P.barrier()
        P.emit()
        print({e: len(P.ins[e]) for e in ENGS}, flush=True)
    return nc


_NC_CACHE = {}


def kernel(**inp):
    x = np.asarray(inp["x"], dtype=np.float32)
    B, S, _ = x.shape
    pos = np.asarray(inp["positions"]).astype(np.int32)
    consts = host_consts()
    wts = host_weights(inp)
    if S not in _NC_CACHE:
        _NC_CACHE[S] = build(S)
    nc = _NC_CACHE[S]
    in_maps = []
    for b in range(B):
        m = dict(x=np.ascontiguousarray(x[b]),
                 pos=np.ascontiguousarray(pos[b].reshape(S // 128, 128).T))
        m.update(consts)
        m.update(wts)
        in_maps.append(m)
    res = run_bass_kernel_spmd(nc, in_maps, core_ids=list(range(B)))
    return np.stack([np.asarray(r["out"], dtype=np.float32) for r in res.results], 0)
```

```python
import math
import threading
from contextlib import ExitStack
import threading
import numpy as np
import concourse.bass as bass
import concourse.mybir as mybir
from concourse.bass_utils import run_bass_kernel_spmd

F32 = mybir.dt.float32
BF16 = mybir.dt.bfloat16
I32 = mybir.dt.int32
ALU = mybir.AluOpType
AF = mybir.ActivationFunctionType
AX = mybir.AxisListType

ENGS = ("pe", "act", "dve", "pool", "sp")


class Buf:
    __slots__ = ("name", "writes", "reads")

    def __init__(self, name=""):
        self.name = name
        self.writes = []
        self.reads = []


class V:
    __slots__ = ("ap", "buf")

    def __init__(self, ap, buf):
        self.ap = ap
        self.buf = buf

    def __getitem__(self, idx):
        return V(self.ap[idx], self.buf)

    def re(self, pattern, **kw):
        return V(self.ap.rearrange(pattern, **kw), self.buf)

    def sub(self, name=""):
        return V(self.ap, Buf(name))

    def bc(self, shape):
        return V(self.ap.to_broadcast(list(shape)), self.buf)

    def unsq(self, axis):
        return V(self.ap.unsqueeze(axis), self.buf)

    def pbc(self, n):
        return V(self.ap.partition_broadcast(n), self.buf)

    def bitcast(self, dt):
        return V(self.ap.bitcast(dt), self.buf)


class DmaPart:
    def __init__(self, sem, name):
        self.sem = sem
        self.total = 0
        self.name = name


class DmaSem:
    def __init__(self, prog, name):
        self.prog = prog
        self.name = name
        self.parts = {}

    def part(self, eng):
        k = "sw" if eng == "pool" else "hw"
        if k not in self.parts:
            p = self.prog
            sem = p.stack.enter_context(p.nc.semaphore(f"ds{len(p.dsems)}_{k}_{self.name}"))
            self.parts[k] = DmaPart(sem, self.name + k)
            p.dsems.append(self.parts[k])
        return self.parts[k]

    @property
    def total(self):
        return sum(x.total for x in self.parts.values())


class Prog:
    def __init__(self, nc, stack):
        self.nc = nc
        self.stack = stack
        self.ins = {e: [] for e in ENGS}
        self.waited = {e: {} for e in ENGS}
        self.esem = {e: stack.enter_context(nc.semaphore("es_" + e)) for e in ENGS}
        self.dsems = []
        self.ntile = 0
        self._il_step = None

    ARENA = 212800

    def init_mem(self):
        nc = self.nc
        self.arena = self.stack.enter_context(nc.sbuf_tensor("arena", [128, self.ARENA], mybir.dt.uint8))
        self.banks = [self.stack.enter_context(nc.psum_tensor(f"bank{i}", [128, 512], F32)) for i in range(8)]
        self.off = 0

    def stage_begin(self):
        self.barrier()
        self.off = 0

    def sb(self, shape, dtype, name=""):
        esz = {F32: 4, BF16: 2, I32: 4}[dtype]
        n = 1
        for d in shape[1:]:
            n *= d
        nb = (n * esz + 31) // 32 * 32
        assert self.off + nb <= self.ARENA, f"SBUF arena overflow: {self.off}+{nb} ({name})"
        ap = self.arena[0:shape[0], self.off:self.off + n * esz].bitcast(dtype)
        self.off += nb
        if len(shape) == 3:
            ap = ap.rearrange("p (a b) -> p a b", a=shape[1])
        elif len(shape) == 4:
            ap = ap.rearrange("p (a b c) -> p a b c", a=shape[1], b=shape[2])
        return V(ap, Buf(name))

    def bank(self, i, dtype=F32, name=""):
        ap = self.banks[i][:, :]
        if dtype != F32:
            ap = ap.bitcast(dtype)
        return V(ap, Buf(name or f"bank{i}"))

    def dram(self, name, shape, dtype, kind="Internal"):
        t = self.nc.dram_tensor(name, list(shape), dtype, kind=kind)
        return V(t.ap(), None)

    def dsem(self, name):
        return DmaSem(self, name)

    def _deps(self, eng, reads, writes):
        toks = []
        for v in reads:
            b = v.buf if isinstance(v, V) else v
            if b is None:
                continue
            toks.extend((t, True) for t in b.writes)
        for v in writes:
            b = v.buf if isinstance(v, V) else v
            if b is None:
                continue
            toks.extend((t, False) for t in b.writes)
            toks.extend((t, False) for t in b.reads)
        waits = []
        wd = self.waited[eng]
        for tk, raw in toks:
            if tk[0] == "e":
                _, f, idx = tk
                if f == eng and (eng == "pe" or not raw):
                    continue
                if wd.get(("e", f), -1) >= idx:
                    continue
                wd[("e", f)] = idx
                self.ins[f][idx]["signal"] = True
                waits.append(tk)
            else:
                _, ds = tk
                if wd.get(("d", id(ds)), -1) >= ds.total:
                    continue
                wd[("d", id(ds))] = ds.total
                waits.append(("d", ds, ds.total))
        return waits

    def _commit(self, tok, reads, writes):
        for v in reads:
            b = v.buf if isinstance(v, V) else v
            if b is None:
                continue
            if tok[0] == "e":
                b.reads = [t for t in b.reads if not (t[0] == "e" and t[1] == tok[1])]
            elif tok in b.reads:
                continue
            b.reads.append(tok)
        for v in writes:
            b = v.buf if isinstance(v, V) else v
            if b is None:
                continue
            b.writes = [tok]
            b.reads = []

    def op(self, eng, fn, reads=(), writes=()):
        waits = self._deps(eng, reads, writes)
        idx = len(self.ins[eng])
        self.ins[eng].append(dict(fn=fn, waits=waits, signal=False, dsem=None))
        self._commit(("e", eng, idx), reads, writes)
        if self._il_step is not None:
            self._il_step()
        return idx

    def dma(self, eng, out, in_, ds, serial=False, **kw):
        ds = ds.part(eng)
        waits = self._deps(eng, [in_], [out])
        if serial and ds.total and self.waited[eng].get(("d", id(ds)), -1) < ds.total:
            self.waited[eng][("d", id(ds))] = ds.total
            waits.append(("d", ds, ds.total))
        idx = len(self.ins[eng])
        ds.total += 16
        oa, ia = out.ap, in_.ap
        self.ins[eng].append(dict(fn=lambda e: e.dma_start(out=oa, in_=ia, **kw), waits=waits,
                                  signal=False, dsem=ds))
        self._commit(("d", ds), [in_], [out])
        if self._il_step is not None:
            self._il_step()

    def interleave(self, fns):
        n = len(fns)
        import os
        if n == 1 or os.environ.get('NO_IL'):
            for f in fns:
                f()
            return
        cv = threading.Condition()
        st = {"turn": 0, "alive": [True] * n, "err": None}
        loc = threading.local()

        def nxt(i):
            for d in range(1, n + 1):
                j = (i + d) % n
                if st["alive"][j]:
                    return j
            return None

        def step():
            i = loc.idx
            with cv:
                j = nxt(i)
                if j is None or j == i:
                    return
                st["turn"] = j
                cv.notify_all()
                while st["turn"] != i:
                    cv.wait()

        def worker(i):
            loc.idx = i
            with cv:
                while st["turn"] != i:
                    cv.wait()
            try:
                fns[i]()
            except BaseException as e:
                st["err"] = e
            with cv:
                st["alive"][i] = False
                st["turn"] = nxt(i)
                cv.notify_all()

        self._il_step = step
        ths = [threading.Thread(target=worker, args=(i,)) for i in range(n)]
        for t in ths:
            t.start()
        for t in ths:
            t.join()
        self._il_step = None
        if st["err"] is not None:
            raise st["err"]

    def wdma(self, out, in_, eng="pool"):
        if not hasattr(self, "wsems"):
            self.wsems = [self.dsem(f"w{i}") for i in range(6)]
            self.wcnt = 0
        ds = self.wsems[self.wcnt % len(self.wsems)]
        self.wcnt += 1
        self.dma(eng, out, in_, ds, serial=True)

    def barrier(self):
        for e in ENGS:
            waits = []
            wd = self.waited[e]
            for f in ENGS:
                if f == e or not self.ins[f]:
                    continue
                idx = None
                for k in range(len(self.ins[f]) - 1, -1, -1):
                    if self.ins[f][k]["dsem"] is None and self.ins[f][k]["fn"] is not None:
                        idx = k
                        break
                if idx is None or wd.get(("e", f), -1) >= idx:
                    continue
                wd[("e", f)] = idx
                self.ins[f][idx]["signal"] = True
                waits.append(("e", f, idx))
            for ds in self.dsems:
                if ds.total and wd.get(("d", id(ds)), -1) < ds.total:
                    wd[("d", id(ds))] = ds.total
                    waits.append(("d", ds, ds.total))
            if waits:
                self.ins[e].append(dict(fn=None, waits=waits, signal=False, dsem=None))

    def emit(self):
        nc = self.nc
        rank = {}
        for e in ENGS:
            c = 0
            r = []
            for rec in self.ins[e]:
                if rec["signal"]:
                    c += 1
                r.append(c)
            rank[e] = r
        print("signals", {e: (rank[e][-1] if rank[e] else 0) for e in ENGS}, "dma_max", max([d.total for d in self.dsems] + [0]), flush=True)
        handles = {"pe": "tensor", "act": "scalar", "dve": "vector", "pool": "gpsimd", "sp": "sync"}

        def run(e, eng):
            for rec in self.ins[e]:
                for w in rec["waits"]:
                    if w[0] == "e":
                        eng.wait_ge(self.esem[w[1]], rank[w[1]][w[2]])
                    else:
                        eng.wait_ge(w[1].sem, w[2])
                if rec["fn"] is None:
                    continue
                ins = rec["fn"](eng)
                if rec["dsem"] is not None:
                    ins.then_inc(rec["dsem"].sem, 16)
                elif rec["signal"]:
                    ins.then_inc(self.esem[e], 1)

        with nc.Block() as block:
            for e in ENGS:
                if not self.ins[e]:
                    continue
                getattr(block, handles[e])(lambda eng, e=e: run(e, eng))

    def mm(self, out, lhsT, rhs, start=True, stop=True):
        o, l, r = out.ap, lhsT.ap, rhs.ap
        return self.op("pe", lambda e: e.matmul(o, l, r, start=start, stop=stop), [lhsT, rhs], [out])

    def tr(self, out, in_, ident):
        o, i, d = out.ap, in_.ap, ident.ap
        return self.op("pe", lambda e: e.transpose(o, i, d), [in_, ident], [out])

    def act(self, out, in_, func, bias=None, scale=None, accum=None, eng="act"):
        o, i = out.ap, in_.ap
        kw = {}
        rd = [in_]
        wr = [out]
        if bias is not None:
            if isinstance(bias, V):
                kw["bias"] = bias.ap
                rd.append(bias)
            else:
                kw["bias"] = bias
        if scale is not None:
            if isinstance(scale, V):
                kw["scale"] = scale.ap
                rd.append(scale)
            else:
                kw["scale"] = scale
        if accum is not None:
            kw["accum_out"] = accum.ap
            wr.append(accum)
        return self.op("act", lambda e: e.activation(o, i, func, **kw), rd, wr)

    def tt(self, out, a, b, op, eng="dve"):
        o, x, y = out.ap, a.ap, b.ap
        return self.op(eng, lambda e: e.tensor_tensor(o, x, y, op), [a, b], [out])

    def ts(self, out, a, s1, op0, s2=None, op1=None, eng="dve", accum=None):
        o, x = out.ap, a.ap
        rd = [a]
        wr = [out]
        a1 = s1
        if isinstance(s1, V):
            rd.append(s1)
            a1 = s1.ap
        a2 = s2
        if isinstance(s2, V):
            rd.append(s2)
            a2 = s2.ap
        kw = {}
        if op1 is not None:
            kw["op1"] = op1
        if accum is not None:
            kw["accum_out"] = accum.ap
            wr.append(accum)
        return self.op(eng, lambda e: e.tensor_scalar(o, x, a1, a2, op0, **kw), rd, wr)

    def stt(self, out, a, s, b, op0, op1, eng="dve"):
        o, x, y = out.ap, a.ap, b.ap
        rd = [a, b]
        sc = s
        if isinstance(s, V):
            rd.append(s)
            sc = s.ap
        return self.op(eng, lambda e: e.scalar_tensor_tensor(o, x, sc, y, op0, op1), rd, [out])

    def copy(self, out, in_, eng="dve"):
        o, i = out.ap, in_.ap
        if eng == "act":
            return self.op("act", lambda e: e.copy(o, i), [in_], [out])
        return self.op(eng, lambda e: e.tensor_copy(o, i), [in_], [out])

    def memset(self, out, val, eng="dve"):
        o = out.ap
        return self.op(eng, lambda e: e.memset(o, val), [], [out])

    def reduce(self, out, in_, op=None, axis=None, eng="dve"):
        o, i = out.ap, in_.ap
        op = op or ALU.add
        axis = axis or AX.X
        return self.op(eng, lambda e: e.tensor_reduce(o, i, axis, op), [in_], [out])


DN_ALPHA = 4.0 ** 0.25
LN_EPS = 1e-5
D = 1024
FH = 2816
NF = 22


def ln_epilogue(P, z, xs, ods, g_bc, b_bc, ys, tmp):
    for n in range(2):
        sl = slice(n * 512, (n + 1) * 512)
        P.stt(z[:, sl], xs[:, sl], DN_ALPHA, ods[n], ALU.mult, ALU.add)
    ln_core(P, z, g_bc, b_bc, ys, tmp)


def ln_core(P, z, g_bc, b_bc, ys, tmp):
    stats, mv, rstd = tmp["stats"], tmp["mv"], tmp["rstd"]
    for n in range(2):
        sl = slice(n * 512, (n + 1) * 512)
        so, zi = stats[:, n, :].ap, z[:, sl].ap
        P.op("dve", lambda e, so=so, zi=zi: e.bn_stats(so, zi), [z], [stats])
    mo, si = mv.ap, stats.re("p a b -> p (a b)").ap
    P.op("dve", lambda e: e.bn_aggr(mo, si), [stats], [mv])
    P.ts(rstd, mv[:, 1:2], LN_EPS, ALU.add)
    P.act(rstd, rstd, AF.Sqrt)
    ro = rstd.ap
    P.op("dve", lambda e: e.reciprocal(ro, ro), [rstd], [rstd])
    P.ts(z, z, mv[:, 0:1], ALU.subtract, rstd[:, 0:1], ALU.mult)
    P.tt(z, z, g_bc, ALU.mult, eng="pool")
    P.tt(ys, z, b_bc, ALU.add, eng="pool")


def stage_ffn(P, S, xin, wgu_d, wd_d, lng_d, lnb_d, yout, ident_d):
    P.stage_begin()
    TT = 256
    NT = S // TT
    ident = P.sb([128, 128], BF16, "ident")
    wgu = P.sb([128, 8, NF * 256], BF16, "wgu")
    wd = P.sb([128, NF, D], BF16, "wd")
    g_bc = P.sb([128, D], F32, "g_bc")
    b_bc = P.sb([128, D], F32, "b_bc")
    ds_c = P.dsem("ffn_c")
    P.dma("pool", ident, ident_d, ds_c)
    P.dma("sp", g_bc, lng_d.pbc(128), ds_c)
    P.dma("sp", b_bc, lnb_d.pbc(128), ds_c)
    wgu_chunks = [wgu[:, :, f * 512:(f + 1) * 512].sub(f"wgu{f}") for f in range(NF // 2)]
    wd_chunks = [wd[:, f * 2:(f + 1) * 2, :].sub(f"wd{f}") for f in range(NF // 2)]
    wgu_src = wgu_d.re("(k p) n -> p k n", p=128)
    wd_src = wd_d.re("(f p) n -> p f n", p=128)

    xs = [[P.sb([128, D], F32, f"xs{s}{u}") for u in range(2)] for s in range(2)]
    xb = [[P.sb([128, D], BF16, f"xb{s}{u}") for u in range(2)] for s in range(2)]
    ds_x = [P.dsem(f"ffn_x{s}") for s in range(2)]
    xT = [P.sb([128, 8, TT], BF16, f"xT{s}") for s in range(2)]
    hT = P.sb([128, NF, TT], BF16, "hT")
    sg = [P.sb([128, TT], F32, f"sg{s}") for s in range(2)]
    z = [P.sb([128, D], F32, f"z{s}") for s in range(2)]
    ys = [P.sb([128, D], F32, f"ys{s}") for s in range(2)]
    ds_y = [P.dsem(f"ffn_y{s}") for s in range(2)]
    tmp = [dict(stats=P.sb([128, 2, 6], F32), mv=P.sb([128, 2], F32), rstd=P.sb([128, 1], F32)) for _ in range(2)]
    tp = P.bank(0, BF16, "tp")
    gu = [P.bank(1 + i, F32, f"gu{i}").re("p (a b) -> p a b", a=2) for i in range(2)]
    od = [[P.bank(3 + 2 * u + n, F32, f"od{u}{n}") for n in range(2)] for u in range(2)]

    def load(t):
        s = t % 2
        for u in range(2):
            rows = slice(t * TT + u * 128, t * TT + (u + 1) * 128)
            P.dma("sp", xs[s][u], xin[rows, :], ds_x[s])
            P.dma("pool", xb[s][u], xin[rows, :], ds_x[s])

    wl = {"gu": 0, "d": 0}

    def compute(t):
        s = t % 2
        for u in range(2):
            for k in range(8):
                P.tr(tp[:, k * 128:(k + 1) * 128], xb[s][u][:, k * 128:(k + 1) * 128], ident)
            P.copy(xT[s][:, :, u * 128:(u + 1) * 128], tp.re("p (k t) -> p k t", k=8), eng="act")
        for f in range(NF):
            if t == 0 and f % 2 == 0:
                c = f // 2
                P.wdma(wgu_chunks[c], wgu_src[:, :, c * 512:(c + 1) * 512])
            wch = wgu_chunks[f // 2]
            g = gu[f % 2]
            for half in range(2):
                for k in range(8):
                    c0 = (f % 2) * 256 + half * 128
                    P.mm(g[:, half, :], wch[:, k, c0:c0 + 128], xT[s][:, k, :], start=(k == 0), stop=(k == 7))
            sgt = sg[f % 2]
            P.act(sgt, g[:, 0, :], AF.Silu)
            P.tt(hT[:, f, :], sgt, g[:, 1, :], ALU.mult)
        for u in range(2):
            for n in range(2):
                for f in range(NF):
                    if t == 0 and u == 0 and n == 0 and f % 2 == 0:
                        c = f // 2
                        P.wdma(wd_chunks[c], wd_src[:, c * 2:(c + 1) * 2, :])
                    P.mm(od[u][n], hT[:, f, u * 128:(u + 1) * 128], wd_chunks[f // 2][:, f % 2, n * 512:(n + 1) * 512],
                         start=(f == 0), stop=(f == NF - 1))
        for u in range(2):
            ln_epilogue(P, z[u], xs[s][u], od[u], g_bc, b_bc, ys[u], tmp[u])
            rows = slice(t * TT + u * 128, t * TT + (u + 1) * 128)
            P.dma("sp", yout[rows, :], ys[u], ds_y[u])

    load(0)
    for t in range(NT):
        if t + 1 < NT:
            load(t + 1)
        compute(t)


RMS_EPS = 1e-6
NH = 16
QR = 768
KVR = 256
MAGIC = 12582912.0
TWO_PI = 2.0 * math.pi


def rms_stats(P, srcs, n, eps, stats, mv, rstd):
    k = len(srcs)
    for i, v in enumerate(srcs):
        so, zi = stats[:, i, :].ap, v.ap
        P.op("dve", lambda e, so=so, zi=zi: e.bn_stats(so, zi), [v], [stats])
    mo, si = mv.ap, stats[:, 0:k, :].re("p a b -> p (a b)").ap
    P.op("dve", lambda e: e.bn_aggr(mo, si), [stats], [mv])
    P.stt(rstd, mv[:, 0:1], mv[:, 0:1], mv[:, 1:2], ALU.mult, ALU.add)
    P.ts(rstd, rstd, eps, ALU.add)
    P.act(rstd, rstd, AF.Sqrt)
    ro = rstd.ap
    P.op("dve", lambda e: e.reciprocal(ro, ro), [rstd], [rstd])


def rope(P, dst1, dst2, x1, x2, cos, sin, t):
    P.tt(t[0], x1, cos, ALU.mult)
    P.tt(t[1], x2, sin, ALU.mult)
    P.tt(dst1, t[0], t[1], ALU.subtract)
    P.tt(t[2], x1, sin, ALU.mult)
    P.tt(t[3], x2, cos, ALU.mult)
    P.tt(dst2, t[2], t[3], ALU.add)


def stage_mla_proj(P, S, xin, pos_d, invf_d, win_d, qng_d, wqb_d, kvng_d, wkvb_d, ident_d, qT_d, kT_d, vA_d):
    P.stage_begin()
    NT = S // 128
    ident = P.sb([128, 128], BF16, "ident")
    win = P.sb([128, 8, 1056], BF16, "win")
    wqb = P.sb([128, 6, 1536], BF16, "wqb")
    wkvb = P.sb([128, 2, 2048], BF16, "wkvb")
    qng = P.sb([128, QR], F32, "qng")
    kvng = P.sb([128, KVR], F32, "kvng")
    invf = P.sb([128, 16], F32, "invf")
    posi = P.sb([128, NT], I32, "posi")
    posf = P.sb([128, NT], F32, "posf")
    ang = P.sb([128, NT, 16], F32, "ang")
    tn = P.sb([128, NT, 16], F32, "tn")
    cosT = P.sb([128, NT, 16], F32, "cos")
    sinT = P.sb([128, NT, 16], F32, "sin")
    ds_c = P.dsem("mp_c")
    P.dma("pool", ident, ident_d, ds_c)
    P.dma("sp", qng, qng_d.pbc(128), ds_c)
    P.dma("sp", kvng, kvng_d.pbc(128), ds_c)
    P.dma("sp", invf, invf_d.pbc(128), ds_c)
    P.dma("sp", posi, pos_d, ds_c)
    win_src = win_d.re("(k p) n -> p k n", p=128)
    for k in range(0, 8, 2):
        P.wdma(win[:, k:k + 2, :].sub(f"win{k}"), win_src[:, k:k + 2, :])
    wqb_src = wqb_d.re("(k p) n -> p k n", p=128)
    for k in range(0, 6, 2):
        P.wdma(wqb[:, k:k + 2, :].sub(f"wqb{k}"), wqb_src[:, k:k + 2, :])
    P.wdma(wkvb.sub("wkvb"), wkvb_d.re("(k p) n -> p k n", p=128))
    P.barrier()
    P.copy(posf, posi)
    P.tt(ang, posf.unsq(2).bc([128, NT, 16]), invf.unsq(1).bc([128, NT, 16]), ALU.mult)
    for (dst, shift) in ((sinT, 0.0), (cosT, math.pi / 2)):
        a2 = ang
        if shift:
            P.ts(dst, ang, shift, ALU.add)
            a2 = dst
        P.ts(tn, a2, 1.0 / TWO_PI, ALU.mult, MAGIC, ALU.add)
        P.ts(tn, tn, MAGIC, ALU.subtract)
        P.stt(dst, tn, -TWO_PI, a2, ALU.mult, ALU.add)
        P.ts(dst, dst, 3.14159, ALU.min, -3.14159, ALU.max)
        P.act(dst, dst, AF.Sin)

    xb = [P.sb([128, D], BF16, f"xb{s}") for s in range(2)]
    ds_x = [P.dsem(f"mp_x{s}") for s in range(2)]
    xT = P.sb([128, 8, 128], BF16, "xT")
    cqn = P.sb([128, QR], BF16, "cqn")
    ckvn = P.sb([128, KVR], BF16, "ckvn")
    cqT = P.sb([128, 6, 128], BF16, "cqT")
    ckvT = P.sb([128, 2, 128], BF16, "ckvT")
    Qf = P.sb([128, NH, 96], BF16, "Qf")
    Kf = P.sb([128, NH, 96], BF16, "Kf")
    Va = [P.sb([128, NH, 65], BF16, f"Va{s}") for s in range(2)]
    ds_v = [P.dsem(f"mp_v{s}") for s in range(2)]
    kr = P.sb([128, 32], BF16, "kr")
    rt = [P.sb([128, NH, 16], F32, f"rt{i}") for i in range(4)]
    QTs = [P.sb([96, NH, 512], BF16, f"QTs{s}") for s in range(2)]
    KTs = [P.sb([96, NH, 512], BF16, f"KTs{s}") for s in range(2)]
    ds_q = [P.dsem(f"mp_q{s}") for s in range(2)]
    stats = P.sb([128, 2, 6], F32, "stats")
    mv = P.sb([128, 2], F32, "mv")
    rstd = P.sb([128, 1], F32, "rstd")
    for s in range(2):
        P.memset(Va[s][:, :, 64:65], 1.0)
    tp = P.bank(0, BF16, "tp")
    bk = [None] + [P.bank(i, F32, f"b{i}") for i in range(1, 8)]
    qT_v = qT_d.re("h d s -> d h s")
    kT_v = kT_d.re("h d s -> d h s")

    def load(t):
        P.dma("pool", xb[t % 2], xin[t * 128:(t + 1) * 128, :], ds_x[t % 2])

    def compute(t):
        s = t % 2
        for k in range(8):
            P.tr(tp[:, k * 128:(k + 1) * 128], xb[s][:, k * 128:(k + 1) * 128], ident)
        P.copy(xT, tp.re("p (k t) -> p k t", k=8), eng="act")
        for n, (c0, c1) in enumerate(((0, 512), (512, 1024), (1024, 1056))):
            for k in range(8):
                P.mm(bk[1 + n][:, 0:c1 - c0], xT[:, k, :], win[:, k, c0:c1], start=(k == 0), stop=(k == 7))
        rms_stats(P, [bk[1], bk[2][:, 0:256]], QR, RMS_EPS, stats, mv, rstd)
        P.stt(cqn[:, 0:512], bk[1], rstd[:, 0:1], qng[:, 0:512], ALU.mult, ALU.mult)
        P.stt(cqn[:, 512:768], bk[2][:, 0:256], rstd[:, 0:1], qng[:, 512:768], ALU.mult, ALU.mult)
        rms_stats(P, [bk[2][:, 256:512]], KVR, RMS_EPS, stats, mv, rstd)
        P.stt(ckvn, bk[2][:, 256:512], rstd[:, 0:1], kvng, ALU.mult, ALU.mult)
        cs = cosT[:, t, :]
        sn = sinT[:, t, :]
        kp = bk[3][:, 0:32].re("p (i two) -> p i two", two=2)
        krv = kr.re("p (i two) -> p i two", two=2)
        rope(P, krv[:, :, 0], krv[:, :, 1], kp[:, :, 0], kp[:, :, 1], cs, sn, [r[:, 0, :] for r in rt])
        for k in range(6):
            P.tr(tp[:, k * 128:(k + 1) * 128], cqn[:, k * 128:(k + 1) * 128], ident)
        P.copy(cqT, tp[:, 0:768].re("p (k t) -> p k t", k=6), eng="act")
        for k in range(2):
            P.tr(tp[:, k * 128:(k + 1) * 128], ckvn[:, k * 128:(k + 1) * 128], ident)
        P.copy(ckvT, tp[:, 0:256].re("p (k t) -> p k t", k=2), eng="act")
        for n in range(3):
            for k in range(6):
                P.mm(bk[4 + n], cqT[:, k, :], wqb[:, k, n * 512:(n + 1) * 512], start=(k == 0), stop=(k == 5))
        kvb = [bk[1], bk[2], bk[3], bk[7]]
        for n in range(4):
            for k in range(2):
                P.mm(kvb[n], ckvT[:, k, :], wkvb[:, k, n * 512:(n + 1) * 512], start=(k == 0), stop=(k == 1))
        for n in range(2):
            P.copy(Qf[:, n * 8:(n + 1) * 8, 0:64], bk[4 + n].re("p (h d) -> p h d", h=8), eng="act")
        qp = bk[6].re("p (h i two) -> p h i two", h=NH, two=2)
        qd = Qf[:, :, 64:96].re("p h (i two) -> p h i two", two=2)
        csb = cs.unsq(1).bc([128, NH, 16])
        snb = sn.unsq(1).bc([128, NH, 16])
        rope(P, qd[:, :, :, 0], qd[:, :, :, 1], qp[:, :, :, 0], qp[:, :, :, 1], csb, snb, rt)
        for n in range(4):
            kv4 = kvb[n].re("p (h d) -> p h d", h=4)
            P.copy(Kf[:, n * 4:(n + 1) * 4, 0:64], kv4[:, :, 0:64], eng="act")
            P.copy(Va[s][:, n * 4:(n + 1) * 4, 0:64], kv4[:, :, 64:128])
        P.copy(Kf[:, :, 64:96], kr.unsq(1).bc([128, NH, 32]))
        w = (t % 4) * 128
        slot = (t // 4) % 2
        for (src, dst) in ((Qf, QTs[slot]), (Kf, KTs[slot])):
            for half in range(2):
                for hh in range(8):
                    h = half * 8 + hh
                    P.tr(tp[0:96, hh * 128:(hh + 1) * 128], src[:, h, :], ident)
                P.copy(dst[:, half * 8:(half + 1) * 8, w:w + 128], tp[0:96, :].re("p (h t) -> p h t", h=8), eng="act")
        P.dma("sp", vA_d[t * 128:(t + 1) * 128, :], Va[s].re("p h d -> p (h d)"), ds_v[s])
        if t % 4 == 3:
            win0 = (t // 4) * 512
            P.dma("sp", qT_v[:, :, win0:win0 + 512], QTs[slot], ds_q[slot])
            P.dma("sp", kT_v[:, :, win0:win0 + 512], KTs[slot], ds_q[slot])

    load(0)
    for t in range(NT):
        if t + 1 < NT:
            load(t + 1)
        compute(t)


def stage_attn(P, S, qT_d, kT_d, vA_d, oT_d, ones_d):
    P.stage_begin()
    NJ = S // 128
    NG = S // 512
    scale = 96.0 ** -0.5
    Vall = P.sb([128, NJ, NH * 65], BF16, "Vall")
    ones = P.sb([128, 64], F32, "ones")
    ds_c = P.dsem("at_c")
    P.dma("sp", ones, ones_d, ds_c)
    vsrc = vA_d.re("(j p) c -> p j c", p=128)
    step = max(1, NJ // 4)
    for j0 in range(0, NJ, step):
        P.dma("sp", Vall[:, j0:j0 + step, :], vsrc[:, j0:j0 + step, :], ds_c)
    KT = [P.sb([96, S], BF16, f"KT{s}") for s in range(2)]
    QT = [P.sb([96, S], BF16, f"QT{s}") for s in range(2)]
    ds_h = [P.dsem(f"at_h{s}") for s in range(2)]
    OT = [P.sb([64, S], BF16, f"OT{s}") for s in range(2)]
    ds_o = [P.dsem(f"at_o{s}") for s in range(2)]
    PT = [P.sb([128, 512], BF16, f"PT{i}") for i in range(4)]
    osb = [P.sb([65, 512], F32, f"osb{i}") for i in range(2)]
    sc = [P.bank(i, F32, f"sc{i}") for i in range(4)]
    oacc = [P.bank(4 + i, F32, f"oacc{i}") for i in range(2)]
    bcp = P.bank(6, F32, "bcp")

    def load(h):
        P.dma("sp", KT[h % 2], kT_d[h], ds_h[h % 2])
        P.dma("sp", QT[h % 2], qT_d[h], ds_h[h % 2])

    items = [(h, g, j) for h in range(NH) for g in range(NG) for j in range(4 * g + 4)]
    LOOK = 2
    pending = []

    def qk(idx):
        h, g, j = items[idx]
        if g == 0 and j == 0 and h + 1 < NH:
            load(h + 1)
        m = j - 4 * g
        c0 = max(0, m) * 128
        n = 512 - c0
        P.mm(sc[idx % 4][:, 0:n], KT[h % 2][:, j * 128:(j + 1) * 128], QT[h % 2][:, g * 512 + c0:(g + 1) * 512])

    def pv(idx):
        h, g, j = items[idx]
        nj = 4 * g + 4
        m = j - 4 * g
        c0 = max(0, m) * 128
        n = 512 - c0
        scb, pt, oa = sc[idx % 4], PT[idx % 4], oacc[g % 2]
        P.act(pt[:, 0:n], scb[:, 0:n], AF.Exp, scale=scale)
        if m >= 0:
            P.memset(pt[64:128, 0:64], 0.0, eng="pool")
        P.mm(oa[0:65, c0:512], Vall[:, j, h * 65:(h + 1) * 65], pt[:, 0:n], start=(j == 0), stop=(j == nj - 1))
        if j == nj - 1:
            ob = osb[g % 2]
            ot = OT[h % 2]
            P.copy(ob, oa[0:65, :], eng="act")
            ro = ob[64:65, :].ap
            P.op("dve", lambda e, ro=ro: e.reciprocal(ro, ro), [ob], [ob])

            def fin(h=h, g=g, ob=ob, ot=ot):
                P.mm(bcp[0:64, :], ones[64:65, 0:64], ob[64:65, :])
                P.tt(ot[:, g * 512:(g + 1) * 512], ob[0:64, :], bcp[0:64, :], ALU.mult)
                if g == NG - 1:
                    P.dma("sp", oT_d[h], ot, ds_o[h % 2])
            pending.append([idx + 3, fin])

    load(0)
    n_it = len(items)
    for idx in range(n_it + LOOK):
        if idx < n_it:
            qk(idx)
        if idx - LOOK >= 0:
            pv(idx - LOOK)
        while pending and pending[0][0] <= idx - LOOK:
            pending.pop(0)[1]()
    while pending:
        pending.pop(0)[1]()


def stage_mla_out(P, S, oT_d, xin, wout_d, lng_d, lnb_d, yout):
    P.stage_begin()
    NW = S // 512
    wout = P.sb([64, NH, D], BF16, "wout")
    g_bc = P.sb([128, D], F32, "g_bc")
    b_bc = P.sb([128, D], F32, "b_bc")
    ds_c = P.dsem("mo_c")
    P.dma("sp", g_bc, lng_d.pbc(128), ds_c)
    P.dma("sp", b_bc, lnb_d.pbc(128), ds_c)
    P.wdma(wout, wout_d.re("(h d) n -> d h n", d=64))
    OTw = [P.sb([64, NH, 512], BF16, f"OTw{s}") for s in range(2)]
    ds_w = [P.dsem(f"mo_w{s}") for s in range(2)]
    xs = [P.sb([128, D], F32, f"xs{s}") for s in range(2)]
    ds_x = [P.dsem(f"mo_x{s}") for s in range(2)]
    z = [P.sb([128, D], F32, f"z{s}") for s in range(2)]
    ys = [P.sb([128, D], F32, f"ys{s}") for s in range(2)]
    ds_y = [P.dsem(f"mo_y{s}") for s in range(2)]
    tmp = [dict(stats=P.sb([128, 2, 6], F32), mv=P.sb([128, 2], F32), rstd=P.sb([128, 1], F32)) for _ in range(2)]
    od = [[P.bank(2 * u + n, F32, f"od{u}{n}") for n in range(2)] for u in range(2)]
    oT_v = oT_d.re("h d s -> d h s")

    def loadw(w):
        P.dma("sp", OTw[w % 2], oT_v[:, :, w * 512:(w + 1) * 512], ds_w[w % 2])

    def loadx(t):
        P.dma("sp", xs[t % 2], xin[t * 128:(t + 1) * 128, :], ds_x[t % 2])

    loadw(0)
    loadx(0)
    for w in range(NW):
        if w + 1 < NW:
            loadw(w + 1)
        for u4 in range(4):
            t = w * 4 + u4
            if t + 1 < S // 128:
                loadx(t + 1)
            u = t % 2
            for n in range(2):
                for h in range(NH):
                    P.mm(od[u][n], OTw[w % 2][:, h, u4 * 128:(u4 + 1) * 128], wout[:, h, n * 512:(n + 1) * 512],
                         start=(h == 0), stop=(h == NH - 1))
            ln_epilogue(P, z[u], xs[t % 2], od[u], g_bc, b_bc, ys[u], tmp[u])
            P.dma("sp", yout[t * 128:(t + 1) * 128, :], ys[u], ds_y[u])


EC_STOP = 99
EC_VAR = ''

EC = math.exp(-0.5)
CB_R, CB_A, CB_B, CB_K, CB_BH, CB_KH, CB_V = [i * 512 for i in range(7)]
CB_GQ, CB_GK, CB_GKH, CB_GV = 3584, 3840, 4096, 4352
WB = 4864
CF_BONUS, CF_G, CF_GS = 0, 512, 1024
WF = 1536


def rsqrt_lnexp(P, out, in_, add=None, mx=None, scale_in=None):
    if scale_in is not None:
        P.ts(out, in_, scale_in, ALU.mult, add if add is not None else 0.0, ALU.add)
    elif mx is not None:
        P.ts(out, in_, mx, ALU.max)
    else:
        P.ts(out, in_, add, ALU.add)
    P.act(out, out, AF.Ln)
    P.act(out, out, AF.Exp, scale=-0.5)


def sigmoid_(P, out, in_, tmp=None):
    P.act(out, in_, AF.Tanh, scale=0.5)
    P.ts(out, out, 0.5, ALU.mult, 0.5, ALU.add)


def stage_even_prep(P, S, xin, c, prepB_d, prepF_d, gam_d):
    P.stage_begin()
    NT = S // 128
    ds_c = P.dsem("ep_c")
    ident = P.sb([128, 128], BF16, "ident")
    P.dma("pool", ident, c["ident"], ds_c)
    win = P.sb([128, 8, 3376], BF16, "win")
    src = c["even_w_in"].re("(k p) n -> p k n", p=128)
    for k in range(8):
        P.wdma(win[:, k:k + 1, :].sub(f"win{k}"), src[:, k:k + 1, :])
    Wwa = P.sb([128, 512], BF16, "Wwa")
    P.wdma(Wwa[0:64, :].sub("w2"), c["rwkv_w2"])
    P.wdma(Wwa[64:128, :].sub("a2"), c["rwkv_a2"])
    G2 = P.sb([128, 2, 512], BF16, "G2")
    P.wdma(G2[:, 0, :].sub("g2a"), c["rwkv_g2"][0:128, :])
    P.wdma(G2[0:32, 1, :].sub("g2b"), c["rwkv_g2"][128:160, :])
    GW = P.sb([48, 256], BF16, "GW")
    P.wdma(GW[32:48, :].sub("gw"), c["gla_gate_w2"])

    def bc(name, n):
        t = P.sb([128, n], F32, name)
        P.dma("sp", t, c[name].pbc(128), ds_c)
        return t
    mu = bc("rwkv_mu", 1824)
    w0 = bc("rwkv_w0", 512)
    a0 = bc("rwkv_a0", 512)
    k_k = bc("rwkv_k_k", 512)
    k_a = bc("rwkv_k_a", 512)
    r_k = bc("rwkv_r_k", 512)
    gate_b = bc("gla_gate_b", 256)

    def cst(name, shape):
        t = P.sb(shape, F32, name)
        P.dma("sp", t, c[name], ds_c)
        return t
    TRI_i = cst("tri_incl", [128, 128])
    TRI_x = cst("tri_excl", [128, 128])
    TRI_r = cst("tri_rev", [128, 128])
    TRIg_i = cst("trig_incl", [128, 128])
    TRIg_r = cst("trig_rev", [128, 128])
    CBk = cst("cb", [128, 2])
    CBg = cst("cbg", [128, 2])
    P.barrier()

    xb = [P.sb([128, 1024], BF16, f"xb{s}") for s in range(2)]
    ds_x = [P.dsem(f"ep_x{s}") for s in range(2)]
    xT = P.sb([128, 8, 128], BF16, "xT")
    Psb = P.sb([128, 3376], F32, "Psb")
    Sh = [P.sb([128, 1824], F32, f"Sh{s}") for s in range(2)]
    ds_sh = [P.dsem(f"ep_sh{s}") for s in range(2)]
    Lin = P.sb([128, 304], BF16, "Lin")
    LT0 = P.sb([128, 128], BF16, "LT0")
    LT1 = P.sb([128, 128], BF16, "LT1")
    LT2 = P.sb([48, 128], BF16, "LT2")
    sigw = P.sb([128, 512], F32, "sigw")
    av = P.sb([128, 512], F32, "a")
    kkn = P.sb([128, 512], F32, "kkn")
    sq = P.sb([128, 512], F32, "sq")
    k2 = P.sb([128, 512], F32, "k2")
    bvec = P.sb([128, 512], F32, "bvec")
    E = P.sb([128, 512], F32, "E")
    ss = P.sb([128, 8], F32, "ss")
    rks = P.sb([128, 8], F32, "rks")
    sp = P.sb([128, 256], F32, "sp")
    Eg = P.sb([128, 256], F32, "Eg")
    OB = [P.sb([128, WB], BF16, f"OB{s}") for s in range(2)]
    OF = [P.sb([128, WF], F32, f"OF{s}") for s in range(2)]
    GM = [P.sb([64, 24], F32, f"GM{s}") for s in range(2)]
    ds_o = [P.dsem(f"ep_o{s}") for s in range(2)]
    ShA = [Sh[i][:, 0:1536].sub(f"ShA{i}") for i in range(2)]
    ShB = [Sh[i][:, 1536:1824].sub(f"ShB{i}") for i in range(2)]
    ds_shb = [P.dsem(f"ep_shb{s}") for s in range(2)]
    P.memset(ShA[0][0:1, :], 0.0)
    P.memset(ShB[0][0:1, :], 0.0)
    tp = P.bank(7, BF16, "tp")
    rb = [P.bank(i, F32, f"rb{i}") for i in range(7)]
    rbi = [0]

    def nb():
        b = rb[rbi[0] % 7]
        rbi[0] += 1
        return b

    def load(t):
        P.dma("pool", xb[t % 2], xin[t * 128:(t + 1) * 128, :], ds_x[t % 2])

    def compute(t):
        s = t % 2
        ob, of, gm = OB[s], OF[s], GM[s]
        for k in range(8):
            P.tr(tp[:, k * 128:(k + 1) * 128], xb[s][:, k * 128:(k + 1) * 128], ident)
        P.copy(xT, tp.re("p (k t) -> p k t", k=8), eng="act")
        for n in range(7):
            c0, c1 = n * 512, min(3376, (n + 1) * 512)
            b = nb()
            for k in range(8):
                P.mm(b[:, 0:c1 - c0], xT[:, k, :], win[:, k, c0:c1], start=(k == 0), stop=(k == 7))
            P.copy(Psb[:, c0:c1], b[:, 0:c1 - c0], eng=("act" if n % 2 == 0 else "dve"))
        sh = Sh[s]
        sha, shb = ShA[s], ShB[s]
        P.dma("sp", sha[1:128, :], Psb[0:127, 1536:3072], ds_sh[s])
        P.dma("sp", shb[1:128, :], Psb[0:127, 3072:3360], ds_shb[s])
        if t + 1 < NT:
            P.dma("sp", ShA[1 - s][0:1, :], Psb[127:128, 1536:3072], ds_sh[1 - s])
            P.dma("sp", ShB[1 - s][0:1, :], Psb[127:128, 3072:3360], ds_shb[1 - s])
        for (v_, c0, c1, m0, eng_) in ((shb, 3072, 3360, 1536, "pool"), (sha, 1536, 3072, 0, "dve")):
            pr_ = Psb[:, c0:c1]
            mu_ = mu[:, m0:m0 + (c1 - c0)]
            P.tt(v_, v_, pr_, ALU.subtract, eng=eng_)
            P.tt(v_, v_, mu_, ALU.mult, eng=eng_)
            P.tt(v_, v_, pr_, ALU.add, eng=eng_)
        r, kx, vx = sha[:, 0:512], sha[:, 512:1024], sha[:, 1024:1536]
        wl, al, gl = shb[:, 0:64], shb[:, 64:128], shb[:, 128:288]
        P.act(Lin[:, 0:64], wl, AF.Tanh)
        P.copy(Lin[:, 64:128], al)
        P.act(sq[:, 0:160], gl, AF.Tanh, scale=0.5)
        P.ts(Lin[:, 128:288], sq[:, 0:160], 0.5, ALU.mult, 0.5, ALU.add)
        P.copy(Lin[:, 288:304], Psb[:, 3360:3376])
        for i, (lt, c0, c1) in enumerate(((LT0, 0, 128), (LT1, 128, 256), (LT2, 256, 304))):
            P.tr(tp[0:c1 - c0, i * 128:(i + 1) * 128], Lin[:, c0:c1], ident)
            P.copy(lt, tp[0:c1 - c0, i * 128:(i + 1) * 128], eng="act")
        wpre, apre, lapre, gpre = rb[4], rb[5], rb[6], rb[3]
        cum, cumx, rev, gmb = rb[0], rb[1], rb[2], rb[3]
        cumg, revg = rb[4], rb[5]
        P.mm(wpre, LT0[0:64, :], Wwa[0:64, :])
        P.mm(apre, LT0[64:128, :], Wwa[64:128, :])
        P.mm(lapre[:, 0:256], LT2[32:48, :], GW[32:48, :])
        P.mm(gpre, LT1, G2[:, 0, :], start=True, stop=False)
        P.mm(gpre, LT2[0:32, :], G2[0:32, 1, :], start=False, stop=True)
        P.copy(of[:, CF_G:CF_G + 512], gpre, eng="act")
        gq, gk = Psb[:, 0:256], Psb[:, 256:512]
        gv, gg = Psb[:, 512:1024], Psb[:, 1024:1536]
        gs = of[:, CF_GS:CF_GS + 512]

        def chainA():
            P.tt(sigw, wpre, w0, ALU.add)
            sigmoid_(P, sigw, sigw)
            P.mm(cum, TRI_i, sigw)
            P.mm(cumx, TRI_x, sigw)
            P.mm(rev, TRI_r, sigw)
            for h in range(8):
                P.mm(gmb[0:64, h * 2:(h + 1) * 2], sigw[:, h * 64:(h + 1) * 64], CBk)

        def chainB():
            P.tt(av, apre, a0, ALU.add)
            sigmoid_(P, av, av)
            P.tt(kkn, kx, k_k, ALU.mult)
            P.tt(sq, kkn, kkn, ALU.mult)
            P.reduce(ss, sq.re("p (h k) -> p h k", h=8))
            rsqrt_lnexp(P, ss, ss, mx=1e-24)
            P.tt(kkn.re("p (h k) -> p h k", h=8), kkn.re("p (h k) -> p h k", h=8), ss.unsq(2).bc([128, 8, 64]), ALU.mult)
            P.stt(k2, av, -1.0, k_a, ALU.add, ALU.mult)
            P.stt(k2, k2, 1.0, kx, ALU.add, ALU.mult)
            P.tt(bvec, kkn, av, ALU.mult)
            P.copy(ob[:, CB_V:CB_V + 512], vx, eng="act")

        def chainC():
            P.tt(sp, lapre[:, 0:256], gate_b, ALU.add)
            P.act(sp, sp, AF.Exp, scale=-1.0)
            P.act(sp, sp, AF.Ln, bias=1.0)
            P.mm(cumg[:, 0:256], TRIg_i, sp)
            P.mm(revg[:, 0:256], TRIg_r, sp)
            P.act(Eg, cumg[:, 0:256], AF.Exp)
            P.stt(ob[:, CB_GQ:CB_GQ + 256], gq, 0.125, Eg, ALU.mult, ALU.mult)
            P.act(Eg, cumg[:, 0:256], AF.Exp, scale=-1.0)
            P.tt(ob[:, CB_GK:CB_GK + 256], gk, Eg, ALU.mult)
            P.act(Eg, revg[:, 0:256], AF.Exp)
            P.tt(ob[:, CB_GKH:CB_GKH + 256], gk, Eg, ALU.mult)
            P.copy(ob[:, CB_GV:CB_GV + 512], gv, eng="act")
            sigmoid_(P, gs, gg)
            P.tt(gs, gs, gg, ALU.mult)

        P.interleave([chainA, chainB, chainC])
        for h in range(4):
            P.mm(gmb[0:64, 16 + h * 2:16 + (h + 1) * 2], sp[:, h * 64:(h + 1) * 64], CBg)
        P.act(gm, gmb[0:64, 0:24], AF.Exp)
        P.dma("sp", gam_d[t], gm, ds_o[s])
        P.act(E, cum, AF.Exp)
        P.tt(ob[:, CB_R:CB_R + 512], r, E, ALU.mult)
        P.act(E, cumx, AF.Exp)
        P.stt(ob[:, CB_A:CB_A + 512], kkn, -1.0, E, ALU.mult, ALU.mult)
        P.act(E, cum, AF.Exp, scale=-1.0)
        P.tt(ob[:, CB_B:CB_B + 512], bvec, E, ALU.mult)
        P.tt(ob[:, CB_K:CB_K + 512], k2, E, ALU.mult)
        P.act(E, rev, AF.Exp)
        P.tt(ob[:, CB_BH:CB_BH + 512], bvec, E, ALU.mult)
        P.tt(ob[:, CB_KH:CB_KH + 512], k2, E, ALU.mult)
        P.tt(sq, r, k2, ALU.mult)
        P.tt(sq, sq, r_k, ALU.mult)
        P.reduce(rks, sq.re("p (h k) -> p h k", h=8))
        P.tt(of[:, CF_BONUS:CF_BONUS + 512].re("p (h k) -> p h k", h=8), vx.re("p (h k) -> p h k", h=8),
             rks.unsq(2).bc([128, 8, 64]), ALU.mult)
        rows = slice(t * 128, (t + 1) * 128)
        P.dma("sp", prepB_d[rows, :], ob, ds_o[s])
        P.dma("sp", prepF_d[rows, :], of, ds_o[s])

    load(0)
    for t in range(NT):
        if t + 1 < NT:
            load(t + 1)
        compute(t)


def stage_even_chunk(P, S, c, prepB_d, prepF_d, gam_d, oT_d):
    P.stage_begin()
    NT = S // 128
    ds_c = P.dsem("ec_c")
    ident = P.sb([128, 128], BF16, "ident")
    P.dma("pool", ident, c["ident"], ds_c)

    def cst(name, shape):
        t = P.sb(shape, F32, name)
        P.dma("sp", t, c[name], ds_c)
        return t
    ML_s = cst("ml_strict", [128, 128])
    MU_s = cst("mu_strict", [128, 128])
    MU_i = cst("mu_incl", [128, 128])

    def bc(name, n):
        t = P.sb([128, n], F32, name)
        P.dma("sp", t, c[name].pbc(128), ds_c)
        return t
    ln_g = bc("rwkv_ln_g", 512)
    ln_b = bc("rwkv_ln_b", 512)
    norm_g = bc("gla_norm_g", 128)

    IB = [P.sb([128, WB], BF16, f"IB{s}") for s in range(2)]
    IF = [P.sb([128, WF], F32, f"IF{s}") for s in range(2)]
    GM = [P.sb([64, 24], F32, f"GMi{s}") for s in range(2)]
    ds_i = [P.dsem(f"ec_i{s}") for s in range(2)]

    def fm(name, nh=8):
        return P.sb([64, nh, 128], BF16, name)
    RT, AT, BT, KT = fm("RT"), fm("AT"), fm("BT"), fm("KT")
    RTm = [fm("RTm0"), fm("RTm1")]
    qtT, ktT = fm("qtT", 4), fm("ktT", 4)
    qtTm = [fm("qtTm0", 4), fm("qtTm1", 4)]

    def mat(name, nh=8):
        return P.sb([128, nh, 128], BF16, name)
    Pm = [mat("Pm0"), mat("Pm1")]
    PTm = [mat("PTm0"), mat("PTm1")]
    Qm = [mat("Qm0"), mat("Qm1")]
    AakT, ArbT, ArkT = mat("AakT"), mat("ArbT"), mat("ArkT")
    attnT = mat("attnT", 4)
    WtT = fm("WtT")
    AV = P.sb([128, 8, 64], BF16, "AV")
    Ut = P.sb([128, 8, 64], F32, "Ut")
    Usb = P.sb([128, 8, 64], BF16, "Usb")
    H = P.sb([64, 8, 64], F32, "H")
    Hb = [P.sb([64, 8, 64], BF16, f"Hb{i}") for i in range(3)]
    Sg = P.sb([64, 4, 128], F32, "Sg")
    Sgb = [P.sb([64, 4, 128], BF16, f"Sgb{i}") for i in range(3)]
    Ysb = P.sb([128, 8, 64], F32, "Ysb")
    ysq = P.sb([128, 8, 64], F32, "ysq")
    Osb = P.sb([128, 4, 128], F32, "Osb")
    s1 = P.sb([128, 8], F32, "s1")
    s2 = P.sb([128, 8], F32, "s2")
    MIX = P.sb([128, 1024], BF16, "MIX")
    mixT = [P.sb([128, 8, 128], BF16, f"mixT{s}") for s in range(2)]
    ds_m = [P.dsem(f"ec_m{s}") for s in range(2)]
    BHm = [P.sb([128, 512], BF16, f"BHm{i}") for i in range(2)]
    KHm = [P.sb([128, 512], BF16, f"KHm{i}") for i in range(2)]
    GKHm = [P.sb([128, 256], BF16, f"GKHm{i}") for i in range(2)]
    for tl in RTm + qtTm + BHm + KHm + GKHm + [Usb]:
        P.memset(tl, 0.0)
    P.memset(H, 0.0)
    P.memset(Hb[0], 0.0)
    P.memset(Sg, 0.0)
    P.memset(Sgb[0], 0.0)
    tp = P.bank(7, BF16, "tp")
    Yb = P.bank(6, F32, "Yb")
    Ob = P.bank(5, F32, "Ob")
    rb = [P.bank(i, F32, f"rb{i}") for i in range(5)]
    rbi = [0]

    def nb():
        b = rb[rbi[0] % 5]
        rbi[0] += 1
        return b
    oT_v = oT_d.re("(k two) d s -> (two d) k s", two=2)

    def load(t):
        s = t % 2
        rows = slice(t * 128, (t + 1) * 128)
        P.dma("sp", IB[s], prepB_d[rows, :], ds_i[s])
        P.dma("sp", IF[s], prepF_d[rows, :], ds_i[s])
        P.dma("sp", GM[s], gam_d[t], ds_i[s])

    def transpose_heads(dst, src, c0, nh, dstm=None):
        for h in range(nh):
            P.tr(tp[0:64, h * 128:(h + 1) * 128], src[:, c0 + h * 64:c0 + (h + 1) * 64], ident)
        tv = tp[0:64, 0:nh * 128].re("p (h t) -> p h t", h=nh)
        P.copy(dst, tv, eng="act")
        if dstm is not None and EC_VAR != 'b':
            P.copy(dstm[0][:, :, 0:64], dst[:, :, 0:64], eng="pool")
            P.copy(dstm[1][:, :, 64:128], dst[:, :, 64:128], eng="pool")

    def headmm(dst, lhs, rhs, mask, nh=8):
        for g in range(nh // 4):
            b = nb()
            for hh in range(4):
                h = g * 4 + hh
                P.mm(b[:, hh * 128:(hh + 1) * 128], lhs[:, h, :], rhs[:, h, :])
            bv = b.re("p (h t) -> p h t", h=4)
            if mask is None:
                P.copy(dst[:, g * 4:(g + 1) * 4, :], bv, eng="act")
            else:
                P.tt(dst[:, g * 4:(g + 1) * 4, :], bv, mask.unsq(1).bc([128, 4, 128]), ALU.mult)

    def compute(t):
        s = t % 2
        ib, iff, gm = IB[s], IF[s], GM[s]
        if EC_STOP <= 0:
            return
        transpose_heads(RT, ib, CB_R, 8, RTm)
        transpose_heads(AT, ib, CB_A, 8)
        transpose_heads(BT, ib, CB_B, 8)
        transpose_heads(KT, ib, CB_K, 8)
        if EC_VAR == 'a':
            return
        L0, M0 = PTm[0], Pm[0]
        headmm(L0, AT, BT, ML_s)
        headmm(M0, BT, AT, MU_s)
        headmm(AakT, KT, AT, MU_s)
        headmm(ArbT, BT, RT, MU_i)
        headmm(ArkT, KT, RT, MU_i)
        if EC_STOP <= 1:
            return
        P.tt(Qm[0], M0, ident.unsq(1).bc([128, 8, 128]), ALU.add)
        cur = 0
        for lvl in range(5):
            nxt = 1 - cur
            if lvl < 4:
                headmm(Pm[nxt], PTm[cur], Pm[cur], None)
            headmm(PTm[nxt], Pm[cur], PTm[cur], None)
            for g in range(2):
                b = nb()
                for hh in range(4):
                    h = g * 4 + hh
                    P.mm(b[:, hh * 128:(hh + 1) * 128], PTm[nxt][:, h, :], Qm[cur][:, h, :])
                P.tt(Qm[nxt][:, g * 4:(g + 1) * 4, :], b.re("p (h t) -> p h t", h=4), Qm[cur][:, g * 4:(g + 1) * 4, :], ALU.add)
            cur = nxt
        TT = Qm[cur]
        if EC_STOP <= 2:
            return
        for g in range(2):
            b = nb()
            for hh in range(4):
                h = g * 4 + hh
                P.mm(b[0:64, hh * 128:(hh + 1) * 128], ib[:, CB_A + h * 64:CB_A + (h + 1) * 64], TT[:, h, :])
            P.copy(WtT[:, g * 4:(g + 1) * 4, :], b[0:64, :].re("p (h t) -> p h t", h=4), eng="act")
        b = nb()
        for h in range(8):
            P.mm(b[:, h * 64:(h + 1) * 64], AakT[:, h, :], ib[:, CB_V + h * 64:CB_V + (h + 1) * 64])
        P.copy(AV, b.re("p (h v) -> p h v", h=8), eng="act")
        b = nb()
        for h in range(8):
            P.mm(b[:, h * 64:(h + 1) * 64], TT[:, h, :], AV[:, h, :])
        P.copy(Ut, b.re("p (h v) -> p h v", h=8), eng="act")
        if EC_STOP <= 3:
            return
        transpose_heads(qtT, ib, CB_GQ, 4, qtTm)
        transpose_heads(ktT, ib, CB_GK, 4)
        headmm(attnT, ktT, qtT, MU_i, nh=4)
        if EC_STOP <= 4:
            return
        for cc in range(2):
            rs = slice(cc * 64, (cc + 1) * 64)
            P.copy(BHm[cc][rs, :], ib[rs, CB_BH:CB_BH + 512], eng="pool")
            P.copy(KHm[cc][rs, :], ib[rs, CB_KH:CB_KH + 512], eng="pool")
            P.copy(GKHm[cc][rs, :], ib[rs, CB_GKH:CB_GKH + 256], eng="pool")
        for cc in range(2):
            rs = slice(cc * 64, (cc + 1) * 64)
            hb_in, hb_out = Hb[(2 * t + cc) % 3], Hb[(2 * t + cc + 1) % 3]
            sb_in, sb_out = Sgb[(2 * t + cc) % 3], Sgb[(2 * t + cc + 1) % 3]
            up = nb()
            for h in range(8):
                P.mm(up[:, h * 64:(h + 1) * 64], WtT[:, h, :], hb_in[:, h, :])
            P.tt(Usb[rs, :, :], up[rs, :].re("p (h v) -> p h v", h=8), Ut[rs, :, :], ALU.add)
            dh = nb()
            for h in range(8):
                P.mm(dh[0:64, h * 64:(h + 1) * 64], BHm[cc][:, h * 64:(h + 1) * 64], Usb[:, h, :],
                     start=True, stop=False)
                P.mm(dh[0:64, h * 64:(h + 1) * 64], KHm[cc][:, h * 64:(h + 1) * 64],
                     ib[:, CB_V + h * 64:CB_V + (h + 1) * 64], start=False, stop=True)
            gsel = gm[:, 0:16].re("p (h c) -> p h c", c=2)[:, :, cc]
            P.tt(H, H, gsel.unsq(2).bc([64, 8, 64]), ALU.mult)
            P.tt(H, H, dh[0:64, :].re("p (h v) -> p h v", h=8), ALU.add)
            P.copy(hb_out, H, eng="act")
            ds_ = nb()
            for h in range(4):
                P.mm(ds_[0:64, h * 128:(h + 1) * 128], GKHm[cc][:, h * 64:(h + 1) * 64],
                     ib[:, CB_GV + h * 128:CB_GV + (h + 1) * 128])
            gsel2 = gm[:, 16:24].re("p (h c) -> p h c", c=2)[:, :, cc]
            P.tt(Sg, Sg, gsel2.unsq(2).bc([64, 4, 128]), ALU.mult)
            P.tt(Sg, Sg, ds_[0:64, :].re("p (h v) -> p h v", h=4), ALU.add)
            P.copy(sb_out, Sg, eng="act")
        for h in range(8):
            ys_ = Yb[:, h * 64:(h + 1) * 64]
            P.mm(ys_, RTm[0][:, h, :], Hb[(2 * t) % 3][:, h, :], start=True, stop=False)
            P.mm(ys_, RTm[1][:, h, :], Hb[(2 * t + 1) % 3][:, h, :], start=False, stop=False)
            P.mm(ys_, ArbT[:, h, :], Usb[:, h, :], start=False, stop=False)
            P.mm(ys_, ArkT[:, h, :], ib[:, CB_V + h * 64:CB_V + (h + 1) * 64], start=False, stop=True)
        for h in range(4):
            os_ = Ob[:, h * 128:(h + 1) * 128]
            P.mm(os_, qtTm[0][:, h, :], Sgb[(2 * t) % 3][:, h, :], start=True, stop=False)
            P.mm(os_, qtTm[1][:, h, :], Sgb[(2 * t + 1) % 3][:, h, :], start=False, stop=False)
            P.mm(os_, attnT[:, h, :], ib[:, CB_GV + h * 128:CB_GV + (h + 1) * 128], start=False, stop=True)
        if EC_STOP <= 5:
            return
        P.copy(Ysb, Yb.re("p (h v) -> p h v", h=8), eng="act")
        P.reduce(s1, Ysb)
        P.tt(ysq, Ysb, Ysb, ALU.mult)
        P.reduce(s2, ysq)
        P.ts(s1, s1, 1.0 / 64, ALU.mult)
        P.stt(ysq[:, :, 0], s1, -1.0, s1, ALU.mult, ALU.mult)
        P.stt(s2, s2, 1.0 / 64, ysq[:, :, 0], ALU.mult, ALU.add)
        _r = rsqrt_lnexp
        _r(P, s2, s2, add=64e-5)
        P.tt(Ysb, Ysb, s1.unsq(2).bc([128, 8, 64]), ALU.subtract)
        P.tt(Ysb, Ysb, s2.unsq(2).bc([128, 8, 64]), ALU.mult)
        yf = Ysb.re("p h v -> p (h v)")
        P.tt(yf, yf, ln_g, ALU.mult)
        P.tt(yf, yf, ln_b, ALU.add)
        P.tt(yf, yf, iff[:, CF_BONUS:CF_BONUS + 512], ALU.add)
        P.tt(MIX[:, 512:1024], yf, iff[:, CF_G:CF_G + 512], ALU.mult)
        if EC_STOP <= 6:
            return
        P.copy(Osb, Ob.re("p (h v) -> p h v", h=4), eng="act")
        osq = ysq.re("p h v -> p (h v)").re("p (h v) -> p h v", h=4)
        P.tt(osq, Osb, Osb, ALU.mult)
        P.reduce(s1[:, 0:4], osq)
        _r(P, s1[:, 0:4], s1[:, 0:4], scale_in=1.0 / 128, add=1e-6)
        P.tt(Osb, Osb, s1[:, 0:4].unsq(2).bc([128, 4, 128]), ALU.mult)
        P.tt(Osb, Osb, norm_g.unsq(1).bc([128, 4, 128]), ALU.mult)
        P.tt(MIX[:, 0:512], Osb.re("p h v -> p (h v)"), iff[:, CF_GS:CF_GS + 512], ALU.mult)
        if EC_STOP <= 7:
            return
        mt = mixT[s]
        for k in range(8):
            P.tr(tp[:, k * 128:(k + 1) * 128], MIX[:, k * 128:(k + 1) * 128], ident)
        P.copy(mt, tp.re("p (k t) -> p k t", k=8), eng="act")
        P.dma("sp", oT_v[:, :, t * 128:(t + 1) * 128], mt, ds_m[s])

    load(0)
    for t in range(NT):
        if t + 1 < NT:
            load(t + 1)
        compute(t)


NCORES = 8


def host_consts():
    c = {}
    c["ident"] = np.eye(128, dtype=np.float32)
    idx = np.arange(128)
    same = (idx[:, None] // 64) == (idx[None, :] // 64)
    lt = idx[:, None] < idx[None, :]
    le = idx[:, None] <= idx[None, :]
    gt = idx[:, None] > idx[None, :]
    EC = math.exp(-0.5)
    c["tri_incl"] = (-EC * (same & le)).astype(np.float32)
    c["tri_excl"] = (-EC * (same & lt)).astype(np.float32)
    c["tri_rev"] = (-EC * (same & gt)).astype(np.float32)
    c["trig_incl"] = (-(1.0 / 16) * (same & le)).astype(np.float32)
    c["trig_rev"] = (-(1.0 / 16) * (same & gt)).astype(np.float32)
    cb = np.zeros((128, 2), np.float32)
    cb[0:64, 0] = 1
    cb[64:128, 1] = 1
    c["cb"] = (-EC * cb).astype(np.float32)
    c["cbg"] = (-(1.0 / 16) * cb).astype(np.float32)
    c["ml_strict"] = (same & gt).astype(np.float32)
    c["mu_strict"] = (same & lt).astype(np.float32)
    c["mu_incl"] = (same & le).astype(np.float32)
    c["invf"] = (10000.0 ** (-np.arange(0, 32, 2, dtype=np.float32) / 32)).astype(np.float32)
    c["ones"] = np.ones((128, 64), np.float32)
    return c


CONST_SHAPES = dict(ident=[128, 128], tri_incl=[128, 128], tri_excl=[128, 128], tri_rev=[128, 128],
                    trig_incl=[128, 128], trig_rev=[128, 128], cb=[128, 2], cbg=[128, 2],
                    ml_strict=[128, 128], mu_strict=[128, 128], mu_incl=[128, 128], invf=[16], ones=[128, 64])

WEIGHT_SHAPES = dict(
    even_w_in=[1024, 3376], gla_gate_w2=[16, 256], gla_gate_b=[256], gla_norm_g=[128], rwkv_mu=[1824],
    rwkv_w0=[512], rwkv_w2=[64, 512], rwkv_a0=[512], rwkv_w2_=None, rwkv_a2=[64, 512], rwkv_g2=[160, 512],
    rwkv_k_k=[512], rwkv_k_a=[512], rwkv_r_k=[512], rwkv_ln_g=[512], rwkv_ln_b=[512], even_w_out=[1024, 1024],
    mla_w_in=[1024, 1056], mla_q_norm_g=[768], mla_w_q_b=[768, 1536], mla_kv_norm_g=[256], mla_w_kv_b=[256, 2048],
    mla_w_out=[1024, 1024], ffn_gu0=[1024, 5632], ffn_gu1=[1024, 5632], ffn_d0=[2816, 1024], ffn_d1=[2816, 1024],
    ln_g00=[1024], ln_g01=[1024], ln_g10=[1024], ln_g11=[1024], ln_b00=[1024], ln_b01=[1024], ln_b10=[1024], ln_b11=[1024])
del WEIGHT_SHAPES["rwkv_w2_"]


def host_weights(inp):
    w = {}
    f = lambda a: np.ascontiguousarray(np.asarray(a, dtype=np.float32))
    ew = np.asarray(inp["even_w_in"][0])
    o = 1552
    perm = np.concatenate([np.arange(0, 1536), o + np.arange(0, 1824), np.arange(1536, 1552)])
    w["even_w_in"] = f(ew[:, perm])
    for k in ("gla_gate_w2", "gla_gate_b", "gla_norm_g", "rwkv_mu", "rwkv_w0", "rwkv_w2", "rwkv_a0", "rwkv_a2",
              "rwkv_g2", "rwkv_k_k", "rwkv_k_a", "rwkv_ln_g", "rwkv_ln_b", "even_w_out", "mla_w_in",
              "mla_q_norm_g", "mla_kv_norm_g", "mla_w_kv_b", "mla_w_out"):
        w[k] = f(inp[k][0])
    w["rwkv_r_k"] = f(np.asarray(inp["rwkv_r_k"][0]).reshape(512))
    wq = np.asarray(inp["mla_w_q_b"][0]).reshape(768, 16, 96)
    w["mla_w_q_b"] = f(np.concatenate([wq[:, :, 0:64].reshape(768, 1024), wq[:, :, 64:96].reshape(768, 512)], 1))
    for i in range(2):
        gu = np.asarray(inp["ffn_w_gate_up"][i])
        w[f"ffn_gu{i}"] = f(gu.reshape(1024, 2, 22, 128).transpose(0, 2, 1, 3).reshape(1024, 5632))
        w[f"ffn_d{i}"] = f(inp["ffn_w_down"][i])
        for j in range(2):
            w[f"ln_g{i}{j}"] = f(inp["ln_g"][i, j])
            w[f"ln_b{i}{j}"] = f(inp["ln_b"][i, j])
    return w


def build(S, debug=False, stages=("ep", "ec", "eo", "f0", "mp", "at", "mo", "f1")):
    nc = bass.Bass("TRN2", target_bir_lowering=False)
    with ExitStack() as st:
        P = Prog(nc, st)
        P.init_mem()
        kind_dbg = "ExternalOutput" if debug else "Internal"
        x = P.dram("x", [S, 1024], F32, kind="ExternalInput")
        pos = P.dram("pos", [128, S // 128], I32, kind="ExternalInput")
        c = {k: P.dram(k, v, F32, kind="ExternalInput") for k, v in CONST_SHAPES.items()}
        w = {k: P.dram(k, v, F32, kind="ExternalInput") for k, v in WEIGHT_SHAPES.items()}
        c.update(w)
        prepB = P.dram("prepB", [S, WB], BF16)
        prepF = P.dram("prepF", [S, WF], F32)
        gam = P.dram("gam", [S // 128, 64, 24], F32)
        oT = P.dram("oT", [16, 64, S], BF16)
        oT2 = P.dram("oT2", [16, 64, S], BF16)
        qT = P.dram("qT", [16, 96, S], BF16)
        kT = P.dram("kT", [16, 96, S], BF16)
        vA = P.dram("vA", [S, 16 * 65], BF16)
        x1 = P.dram("x1", [S, 1024], F32, kind=kind_dbg if "eo" in stages else "ExternalInput")
        x2 = P.dram("x2", [S, 1024], F32, kind=kind_dbg if "f0" in stages else "ExternalInput")
        x3 = P.dram("x3", [S, 1024], F32, kind=kind_dbg if "mo" in stages else "ExternalInput")
        out = P.dram("out", [S, 1024], F32, kind="ExternalOutput")
        if "ep" in stages:
            stage_even_prep(P, S, x, c, prepB, prepF, gam)
        if "ec" in stages:
            stage_even_chunk(P, S, c, prepB, prepF, gam, oT)
        if "eo" in stages:
            stage_mla_out(P, S, oT, x, c["even_w_out"], c["ln_g00"], c["ln_b00"], x1)
        if "f0" in stages:
            stage_ffn(P, S, x1, c["ffn_gu0"], c["ffn_d0"], c["ln_g01"], c["ln_b01"], x2, c["ident"])
        if "mp" in stages:
            stage_mla_proj(P, S, x2, pos, c["invf"], c["mla_w_in"], c["mla_q_norm_g"], c["mla_w_q_b"],
                           c["mla_kv_norm_g"], c["mla_w_kv_b"], c["ident"], qT, kT, vA)
        if "at" in stages:
            stage_attn(P, S, qT, kT, vA, oT2, c["ones"])
        if "mo" in stages:
            stage_mla_out(P, S, oT2, x2, c["mla_w_out"], c["ln_g10"], c["ln_b10"], x3)
        if "f1" in stages:
            stage_ffn(P, S, x3, c["ffn_gu1"], c["ffn_d1"], c["ln_g11"], c["ln_b11"], out, c["ident"])
        P.barrier()
        P.emit()
        print({e: len(P.ins[e]) for e in ENGS}, flush=True)
    return nc


_NC_CACHE = {}


def kernel(**inp):
    x = np.asarray(inp["x"], dtype=np.float32)
    B, S, _ = x.shape
    pos = np.asarray(inp["positions"]).astype(np.int32)
    consts = host_consts()
    wts = host_weights(inp)
    if S not in _NC_CACHE:
        _NC_CACHE[S] = build(S)
    nc = _NC_CACHE[S]
    in_maps = []
    for b in range(B):
        m = dict(x=np.ascontiguousarray(x[b]),
                 pos=np.ascontiguousarray(pos[b].reshape(S // 128, 128).T))
        m.update(consts)
        m.update(wts)
        in_maps.append(m)
    res = run_bass_kernel_spmd(nc, in_maps, core_ids=list(range(B)))
    return np.stack([np.asarray(r["out"], dtype=np.float32) for r in res.results], 0)
```

```python
import math
import threading
from contextlib import ExitStack
import threading
import numpy as np
import concourse.bass as bass
import concourse.mybir as mybir
from concourse.bass_utils import run_bass_kernel_spmd

F32 = mybir.dt.float32
BF16 = mybir.dt.bfloat16
I32 = mybir.dt.int32
ALU = mybir.AluOpType
AF = mybir.ActivationFunctionType
AX = mybir.AxisListType

ENGS = ("pe", "act", "dve", "pool", "sp")


class Buf:
    __slots__ = ("name", "writes", "reads")

    def __init__(self, name=""):
        self.name = name
        self.writes = []
        self.reads = []


class V:
    __slots__ = ("ap", "buf")

    def __init__(self, ap, buf):
        self.ap = ap
        self.buf = buf

    def __getitem__(self, idx):
        return V(self.ap[idx], self.buf)

    def re(self, pattern, **kw):
        return V(self.ap.rearrange(pattern, **kw), self.buf)

    def sub(self, name=""):
        return V(self.ap, Buf(name))

    def bc(self, shape):
        return V(self.ap.to_broadcast(list(shape)), self.buf)

    def unsq(self, axis):
        return V(self.ap.unsqueeze(axis), self.buf)

    def pbc(self, n):
        return V(self.ap.partition_broadcast(n), self.buf)

    def bitcast(self, dt):
        return V(self.ap.bitcast(dt), self.buf)


class DmaPart:
    def __init__(self, sem, name):
        self.sem = sem
        self.total = 0
        self.name = name


class DmaSem:
    def __init__(self, prog, name):
        self.prog = prog
        self.name = name
        self.parts = {}

    def part(self, eng):
        k = "sw" if eng == "pool" else "hw"
        if k not in self.parts:
            p = self.prog
            sem = p.stack.enter_context(p.nc.semaphore(f"ds{len(p.dsems)}_{k}_{self.name}"))
            self.parts[k] = DmaPart(sem, self.name + k)
            p.dsems.append(self.parts[k])
        return self.parts[k]

    @property
    def total(self):
        return sum(x.total for x in self.parts.values())


class Prog:
    def __init__(self, nc, stack):
        self.nc = nc
        self.stack = stack
        self.ins = {e: [] for e in ENGS}
        self.waited = {e: {} for e in ENGS}
        self.esem = {e: stack.enter_context(nc.semaphore("es_" + e)) for e in ENGS}
        self.dsems = []
        self.ntile = 0
        self._il_step = None

    ARENA = 212800

    def init_mem(self):
        nc = self.nc
        self.arena = self.stack.enter_context(nc.sbuf_tensor("arena", [128, self.ARENA], mybir.dt.uint8))
        self.banks = [self.stack.enter_context(nc.psum_tensor(f"bank{i}", [128, 512], F32)) for i in range(8)]
        self.off = 0

    def stage_begin(self):
        self.barrier()
        self.off = 0

    def sb(self, shape, dtype, name=""):
        esz = {F32: 4, BF16: 2, I32: 4}[dtype]
        n = 1
        for d in shape[1:]:
            n *= d
        nb = (n * esz + 31) // 32 * 32
        assert self.off + nb <= self.ARENA, f"SBUF arena overflow: {self.off}+{nb} ({name})"
        ap = self.arena[0:shape[0], self.off:self.off + n * esz].bitcast(dtype)
        self.off += nb
        if len(shape) == 3:
            ap = ap.rearrange("p (a b) -> p a b", a=shape[1])
        elif len(shape) == 4:
            ap = ap.rearrange("p (a b c) -> p a b c", a=shape[1], b=shape[2])
        return V(ap, Buf(name))

    def bank(self, i, dtype=F32, name=""):
        ap = self.banks[i][:, :]
        if dtype != F32:
            ap = ap.bitcast(dtype)
        return V(ap, Buf(name or f"bank{i}"))

    def dram(self, name, shape, dtype, kind="Internal"):
        t = self.nc.dram_tensor(name, list(shape), dtype, kind=kind)
        return V(t.ap(), None)

    def dsem(self, name):
        return DmaSem(self, name)

    def _deps(self, eng, reads, writes):
        toks = []
        for v in reads:
            b = v.buf if isinstance(v, V) else v
            if b is None:
                continue
            toks.extend((t, True) for t in b.writes)
        for v in writes:
            b = v.buf if isinstance(v, V) else v
            if b is None:
                continue
            toks.extend((t, False) for t in b.writes)
            toks.extend((t, False) for t in b.reads)
        waits = []
        wd = self.waited[eng]
        for tk, raw in toks:
            if tk[0] == "e":
                _, f, idx = tk
                if f == eng and (eng == "pe" or not raw):
                    continue
                if wd.get(("e", f), -1) >= idx:
                    continue
                wd[("e", f)] = idx
                self.ins[f][idx]["signal"] = True
                waits.append(tk)
            else:
                _, ds = tk
                if wd.get(("d", id(ds)), -1) >= ds.total:
                    continue
                wd[("d", id(ds))] = ds.total
                waits.append(("d", ds, ds.total))
        return waits

    def _commit(self, tok, reads, writes):
        for v in reads:
            b = v.buf if isinstance(v, V) else v
            if b is None:
                continue
            if tok[0] == "e":
                b.reads = [t for t in b.reads if not (t[0] == "e" and t[1] == tok[1])]
            elif tok in b.reads:
                continue
            b.reads.append(tok)
        for v in writes:
            b = v.buf if isinstance(v, V) else v
            if b is None:
                continue
            b.writes = [tok]
            b.reads = []

    def op(self, eng, fn, reads=(), writes=()):
        waits = self._deps(eng, reads, writes)
        idx = len(self.ins[eng])
        self.ins[eng].append(dict(fn=fn, waits=waits, signal=False, dsem=None))
        self._commit(("e", eng, idx), reads, writes)
        if self._il_step is not None:
            self._il_step()
        return idx

    def dma(self, eng, out, in_, ds, serial=False, **kw):
        ds = ds.part(eng)
        waits = self._deps(eng, [in_], [out])
        if serial and ds.total and self.waited[eng].get(("d", id(ds)), -1) < ds.total:
            self.waited[eng][("d", id(ds))] = ds.total
            waits.append(("d", ds, ds.total))
        idx = len(self.ins[eng])
        ds.total += 16
        oa, ia = out.ap, in_.ap
        self.ins[eng].append(dict(fn=lambda e: e.dma_start(out=oa, in_=ia, **kw), waits=waits,
                                  signal=False, dsem=ds))
        self._commit(("d", ds), [in_], [out])
        if self._il_step is not None:
            self._il_step()

    def interleave(self, fns):
        n = len(fns)
        import os
        if n == 1 or os.environ.get('NO_IL'):
            for f in fns:
                f()
            return
        cv = threading.Condition()
        st = {"turn": 0, "alive": [True] * n, "err": None}
        loc = threading.local()

        def nxt(i):
            for d in range(1, n + 1):
                j = (i + d) % n
                if st["alive"][j]:
                    return j
            return None

        def step():
            i = loc.idx
            with cv:
                j = nxt(i)
                if j is None or j == i:
                    return
                st["turn"] = j
                cv.notify_all()
                while st["turn"] != i:
                    cv.wait()

        def worker(i):
            loc.idx = i
            with cv:
                while st["turn"] != i:
                    cv.wait()
            try:
                fns[i]()
            except BaseException as e:
                st["err"] = e
            with cv:
                st["alive"][i] = False
                st["turn"] = nxt(i)
                cv.notify_all()

        self._il_step = step
        ths = [threading.Thread(target=worker, args=(i,)) for i in range(n)]
        for t in ths:
            t.start()
        for t in ths:
            t.join()
        self._il_step = None
        if st["err"] is not None:
            raise st["err"]

    def wdma(self, out, in_, eng="pool"):
        if not hasattr(self, "wsems"):
            self.wsems = [self.dsem(f"w{i}") for i in range(6)]
            self.wcnt = 0
        ds = self.wsems[self.wcnt % len(self.wsems)]
        self.wcnt += 1
        self.dma(eng, out, in_, ds, serial=True)

    def barrier(self):
        for e in ENGS:
            waits = []
            wd = self.waited[e]
            for f in ENGS:
                if f == e or not self.ins[f]:
                    continue
                idx = None
                for k in range(len(self.ins[f]) - 1, -1, -1):
                    if self.ins[f][k]["dsem"] is None and self.ins[f][k]["fn"] is not None:
                        idx = k
                        break
                if idx is None or wd.get(("e", f), -1) >= idx:
                    continue
                wd[("e", f)] = idx
                self.ins[f][idx]["signal"] = True
                waits.append(("e", f, idx))
            for ds in self.dsems:
                if ds.total and wd.get(("d", id(ds)), -1) < ds.total:
                    wd[("d", id(ds))] = ds.total
                    waits.append(("d", ds, ds.total))
            if waits:
                self.ins[e].append(dict(fn=None, waits=waits, signal=False, dsem=None))

    def emit(self):
        nc = self.nc
        rank = {}
        for e in ENGS:
            c = 0
            r = []
            for rec in self.ins[e]:
                if rec["signal"]:
                    c += 1
                r.append(c)
            rank[e] = r
        print("signals", {e: (rank[e][-1] if rank[e] else 0) for e in ENGS}, "dma_max", max([d.total for d in self.dsems] + [0]), flush=True)
        handles = {"pe": "tensor", "act": "scalar", "dve": "vector", "pool": "gpsimd", "sp": "sync"}

        def run(e, eng):
            for rec in self.ins[e]:
                for w in rec["waits"]:
                    if w[0] == "e":
                        eng.wait_ge(self.esem[w[1]], rank[w[1]][w[2]])
                    else:
                        eng.wait_ge(w[1].sem, w[2])
                if rec["fn"] is None:
                    continue
                ins = rec["fn"](eng)
                if rec["dsem"] is not None:
                    ins.then_inc(rec["dsem"].sem, 16)
                elif rec["signal"]:
                    ins.then_inc(self.esem[e], 1)

        with nc.Block() as block:
            for e in ENGS:
                if not self.ins[e]:
                    continue
                getattr(block, handles[e])(lambda eng, e=e: run(e, eng))

    def mm(self, out, lhsT, rhs, start=True, stop=True):
        o, l, r = out.ap, lhsT.ap, rhs.ap
        return self.op("pe", lambda e: e.matmul(o, l, r, start=start, stop=stop), [lhsT, rhs], [out])

    def tr(self, out, in_, ident):
        o, i, d = out.ap, in_.ap, ident.ap
        return self.op("pe", lambda e: e.transpose(o, i, d), [in_, ident], [out])

    def act(self, out, in_, func, bias=None, scale=None, accum=None, eng="act"):
        o, i = out.ap, in_.ap
        kw = {}
        rd = [in_]
        wr = [out]
        if bias is not None:
            if isinstance(bias, V):
                kw["bias"] = bias.ap
                rd.append(bias)
            else:
                kw["bias"] = bias
        if scale is not None:
            if isinstance(scale, V):
                kw["scale"] = scale.ap
                rd.append(scale)
            else:
                kw["scale"] = scale
        if accum is not None:
            kw["accum_out"] = accum.ap
            wr.append(accum)
        return self.op("act", lambda e: e.activation(o, i, func, **kw), rd, wr)

    def tt(self, out, a, b, op, eng="dve"):
        o, x, y = out.ap, a.ap, b.ap
        return self.op(eng, lambda e: e.tensor_tensor(o, x, y, op), [a, b], [out])

    def ts(self, out, a, s1, op0, s2=None, op1=None, eng="dve", accum=None):
        o, x = out.ap, a.ap
        rd = [a]
        wr = [out]
        a1 = s1
        if isinstance(s1, V):
            rd.append(s1)
            a1 = s1.ap
        a2 = s2
        if isinstance(s2, V):
            rd.append(s2)
            a2 = s2.ap
        kw = {}
        if op1 is not None:
            kw["op1"] = op1
        if accum is not None:
            kw["accum_out"] = accum.ap
            wr.append(accum)
        return self.op(eng, lambda e: e.tensor_scalar(o, x, a1, a2, op0, **kw), rd, wr)

    def stt(self, out, a, s, b, op0, op1, eng="dve"):
        o, x, y = out.ap, a.ap, b.ap
        rd = [a, b]
        sc = s
        if isinstance(s, V):
            rd.append(s)
            sc = s.ap
        return self.op(eng, lambda e: e.scalar_tensor_tensor(o, x, sc, y, op0, op1), rd, [out])

    def copy(self, out, in_, eng="dve"):
        o, i = out.ap, in_.ap
        if eng == "act":
            return self.op("act", lambda e: e.copy(o, i), [in_], [out])
        return self.op(eng, lambda e: e.tensor_copy(o, i), [in_], [out])

    def memset(self, out, val, eng="dve"):
        o = out.ap
        return self.op(eng, lambda e: e.memset(o, val), [], [out])

    def reduce(self, out, in_, op=None, axis=None, eng="dve"):
        o, i = out.ap, in_.ap
        op = op or ALU.add
        axis = axis or AX.X
        return self.op(eng, lambda e: e.tensor_reduce(o, i, axis, op), [in_], [out])


DN_ALPHA = 4.0 ** 0.25
LN_EPS = 1e-5
D = 1024
FH = 2816
NF = 22


def ln_epilogue(P, z, xs, ods, g_bc, b_bc, ys, tmp):
    for n in range(2):
        sl = slice(n * 512, (n + 1) * 512)
        P.stt(z[:, sl], xs[:, sl], DN_ALPHA, ods[n], ALU.mult, ALU.add)
    ln_core(P, z, g_bc, b_bc, ys, tmp)


def ln_core(P, z, g_bc, b_bc, ys, tmp):
    stats, mv, rstd = tmp["stats"], tmp["mv"], tmp["rstd"]
    for n in range(2):
        sl = slice(n * 512, (n + 1) * 512)
        so, zi = stats[:, n, :].ap, z[:, sl].ap
        P.op("dve", lambda e, so=so, zi=zi: e.bn_stats(so, zi), [z], [stats])
    mo, si = mv.ap, stats.re("p a b -> p (a b)").ap
    P.op("dve", lambda e: e.bn_aggr(mo, si), [stats], [mv])
    P.ts(rstd, mv[:, 1:2], LN_EPS, ALU.add)
    P.act(rstd, rstd, AF.Sqrt)
    ro = rstd.ap
    P.op("dve", lambda e: e.reciprocal(ro, ro), [rstd], [rstd])
    P.ts(z, z, mv[:, 0:1], ALU.subtract, rstd[:, 0:1], ALU.mult)
    P.tt(z, z, g_bc, ALU.mult, eng="pool")
    P.tt(ys, z, b_bc, ALU.add, eng="pool")


def stage_ffn(P, S, xin, wgu_d, wd_d, lng_d, lnb_d, yout, ident_d):
    P.stage_begin()
    TT = 256
    NT = S // TT
    ident = P.sb([128, 128], BF16, "ident")
    wgu = P.sb([128, 8, NF * 256], BF16, "wgu")
    wd = P.sb([128, NF, D], BF16, "wd")
    g_bc = P.sb([128, D], F32, "g_bc")
    b_bc = P.sb([128, D], F32, "b_bc")
    ds_c = P.dsem("ffn_c")
    P.dma("pool", ident, ident_d, ds_c)
    P.dma("sp", g_bc, lng_d.pbc(128), ds_c)
    P.dma("sp", b_bc, lnb_d.pbc(128), ds_c)
    wgu_chunks = [wgu[:, :, f * 512:(f + 1) * 512].sub(f"wgu{f}") for f in range(NF // 2)]
    wd_chunks = [wd[:, f * 2:(f + 1) * 2, :].sub(f"wd{f}") for f in range(NF // 2)]
    wgu_src = wgu_d.re("(k p) n -> p k n", p=128)
    wd_src = wd_d.re("(f p) n -> p f n", p=128)

    xs = [[P.sb([128, D], F32, f"xs{s}{u}") for u in range(2)] for s in range(2)]
    xb = [[P.sb([128, D], BF16, f"xb{s}{u}") for u in range(2)] for s in range(2)]
    ds_x = [P.dsem(f"ffn_x{s}") for s in range(2)]
    xT = [P.sb([128, 8, TT], BF16, f"xT{s}") for s in range(2)]
    hT = P.sb([128, NF, TT], BF16, "hT")
    sg = [P.sb([128, TT], F32, f"sg{s}") for s in range(2)]
    z = [P.sb([128, D], F32, f"z{s}") for s in range(2)]
    ys = [P.sb([128, D], F32, f"ys{s}") for s in range(2)]
    ds_y = [P.dsem(f"ffn_y{s}") for s in range(2)]
    tmp = [dict(stats=P.sb([128, 2, 6], F32), mv=P.sb([128, 2], F32), rstd=P.sb([128, 1], F32)) for _ in range(2)]
    tp = P.bank(0, BF16, "tp")
    gu = [P.bank(1 + i, F32, f"gu{i}").re("p (a b) -> p a b", a=2) for i in range(2)]
    od = [[P.bank(3 + 2 * u + n, F32, f"od{u}{n}") for n in range(2)] for u in range(2)]

    def load(t):
        s = t % 2
        for u in range(2):
            rows = slice(t * TT + u * 128, t * TT + (u + 1) * 128)
            P.dma("sp", xs[s][u], xin[rows, :], ds_x[s])
            P.dma("pool", xb[s][u], xin[rows, :], ds_x[s])

    wl = {"gu": 0, "d": 0}

    def compute(t):
        s = t % 2
        for u in range(2):
            for k in range(8):
                P.tr(tp[:, k * 128:(k + 1) * 128], xb[s][u][:, k * 128:(k + 1) * 128], ident)
            P.copy(xT[s][:, :, u * 128:(u + 1) * 128], tp.re("p (k t) -> p k t", k=8), eng="act")
        for f in range(NF):
            if t == 0 and f % 2 == 0:
                c = f // 2
                P.wdma(wgu_chunks[c], wgu_src[:, :, c * 512:(c + 1) * 512])
            wch = wgu_chunks[f // 2]
            g = gu[f % 2]
            for half in range(2):
                for k in range(8):
                    c0 = (f % 2) * 256 + half * 128
                    P.mm(g[:, half, :], wch[:, k, c0:c0 + 128], xT[s][:, k, :], start=(k == 0), stop=(k == 7))
            sgt = sg[f % 2]
            P.act(sgt, g[:, 0, :], AF.Silu)
            P.tt(hT[:, f, :], sgt, g[:, 1, :], ALU.mult)
        for u in range(2):
            for n in range(2):
                for f in range(NF):
                    if t == 0 and u == 0 and n == 0 and f % 2 == 0:
                        c = f // 2
                        P.wdma(wd_chunks[c], wd_src[:, c * 2:(c + 1) * 2, :])
                    P.mm(od[u][n], hT[:, f, u * 128:(u + 1) * 128], wd_chunks[f // 2][:, f % 2, n * 512:(n + 1) * 512],
                         start=(f == 0), stop=(f == NF - 1))
        for u in range(2):
            ln_epilogue(P, z[u], xs[s][u], od[u], g_bc, b_bc, ys[u], tmp[u])
            rows = slice(t * TT + u * 128, t * TT + (u + 1) * 128)
            P.dma("sp", yout[rows, :], ys[u], ds_y[u])

    load(0)
    for t in range(NT):
        if t + 1 < NT:
            load(t + 1)
        compute(t)


RMS_EPS = 1e-6
NH = 16
QR = 768
KVR = 256
MAGIC = 12582912.0
TWO_PI = 2.0 * math.pi


def rms_stats(P, srcs, n, eps, stats, mv, rstd):
    k = len(srcs)
    for i, v in enumerate(srcs):
        so, zi = stats[:, i, :].ap, v.ap
        P.op("dve", lambda e, so=so, zi=zi: e.bn_stats(so, zi), [v], [stats])
    mo, si = mv.ap, stats[:, 0:k, :].re("p a b -> p (a b)").ap
    P.op("dve", lambda e: e.bn_aggr(mo, si), [stats], [mv])
    P.stt(rstd, mv[:, 0:1], mv[:, 0:1], mv[:, 1:2], ALU.mult, ALU.add)
    P.ts(rstd, rstd, eps, ALU.add)
    P.act(rstd, rstd, AF.Sqrt)
    ro = rstd.ap
    P.op("dve", lambda e: e.reciprocal(ro, ro), [rstd], [rstd])


def rope(P, dst1, dst2, x1, x2, cos, sin, t):
    P.tt(t[0], x1, cos, ALU.mult)
    P.tt(t[1], x2, sin, ALU.mult)
    P.tt(dst1, t[0], t[1], ALU.subtract)
    P.tt(t[2], x1, sin, ALU.mult)
    P.tt(t[3], x2, cos, ALU.mult)
    P.tt(dst2, t[2], t[3], ALU.add)


def stage_mla_proj(P, S, xin, pos_d, invf_d, win_d, qng_d, wqb_d, kvng_d, wkvb_d, ident_d, qT_d, kT_d, vA_d):
    P.stage_begin()
    NT = S // 128
    ident = P.sb([128, 128], BF16, "ident")
    win = P.sb([128, 8, 1056], BF16, "win")
    wqb = P.sb([128, 6, 1536], BF16, "wqb")
    wkvb = P.sb([128, 2, 2048], BF16, "wkvb")
    qng = P.sb([128, QR], F32, "qng")
    kvng = P.sb([128, KVR], F32, "kvng")
    invf = P.sb([128, 16], F32, "invf")
    posi = P.sb([128, NT], I32, "posi")
    posf = P.sb([128, NT], F32, "posf")
    ang = P.sb([128, NT, 16], F32, "ang")
    tn = P.sb([128, NT, 16], F32, "tn")
    cosT = P.sb([128, NT, 16], F32, "cos")
    sinT = P.sb([128, NT, 16], F32, "sin")
    ds_c = P.dsem("mp_c")
    P.dma("pool", ident, ident_d, ds_c)
    P.dma("sp", qng, qng_d.pbc(128), ds_c)
    P.dma("sp", kvng, kvng_d.pbc(128), ds_c)
    P.dma("sp", invf, invf_d.pbc(128), ds_c)
    P.dma("sp", posi, pos_d, ds_c)
    win_src = win_d.re("(k p) n -> p k n", p=128)
    for k in range(0, 8, 2):
        P.wdma(win[:, k:k + 2, :].sub(f"win{k}"), win_src[:, k:k + 2, :])
    wqb_src = wqb_d.re("(k p) n -> p k n", p=128)
    for k in range(0, 6, 2):
        P.wdma(wqb[:, k:k + 2, :].sub(f"wqb{k}"), wqb_src[:, k:k + 2, :])
    P.wdma(wkvb.sub("wkvb"), wkvb_d.re("(k p) n -> p k n", p=128))
    P.barrier()
    P.copy(posf, posi)
    P.tt(ang, posf.unsq(2).bc([128, NT, 16]), invf.unsq(1).bc([128, NT, 16]), ALU.mult)
    for (dst, shift) in ((sinT, 0.0), (cosT, math.pi / 2)):
        a2 = ang
        if shift:
            P.ts(dst, ang, shift, ALU.add)
            a2 = dst
        P.ts(tn, a2, 1.0 / TWO_PI, ALU.mult, MAGIC, ALU.add)
        P.ts(tn, tn, MAGIC, ALU.subtract)
        P.stt(dst, tn, -TWO_PI, a2, ALU.mult, ALU.add)
        P.ts(dst, dst, 3.14159, ALU.min, -3.14159, ALU.max)
        P.act(dst, dst, AF.Sin)

    xb = [P.sb([128, D], BF16, f"xb{s}") for s in range(2)]
    ds_x = [P.dsem(f"mp_x{s}") for s in range(2)]
    xT = P.sb([128, 8, 128], BF16, "xT")
    cqn = P.sb([128, QR], BF16, "cqn")
    ckvn = P.sb([128, KVR], BF16, "ckvn")
    cqT = P.sb([128, 6, 128], BF16, "cqT")
    ckvT = P.sb([128, 2, 128], BF16, "ckvT")
    Qf = P.sb([128, NH, 96], BF16, "Qf")
    Kf = P.sb([128, NH, 96], BF16, "Kf")
    Va = [P.sb([128, NH, 65], BF16, f"Va{s}") for s in range(2)]
    ds_v = [P.dsem(f"mp_v{s}") for s in range(2)]
    kr = P.sb([128, 32], BF16, "kr")
    rt = [P.sb([128, NH, 16], F32, f"rt{i}") for i in range(4)]
    QTs = [P.sb([96, NH, 512], BF16, f"QTs{s}") for s in range(2)]
    KTs = [P.sb([96, NH, 512], BF16, f"KTs{s}") for s in range(2)]
    ds_q = [P.dsem(f"mp_q{s}") for s in range(2)]
    stats = P.sb([128, 2, 6], F32, "stats")
    mv = P.sb([128, 2], F32, "mv")
    rstd = P.sb([128, 1], F32, "rstd")
    for s in range(2):
        P.memset(Va[s][:, :, 64:65], 1.0)
    tp = P.bank(0, BF16, "tp")
    bk = [None] + [P.bank(i, F32, f"b{i}") for i in range(1, 8)]
    qT_v = qT_d.re("h d s -> d h s")
    kT_v = kT_d.re("h d s -> d h s")

    def load(t):
        P.dma("pool", xb[t % 2], xin[t * 128:(t + 1) * 128, :], ds_x[t % 2])

    def compute(t):
        s = t % 2
        for k in range(8):
            P.tr(tp[:, k * 128:(k + 1) * 128], xb[s][:, k * 128:(k + 1) * 128], ident)
        P.copy(xT, tp.re("p (k t) -> p k t", k=8), eng="act")
        for n, (c0, c1) in enumerate(((0, 512), (512, 1024), (1024, 1056))):
            for k in range(8):
                P.mm(bk[1 + n][:, 0:c1 - c0], xT[:, k, :], win[:, k, c0:c1], start=(k == 0), stop=(k == 7))
        rms_stats(P, [bk[1], bk[2][:, 0:256]], QR, RMS_EPS, stats, mv, rstd)
        P.stt(cqn[:, 0:512], bk[1], rstd[:, 0:1], qng[:, 0:512], ALU.mult, ALU.mult)
        P.stt(cqn[:, 512:768], bk[2][:, 0:256], rstd[:, 0:1], qng[:, 512:768], ALU.mult, ALU.mult)
        rms_stats(P, [bk[2][:, 256:512]], KVR, RMS_EPS, stats, mv, rstd)
        P.stt(ckvn, bk[2][:, 256:512], rstd[:, 0:1], kvng, ALU.mult, ALU.mult)
        cs = cosT[:, t, :]
        sn = sinT[:, t, :]
        kp = bk[3][:, 0:32].re("p (i two) -> p i two", two=2)
        krv = kr.re("p (i two) -> p i two", two=2)
        rope(P, krv[:, :, 0], krv[:, :, 1], kp[:, :, 0], kp[:, :, 1], cs, sn, [r[:, 0, :] for r in rt])
        for k in range(6):
            P.tr(tp[:, k * 128:(k + 1) * 128], cqn[:, k * 128:(k + 1) * 128], ident)
        P.copy(cqT, tp[:, 0:768].re("p (k t) -> p k t", k=6), eng="act")
        for k in range(2):
            P.tr(tp[:, k * 128:(k + 1) * 128], ckvn[:, k * 128:(k + 1) * 128], ident)
        P.copy(ckvT, tp[:, 0:256].re("p (k t) -> p k t", k=2), eng="act")
        for n in range(3):
            for k in range(6):
                P.mm(bk[4 + n], cqT[:, k, :], wqb[:, k, n * 512:(n + 1) * 512], start=(k == 0), stop=(k == 5))
        kvb = [bk[1], bk[2], bk[3], bk[7]]
        for n in range(4):
            for k in range(2):
                P.mm(kvb[n], ckvT[:, k, :], wkvb[:, k, n * 512:(n + 1) * 512], start=(k == 0), stop=(k == 1))
        for n in range(2):
            P.copy(Qf[:, n * 8:(n + 1) * 8, 0:64], bk[4 + n].re("p (h d) -> p h d", h=8), eng="act")
        qp = bk[6].re("p (h i two) -> p h i two", h=NH, two=2)
        qd = Qf[:, :, 64:96].re("p h (i two) -> p h i two", two=2)
        csb = cs.unsq(1).bc([128, NH, 16])
        snb = sn.unsq(1).bc([128, NH, 16])
        rope(P, qd[:, :, :, 0], qd[:, :, :, 1], qp[:, :, :, 0], qp[:, :, :, 1], csb, snb, rt)
        for n in range(4):
            kv4 = kvb[n].re("p (h d) -> p h d", h=4)
            P.copy(Kf[:, n * 4:(n + 1) * 4, 0:64], kv4[:, :, 0:64], eng="act")
            P.copy(Va[s][:, n * 4:(n + 1) * 4, 0:64], kv4[:, :, 64:128])
        P.copy(Kf[:, :, 64:96], kr.unsq(1).bc([128, NH, 32]))
        w = (t % 4) * 128
        slot = (t // 4) % 2
        for (src, dst) in ((Qf, QTs[slot]), (Kf, KTs[slot])):
            for half in range(2):
                for hh in range(8):
                    h = half * 8 + hh
                    P.tr(tp[0:96, hh * 128:(hh + 1) * 128], src[:, h, :], ident)
                P.copy(dst[:, half * 8:(half + 1) * 8, w:w + 128], tp[0:96, :].re("p (h t) -> p h t", h=8), eng="act")
        P.dma("sp", vA_d[t * 128:(t + 1) * 128, :], Va[s].re("p h d -> p (h d)"), ds_v[s])
        if t % 4 == 3:
            win0 = (t // 4) * 512
            P.dma("sp", qT_v[:, :, win0:win0 + 512], QTs[slot], ds_q[slot])
            P.dma("sp", kT_v[:, :, win0:win0 + 512], KTs[slot], ds_q[slot])

    load(0)
    for t in range(NT):
        if t + 1 < NT:
            load(t + 1)
        compute(t)


def stage_attn(P, S, qT_d, kT_d, vA_d, oT_d, ones_d):
    P.stage_begin()
    NJ = S // 128
    NG = S // 512
    scale = 96.0 ** -0.5
    Vall = P.sb([128, NJ, NH * 65], BF16, "Vall")
    ones = P.sb([128, 64], F32, "ones")
    ds_c = P.dsem("at_c")
    P.dma("sp", ones, ones_d, ds_c)
    vsrc = vA_d.re("(j p) c -> p j c", p=128)
    step = max(1, NJ // 4)
    for j0 in range(0, NJ, step):
        P.dma("sp", Vall[:, j0:j0 + step, :], vsrc[:, j0:j0 + step, :], ds_c)
    KT = [P.sb([96, S], BF16, f"KT{s}") for s in range(2)]
    QT = [P.sb([96, S], BF16, f"QT{s}") for s in range(2)]
    ds_h = [P.dsem(f"at_h{s}") for s in range(2)]
    OT = [P.sb([64, S], BF16, f"OT{s}") for s in range(2)]
    ds_o = [P.dsem(f"at_o{s}") for s in range(2)]
    PT = [P.sb([128, 512], BF16, f"PT{i}") for i in range(4)]
    osb = [P.sb([65, 512], F32, f"osb{i}") for i in range(2)]
    sc = [P.bank(i, F32, f"sc{i}") for i in range(4)]
    oacc = [P.bank(4 + i, F32, f"oacc{i}") for i in range(2)]
    bcp = P.bank(6, F32, "bcp")

    def load(h):
        P.dma("sp", KT[h % 2], kT_d[h], ds_h[h % 2])
        P.dma("sp", QT[h % 2], qT_d[h], ds_h[h % 2])

    items = [(h, g, j) for h in range(NH) for g in range(NG) for j in range(4 * g + 4)]
    LOOK = 2
    pending = []

    def qk(idx):
        h, g, j = items[idx]
        if g == 0 and j == 0 and h + 1 < NH:
            load(h + 1)
        m = j - 4 * g
        c0 = max(0, m) * 128
        n = 512 - c0
        P.mm(sc[idx % 4][:, 0:n], KT[h % 2][:, j * 128:(j + 1) * 128], QT[h % 2][:, g * 512 + c0:(g + 1) * 512])

    def pv(idx):
        h, g, j = items[idx]
        nj = 4 * g + 4
        m = j - 4 * g
        c0 = max(0, m) * 128
        n = 512 - c0
        scb, pt, oa = sc[idx % 4], PT[idx % 4], oacc[g % 2]
        P.act(pt[:, 0:n], scb[:, 0:n], AF.Exp, scale=scale)
        if m >= 0:
            P.memset(pt[64:128, 0:64], 0.0, eng="pool")
        P.mm(oa[0:65, c0:512], Vall[:, j, h * 65:(h + 1) * 65], pt[:, 0:n], start=(j == 0), stop=(j == nj - 1))
        if j == nj - 1:
            ob = osb[g % 2]
            ot = OT[h % 2]
            P.copy(ob, oa[0:65, :], eng="act")
            ro = ob[64:65, :].ap
            P.op("dve", lambda e, ro=ro: e.reciprocal(ro, ro), [ob], [ob])

            def fin(h=h, g=g, ob=ob, ot=ot):
                P.mm(bcp[0:64, :], ones[64:65, 0:64], ob[64:65, :])
                P.tt(ot[:, g * 512:(g + 1) * 512], ob[0:64, :], bcp[0:64, :], ALU.mult)
                if g == NG - 1:
                    P.dma("sp", oT_d[h], ot, ds_o[h % 2])
            pending.append([idx + 3, fin])

    load(0)
    n_it = len(items)
    for idx in range(n_it + LOOK):
        if idx < n_it:
            qk(idx)
        if idx - LOOK >= 0:
            pv(idx - LOOK)
        while pending and pending[0][0] <= idx - LOOK:
            pending.pop(0)[1]()
    while pending:
        pending.pop(0)[1]()


def stage_mla_out(P, S, oT_d, xin, wout_d, lng_d, lnb_d, yout):
    P.stage_begin()
    NW = S // 512
    wout = P.sb([64, NH, D], BF16, "wout")
    g_bc = P.sb([128, D], F32, "g_bc")
    b_bc = P.sb([128, D], F32, "b_bc")
    ds_c = P.dsem("mo_c")
    P.dma("sp", g_bc, lng_d.pbc(128), ds_c)
    P.dma("sp", b_bc, lnb_d.pbc(128), ds_c)
    P.wdma(wout, wout_d.re("(h d) n -> d h n", d=64))
    OTw = [P.sb([64, NH, 512], BF16, f"OTw{s}") for s in range(2)]
    ds_w = [P.dsem(f"mo_w{s}") for s in range(2)]
    xs = [P.sb([128, D], F32, f"xs{s}") for s in range(2)]
    ds_x = [P.dsem(f"mo_x{s}") for s in range(2)]
    z = [P.sb([128, D], F32, f"z{s}") for s in range(2)]
    ys = [P.sb([128, D], F32, f"ys{s}") for s in range(2)]
    ds_y = [P.dsem(f"mo_y{s}") for s in range(2)]
    tmp = [dict(stats=P.sb([128, 2, 6], F32), mv=P.sb([128, 2], F32), rstd=P.sb([128, 1], F32)) for _ in range(2)]
    od = [[P.bank(2 * u + n, F32, f"od{u}{n}") for n in range(2)] for u in range(2)]
    oT_v = oT_d.re("h d s -> d h s")

    def loadw(w):
        P.dma("sp", OTw[w % 2], oT_v[:, :, w * 512:(w + 1) * 512], ds_w[w % 2])

    def loadx(t):
        P.dma("sp", xs[t % 2], xin[t * 128:(t + 1) * 128, :], ds_x[t % 2])

    loadw(0)
    loadx(0)
    for w in range(NW):
        if w + 1 < NW:
            loadw(w + 1)
        for u4 in range(4):
            t = w * 4 + u4
            if t + 1 < S // 128:
                loadx(t + 1)
            u = t % 2
            for n in range(2):
                for h in range(NH):
                    P.mm(od[u][n], OTw[w % 2][:, h, u4 * 128:(u4 + 1) * 128], wout[:, h, n * 512:(n + 1) * 512],
                         start=(h == 0), stop=(h == NH - 1))
            ln_epilogue(P, z[u], xs[t % 2], od[u], g_bc, b_bc, ys[u], tmp[u])
            P.dma("sp", yout[t * 128:(t + 1) * 128, :], ys[u], ds_y[u])


EC_STOP = 99
EC_VAR = ''

EC = math.exp(-0.5)
CB_R, CB_A, CB_B, CB_K, CB_BH, CB_KH, CB_V = [i * 512 for i in range(7)]
CB_GQ, CB_GK, CB_GKH, CB_GV = 3584, 3840, 4096, 4352
WB = 4864
CF_BONUS, CF_G, CF_GS = 0, 512, 1024
WF = 1536


def rsqrt_lnexp(P, out, in_, add=None, mx=None, scale_in=None):
    if scale_in is not None:
        P.ts(out, in_, scale_in, ALU.mult, add if add is not None else 0.0, ALU.add)
    elif mx is not None:
        P.ts(out, in_, mx, ALU.max)
    else:
        P.ts(out, in_, add, ALU.add)
    P.act(out, out, AF.Ln)
    P.act(out, out, AF.Exp, scale=-0.5)


def sigmoid_(P, out, in_, tmp=None):
    P.act(out, in_, AF.Tanh, scale=0.5)
    P.ts(out, out, 0.5, ALU.mult, 0.5, ALU.add)


def stage_even_prep(P, S, xin, c, prepB_d, prepF_d, gam_d):
    P.stage_begin()
    NT = S // 128
    ds_c = P.dsem("ep_c")
    ident = P.sb([128, 128], BF16, "ident")
    P.dma("pool", ident, c["ident"], ds_c)
    win = P.sb([128, 8, 3376], BF16, "win")
    src = c["even_w_in"].re("(k p) n -> p k n", p=128)
    for k in range(8):
        P.wdma(win[:, k:k + 1, :].sub(f"win{k}"), src[:, k:k + 1, :])
    Wwa = P.sb([128, 512], BF16, "Wwa")
    P.wdma(Wwa[0:64, :].sub("w2"), c["rwkv_w2"])
    P.wdma(Wwa[64:128, :].sub("a2"), c["rwkv_a2"])
    G2 = P.sb([128, 2, 512], BF16, "G2")
    P.wdma(G2[:, 0, :].sub("g2a"), c["rwkv_g2"][0:128, :])
    P.wdma(G2[0:32, 1, :].sub("g2b"), c["rwkv_g2"][128:160, :])
    GW = P.sb([48, 256], BF16, "GW")
    P.wdma(GW[32:48, :].sub("gw"), c["gla_gate_w2"])

    def bc(name, n):
        t = P.sb([128, n], F32, name)
        P.dma("sp", t, c[name].pbc(128), ds_c)
        return t
    mu = bc("rwkv_mu", 1824)
    w0 = bc("rwkv_w0", 512)
    a0 = bc("rwkv_a0", 512)
    k_k = bc("rwkv_k_k", 512)
    k_a = bc("rwkv_k_a", 512)
    r_k = bc("rwkv_r_k", 512)
    gate_b = bc("gla_gate_b", 256)

    def cst(name, shape):
        t = P.sb(shape, F32, name)
        P.dma("sp", t, c[name], ds_c)
        return t
    TRI_i = cst("tri_incl", [128, 128])
    TRI_x = cst("tri_excl", [128, 128])
    TRI_r = cst("tri_rev", [128, 128])
    TRIg_i = cst("trig_incl", [128, 128])
    TRIg_r = cst("trig_rev", [128, 128])
    CBk = cst("cb", [128, 2])
    CBg = cst("cbg", [128, 2])
    P.barrier()

    xb = [P.sb([128, 1024], BF16, f"xb{s}") for s in range(2)]
    ds_x = [P.dsem(f"ep_x{s}") for s in range(2)]
    xT = P.sb([128, 8, 128], BF16, "xT")
    Psb = P.sb([128, 3376], F32, "Psb")
    Sh = [P.sb([128, 1824], F32, f"Sh{s}") for s in range(2)]
    ds_sh = [P.dsem(f"ep_sh{s}") for s in range(2)]
    Lin = P.sb([128, 304], BF16, "Lin")
    LT0 = P.sb([128, 128], BF16, "LT0")
    LT1 = P.sb([128, 128], BF16, "LT1")
    LT2 = P.sb([48, 128], BF16, "LT2")
    sigw = P.sb([128, 512], F32, "sigw")
    av = P.sb([128, 512], F32, "a")
    kkn = P.sb([128, 512], F32, "kkn")
    sq = P.sb([128, 512], F32, "sq")
    k2 = P.sb([128, 512], F32, "k2")
    bvec = P.sb([128, 512], F32, "bvec")
    E = P.sb([128, 512], F32, "E")
    ss = P.sb([128, 8], F32, "ss")
    rks = P.sb([128, 8], F32, "rks")
    sp = P.sb([128, 256], F32, "sp")
    Eg = P.sb([128, 256], F32, "Eg")
    OB = [P.sb([128, WB], BF16, f"OB{s}") for s in range(2)]
    OF = [P.sb([128, WF], F32, f"OF{s}") for s in range(2)]
    GM = [P.sb([64, 24], F32, f"GM{s}") for s in range(2)]
    ds_o = [P.dsem(f"ep_o{s}") for s in range(2)]
    ShA = [Sh[i][:, 0:1536].sub(f"ShA{i}") for i in range(2)]
    ShB = [Sh[i][:, 1536:1824].sub(f"ShB{i}") for i in range(2)]
    ds_shb = [P.dsem(f"ep_shb{s}") for s in range(2)]
    xTs = P.sb([128, 8, 128], BF16, "xTs")
    lastc = P.sb([128, 8, 1], BF16, "lastc")
    P.memset(lastc, 0.0)
    tp = P.bank(7, BF16, "tp")
    rb = [P.bank(i, F32, f"rb{i}") for i in range(7)]
    rbi = [0]

    def nb():
        b = rb[rbi[0] % 7]
        rbi[0] += 1
        return b

    def load(t):
        P.dma("pool", xb[t % 2], xin[t * 128:(t + 1) * 128, :], ds_x[t % 2])

    def compute(t):
        s = t % 2
        ob, of, gm = OB[s], OF[s], GM[s]
        for k in range(8):
            P.tr(tp[:, k * 128:(k + 1) * 128], xb[s][:, k * 128:(k + 1) * 128], ident)
        P.copy(xT, tp.re("p (k t) -> p k t", k=8), eng="act")
        P.copy(xTs[:, :, 1:128], xT[:, :, 0:127], eng="pool")
        P.copy(xTs[:, :, 0:1], lastc, eng="pool")
        P.copy(lastc, xT[:, :, 127:128], eng="pool")
        for n in range(7):
            c0, c1 = n * 512, min(3376, (n + 1) * 512)
            b = nb()
            for k in range(8):
                P.mm(b[:, 0:c1 - c0], xT[:, k, :], win[:, k, c0:c1], start=(k == 0), stop=(k == 7))
            P.copy(Psb[:, c0:c1], b[:, 0:c1 - c0], eng=("act" if n % 2 == 0 else "dve"))
        sha, shb = ShA[s], ShB[s]
        for n in range(4):
            c0, c1 = 1536 + n * 512, min(3360, 1536 + (n + 1) * 512)
            b = nb()
            for k in range(8):
                P.mm(b[:, 0:c1 - c0], xTs[:, k, :], win[:, k, c0:c1], start=(k == 0), stop=(k == 7))
            dst = sha[:, n * 512:(n + 1) * 512] if n < 3 else shb
            P.copy(dst, b[:, 0:c1 - c0], eng=("act" if n % 2 == 1 else "dve"))
        for (v_, c0, c1, m0, eng_) in ((shb, 3072, 3360, 1536, "pool"), (sha, 1536, 3072, 0, "dve")):
            pr_ = Psb[:, c0:c1]
            mu_ = mu[:, m0:m0 + (c1 - c0)]
            P.tt(v_, v_, pr_, ALU.subtract, eng=eng_)
            P.tt(v_, v_, mu_, ALU.mult, eng=eng_)
            P.tt(v_, v_, pr_, ALU.add, eng=eng_)
        r, kx, vx = sha[:, 0:512], sha[:, 512:1024], sha[:, 1024:1536]
        wl, al, gl = shb[:, 0:64], shb[:, 64:128], shb[:, 128:288]
        P.act(Lin[:, 0:64], wl, AF.Tanh)
        P.copy(Lin[:, 64:128], al)
        P.act(sq[:, 0:160], gl, AF.Tanh, scale=0.5)
        P.ts(Lin[:, 128:288], sq[:, 0:160], 0.5, ALU.mult, 0.5, ALU.add)
        P.copy(Lin[:, 288:304], Psb[:, 3360:3376])
        for i, (lt, c0, c1) in enumerate(((LT0, 0, 128), (LT1, 128, 256), (LT2, 256, 304))):
            P.tr(tp[0:c1 - c0, i * 128:(i + 1) * 128], Lin[:, c0:c1], ident)
            P.copy(lt, tp[0:c1 - c0, i * 128:(i + 1) * 128], eng="act")
        wpre, apre, lapre, gpre = rb[4], rb[5], rb[6], rb[3]
        cum, cumx, rev, gmb = rb[0], rb[1], rb[2], rb[3]
        cumg, revg = rb[4], rb[5]
        P.mm(wpre, LT0[0:64, :], Wwa[0:64, :])
        P.mm(apre, LT0[64:128, :], Wwa[64:128, :])
        P.mm(lapre[:, 0:256], LT2[32:48, :], GW[32:48, :])
        P.mm(gpre, LT1, G2[:, 0, :], start=True, stop=False)
        P.mm(gpre, LT2[0:32, :], G2[0:32, 1, :], start=False, stop=True)
        P.copy(of[:, CF_G:CF_G + 512], gpre, eng="act")
        gq, gk = Psb[:, 0:256], Psb[:, 256:512]
        gv, gg = Psb[:, 512:1024], Psb[:, 1024:1536]
        gs = of[:, CF_GS:CF_GS + 512]

        def chainA():
            P.tt(sigw, wpre, w0, ALU.add)
            sigmoid_(P, sigw, sigw)
            P.mm(cum, TRI_i, sigw)
            P.mm(cumx, TRI_x, sigw)
            P.mm(rev, TRI_r, sigw)
            for h in range(8):
                P.mm(gmb[0:64, h * 2:(h + 1) * 2], sigw[:, h * 64:(h + 1) * 64], CBk)

        def chainB():
            P.tt(av, apre, a0, ALU.add)
            sigmoid_(P, av, av)
            P.tt(kkn, kx, k_k, ALU.mult)
            P.tt(sq, kkn, kkn, ALU.mult)
            P.reduce(ss, sq.re("p (h k) -> p h k", h=8))
            rsqrt_lnexp(P, ss, ss, mx=1e-24)
            P.tt(kkn.re("p (h k) -> p h k", h=8), kkn.re("p (h k) -> p h k", h=8), ss.unsq(2).bc([128, 8, 64]), ALU.mult)
            P.stt(k2, av, -1.0, k_a, ALU.add, ALU.mult)
            P.stt(k2, k2, 1.0, kx, ALU.add, ALU.mult)
            P.tt(bvec, kkn, av, ALU.mult)
            P.copy(ob[:, CB_V:CB_V + 512], vx, eng="act")

        def chainC():
            P.tt(sp, lapre[:, 0:256], gate_b, ALU.add)
            P.act(sp, sp, AF.Exp, scale=-1.0)
            P.act(sp, sp, AF.Ln, bias=1.0)
            P.mm(cumg[:, 0:256], TRIg_i, sp)
            P.mm(revg[:, 0:256], TRIg_r, sp)
            P.act(Eg, cumg[:, 0:256], AF.Exp)
            P.stt(ob[:, CB_GQ:CB_GQ + 256], gq, 0.125, Eg, ALU.mult, ALU.mult)
            P.act(Eg, cumg[:, 0:256], AF.Exp, scale=-1.0)
            P.tt(ob[:, CB_GK:CB_GK + 256], gk, Eg, ALU.mult)
            P.act(Eg, revg[:, 0:256], AF.Exp)
            P.tt(ob[:, CB_GKH:CB_GKH + 256], gk, Eg, ALU.mult)
            P.copy(ob[:, CB_GV:CB_GV + 512], gv, eng="act")
            sigmoid_(P, gs, gg)
            P.tt(gs, gs, gg, ALU.mult)

        P.interleave([chainA, chainB, chainC])
        for h in range(4):
            P.mm(gmb[0:64, 16 + h * 2:16 + (h + 1) * 2], sp[:, h * 64:(h + 1) * 64], CBg)
        P.act(gm, gmb[0:64, 0:24], AF.Exp)
        P.dma("sp", gam_d[t], gm, ds_o[s])
        P.act(E, cum, AF.Exp)
        P.tt(ob[:, CB_R:CB_R + 512], r, E, ALU.mult)
        P.act(E, cumx, AF.Exp)
        P.stt(ob[:, CB_A:CB_A + 512], kkn, -1.0, E, ALU.mult, ALU.mult)
        P.act(E, cum, AF.Exp, scale=-1.0)
        P.tt(ob[:, CB_B:CB_B + 512], bvec, E, ALU.mult)
        P.tt(ob[:, CB_K:CB_K + 512], k2, E, ALU.mult)
        P.act(E, rev, AF.Exp)
        P.tt(ob[:, CB_BH:CB_BH + 512], bvec, E, ALU.mult)
        P.tt(ob[:, CB_KH:CB_KH + 512], k2, E, ALU.mult)
        P.tt(sq, r, k2, ALU.mult)
        P.tt(sq, sq, r_k, ALU.mult)
        P.reduce(rks, sq.re("p (h k) -> p h k", h=8))
        P.tt(of[:, CF_BONUS:CF_BONUS + 512].re("p (h k) -> p h k", h=8), vx.re("p (h k) -> p h k", h=8),
             rks.unsq(2).bc([128, 8, 64]), ALU.mult)
        rows = slice(t * 128, (t + 1) * 128)
        P.dma("sp", prepB_d[rows, :], ob, ds_o[s])
        P.dma("sp", prepF_d[rows, :], of, ds_o[s])

    load(0)
    for t in range(NT):
        if t + 1 < NT:
            load(t + 1)
        compute(t)


def stage_even_chunk(P, S, c, prepB_d, prepF_d, gam_d, oT_d):
    P.stage_begin()
    NT = S // 128
    ds_c = P.dsem("ec_c")
    ident = P.sb([128, 128], BF16, "ident")
    P.dma("pool", ident, c["ident"], ds_c)

    def cst(name, shape):
        t = P.sb(shape, F32, name)
        P.dma("sp", t, c[name], ds_c)
        return t
    ML_s = cst("ml_strict", [128, 128])
    MU_s = cst("mu_strict", [128, 128])
    MU_i = cst("mu_incl", [128, 128])

    def bc(name, n):
        t = P.sb([128, n], F32, name)
        P.dma("sp", t, c[name].pbc(128), ds_c)
        return t
    ln_g = bc("rwkv_ln_g", 512)
    ln_b = bc("rwkv_ln_b", 512)
    norm_g = bc("gla_norm_g", 128)

    IB = [P.sb([128, WB], BF16, f"IB{s}") for s in range(2)]
    IF = [P.sb([128, WF], F32, f"IF{s}") for s in range(2)]
    GM = [P.sb([64, 24], F32, f"GMi{s}") for s in range(2)]
    ds_i = [P.dsem(f"ec_i{s}") for s in range(2)]

    def fm(name, nh=8):
        return P.sb([64, nh, 128], BF16, name)
    RT, AT, BT, KT = fm("RT"), fm("AT"), fm("BT"), fm("KT")
    RTm = [fm("RTm0"), fm("RTm1")]
    qtT, ktT = fm("qtT", 4), fm("ktT", 4)
    qtTm = [fm("qtTm0", 4), fm("qtTm1", 4)]

    def mat(name, nh=8):
        return P.sb([128, nh, 128], BF16, name)
    Pm = [mat("Pm0"), mat("Pm1")]
    PTm = [mat("PTm0"), mat("PTm1")]
    Qm = [mat("Qm0"), mat("Qm1")]
    AakT, ArbT, ArkT = mat("AakT"), mat("ArbT"), mat("ArkT")
    attnT = mat("attnT", 4)
    WtT = fm("WtT")
    AV = P.sb([128, 8, 64], BF16, "AV")
    Ut = P.sb([128, 8, 64], F32, "Ut")
    Usb = P.sb([128, 8, 64], BF16, "Usb")
    H = P.sb([64, 8, 64], F32, "H")
    Hb = [P.sb([64, 8, 64], BF16, f"Hb{i}") for i in range(3)]
    Sg = P.sb([64, 4, 128], F32, "Sg")
    Sgb = [P.sb([64, 4, 128], BF16, f"Sgb{i}") for i in range(3)]
    Ysb = P.sb([128, 8, 64], F32, "Ysb")
    ysq = P.sb([128, 8, 64], F32, "ysq")
    Osb = P.sb([128, 4, 128], F32, "Osb")
    s1 = P.sb([128, 8], F32, "s1")
    s2 = P.sb([128, 8], F32, "s2")
    MIX = P.sb([128, 1024], BF16, "MIX")
    mixT = [P.sb([128, 8, 128], BF16, f"mixT{s}") for s in range(2)]
    ds_m = [P.dsem(f"ec_m{s}") for s in range(2)]
    BHm = [P.sb([128, 512], BF16, f"BHm{i}") for i in range(2)]
    KHm = [P.sb([128, 512], BF16, f"KHm{i}") for i in range(2)]
    GKHm = [P.sb([128, 256], BF16, f"GKHm{i}") for i in range(2)]
    for tl in RTm + qtTm + BHm + KHm + GKHm + [Usb]:
        P.memset(tl, 0.0)
    P.memset(H, 0.0)
    P.memset(Hb[0], 0.0)
    P.memset(Sg, 0.0)
    P.memset(Sgb[0], 0.0)
    tp = P.bank(7, BF16, "tp")
    Yb = P.bank(6, F32, "Yb")
    Ob = P.bank(5, F32, "Ob")
    rb = [P.bank(i, F32, f"rb{i}") for i in range(5)]
    rbi = [0]

    def nb():
        b = rb[rbi[0] % 5]
        rbi[0] += 1
        return b
    oT_v = oT_d.re("(k two) d s -> (two d) k s", two=2)

    def load(t):
        s = t % 2
        rows = slice(t * 128, (t + 1) * 128)
        P.dma("sp", IB[s], prepB_d[rows, :], ds_i[s])
        P.dma("sp", IF[s], prepF_d[rows, :], ds_i[s])
        P.dma("sp", GM[s], gam_d[t], ds_i[s])

    def transpose_heads(dst, src, c0, nh, dstm=None):
        for h in range(nh):
            P.tr(tp[0:64, h * 128:(h + 1) * 128], src[:, c0 + h * 64:c0 + (h + 1) * 64], ident)
        tv = tp[0:64, 0:nh * 128].re("p (h t) -> p h t", h=nh)
        P.copy(dst, tv, eng="act")
        if dstm is not None and EC_VAR != 'b':
            P.copy(dstm[0][:, :, 0:64], dst[:, :, 0:64], eng="pool")
            P.copy(dstm[1][:, :, 64:128], dst[:, :, 64:128], eng="pool")

    def headmm(dst, lhs, rhs, mask, nh=8):
        for g in range(nh // 4):
            b = nb()
            for hh in range(4):
                h = g * 4 + hh
                P.mm(b[:, hh * 128:(hh + 1) * 128], lhs[:, h, :], rhs[:, h, :])
            bv = b.re("p (h t) -> p h t", h=4)
            if mask is None:
                P.copy(dst[:, g * 4:(g + 1) * 4, :], bv, eng="act")
            else:
                P.tt(dst[:, g * 4:(g + 1) * 4, :], bv, mask.unsq(1).bc([128, 4, 128]), ALU.mult)

    def compute(t):
        s = t % 2
        ib, iff, gm = IB[s], IF[s], GM[s]
        if EC_STOP <= 0:
            return
        transpose_heads(RT, ib, CB_R, 8, RTm)
        transpose_heads(AT, ib, CB_A, 8)
        transpose_heads(BT, ib, CB_B, 8)
        transpose_heads(KT, ib, CB_K, 8)
        if EC_VAR == 'a':
            return
        L0, M0 = PTm[0], Pm[0]
        headmm(L0, AT, BT, ML_s)
        headmm(M0, BT, AT, MU_s)
        headmm(AakT, KT, AT, MU_s)
        headmm(ArbT, BT, RT, MU_i)
        headmm(ArkT, KT, RT, MU_i)
        if EC_STOP <= 1:
            return
        P.tt(Qm[0], M0, ident.unsq(1).bc([128, 8, 128]), ALU.add)
        cur = 0
        for lvl in range(5):
            nxt = 1 - cur
            if lvl < 4:
                headmm(Pm[nxt], PTm[cur], Pm[cur], None)
            headmm(PTm[nxt], Pm[cur], PTm[cur], None)
            for g in range(2):
                b = nb()
                for hh in range(4):
                    h = g * 4 + hh
                    P.mm(b[:, hh * 128:(hh + 1) * 128], PTm[nxt][:, h, :], Qm[cur][:, h, :])
                P.tt(Qm[nxt][:, g * 4:(g + 1) * 4, :], b.re("p (h t) -> p h t", h=4), Qm[cur][:, g * 4:(g + 1) * 4, :], ALU.add)
            cur = nxt
        TT = Qm[cur]
        if EC_STOP <= 2:
            return
        for g in range(2):
            b = nb()
            for hh in range(4):
                h = g * 4 + hh
                P.mm(b[0:64, hh * 128:(hh + 1) * 128], ib[:, CB_A + h * 64:CB_A + (h + 1) * 64], TT[:, h, :])
            P.copy(WtT[:, g * 4:(g + 1) * 4, :], b[0:64, :].re("p (h t) -> p h t", h=4), eng="act")
        b = nb()
        for h in range(8):
            P.mm(b[:, h * 64:(h + 1) * 64], AakT[:, h, :], ib[:, CB_V + h * 64:CB_V + (h + 1) * 64])
        P.copy(AV, b.re("p (h v) -> p h v", h=8), eng="act")
        b = nb()
        for h in range(8):
            P.mm(b[:, h * 64:(h + 1) * 64], TT[:, h, :], AV[:, h, :])
        P.copy(Ut, b.re("p (h v) -> p h v", h=8), eng="act")
        if EC_STOP <= 3:
            return
        transpose_heads(qtT, ib, CB_GQ, 4, qtTm)
        transpose_heads(ktT, ib, CB_GK, 4)
        headmm(attnT, ktT, qtT, MU_i, nh=4)
        if EC_STOP <= 4:
            return
        for cc in range(2):
            rs = slice(cc * 64, (cc + 1) * 64)
            P.copy(BHm[cc][rs, :], ib[rs, CB_BH:CB_BH + 512], eng="pool")
            P.copy(KHm[cc][rs, :], ib[rs, CB_KH:CB_KH + 512], eng="pool")
            P.copy(GKHm[cc][rs, :], ib[rs, CB_GKH:CB_GKH + 256], eng="pool")
        for cc in range(2):
            rs = slice(cc * 64, (cc + 1) * 64)
            hb_in, hb_out = Hb[(2 * t + cc) % 3], Hb[(2 * t + cc + 1) % 3]
            sb_in, sb_out = Sgb[(2 * t + cc) % 3], Sgb[(2 * t + cc + 1) % 3]
            up = nb()
            for h in range(8):
                P.mm(up[:, h * 64:(h + 1) * 64], WtT[:, h, :], hb_in[:, h, :])
            P.tt(Usb[rs, :, :], up[rs, :].re("p (h v) -> p h v", h=8), Ut[rs, :, :], ALU.add)
            dh = nb()
            for h in range(8):
                P.mm(dh[0:64, h * 64:(h + 1) * 64], BHm[cc][:, h * 64:(h + 1) * 64], Usb[:, h, :],
                     start=True, stop=False)
                P.mm(dh[0:64, h * 64:(h + 1) * 64], KHm[cc][:, h * 64:(h + 1) * 64],
                     ib[:, CB_V + h * 64:CB_V + (h + 1) * 64], start=False, stop=True)
            gsel = gm[:, 0:16].re("p (h c) -> p h c", c=2)[:, :, cc]
            P.tt(H, H, gsel.unsq(2).bc([64, 8, 64]), ALU.mult)
            P.tt(H, H, dh[0:64, :].re("p (h v) -> p h v", h=8), ALU.add)
            P.copy(hb_out, H, eng="act")
            ds_ = nb()
            for h in range(4):
                P.mm(ds_[0:64, h * 128:(h + 1) * 128], GKHm[cc][:, h * 64:(h + 1) * 64],
                     ib[:, CB_GV + h * 128:CB_GV + (h + 1) * 128])
            gsel2 = gm[:, 16:24].re("p (h c) -> p h c", c=2)[:, :, cc]
            P.tt(Sg, Sg, gsel2.unsq(2).bc([64, 4, 128]), ALU.mult)
            P.tt(Sg, Sg, ds_[0:64, :].re("p (h v) -> p h v", h=4), ALU.add)
            P.copy(sb_out, Sg, eng="act")
        for h in range(8):
            ys_ = Yb[:, h * 64:(h + 1) * 64]
            P.mm(ys_, RTm[0][:, h, :], Hb[(2 * t) % 3][:, h, :], start=True, stop=False)
            P.mm(ys_, RTm[1][:, h, :], Hb[(2 * t + 1) % 3][:, h, :], start=False, stop=False)
            P.mm(ys_, ArbT[:, h, :], Usb[:, h, :], start=False, stop=False)
            P.mm(ys_, ArkT[:, h, :], ib[:, CB_V + h * 64:CB_V + (h + 1) * 64], start=False, stop=True)
        for h in range(4):
            os_ = Ob[:, h * 128:(h + 1) * 128]
            P.mm(os_, qtTm[0][:, h, :], Sgb[(2 * t) % 3][:, h, :], start=True, stop=False)
            P.mm(os_, qtTm[1][:, h, :], Sgb[(2 * t + 1) % 3][:, h, :], start=False, stop=False)
            P.mm(os_, attnT[:, h, :], ib[:, CB_GV + h * 128:CB_GV + (h + 1) * 128], start=False, stop=True)
        if EC_STOP <= 5:
            return
        P.copy(Ysb, Yb.re("p (h v) -> p h v", h=8), eng="act")
        P.reduce(s1, Ysb)
        P.tt(ysq, Ysb, Ysb, ALU.mult)
        P.reduce(s2, ysq)
        P.ts(s1, s1, 1.0 / 64, ALU.mult)
        P.stt(ysq[:, :, 0], s1, -1.0, s1, ALU.mult, ALU.mult)
        P.stt(s2, s2, 1.0 / 64, ysq[:, :, 0], ALU.mult, ALU.add)
        _r = rsqrt_lnexp
        _r(P, s2, s2, add=64e-5)
        P.tt(Ysb, Ysb, s1.unsq(2).bc([128, 8, 64]), ALU.subtract)
        P.tt(Ysb, Ysb, s2.unsq(2).bc([128, 8, 64]), ALU.mult)
        yf = Ysb.re("p h v -> p (h v)")
        P.tt(yf, yf, ln_g, ALU.mult)
        P.tt(yf, yf, ln_b, ALU.add)
        P.tt(yf, yf, iff[:, CF_BONUS:CF_BONUS + 512], ALU.add)
        P.tt(MIX[:, 512:1024], yf, iff[:, CF_G:CF_G + 512], ALU.mult)
        if EC_STOP <= 6:
            return
        P.copy(Osb, Ob.re("p (h v) -> p h v", h=4), eng="act")
        osq = ysq.re("p h v -> p (h v)").re("p (h v) -> p h v", h=4)
        P.tt(osq, Osb, Osb, ALU.mult)
        P.reduce(s1[:, 0:4], osq)
        _r(P, s1[:, 0:4], s1[:, 0:4], scale_in=1.0 / 128, add=1e-6)
        P.tt(Osb, Osb, s1[:, 0:4].unsq(2).bc([128, 4, 128]), ALU.mult)
        P.tt(Osb, Osb, norm_g.unsq(1).bc([128, 4, 128]), ALU.mult)
        P.tt(MIX[:, 0:512], Osb.re("p h v -> p (h v)"), iff[:, CF_GS:CF_GS + 512], ALU.mult)
        if EC_STOP <= 7:
            return
        mt = mixT[s]
        for k in range(8):
            P.tr(tp[:, k * 128:(k + 1) * 128], MIX[:, k * 128:(k + 1) * 128], ident)
        P.copy(mt, tp.re("p (k t) -> p k t", k=8), eng="act")
        P.dma("sp", oT_v[:, :, t * 128:(t + 1) * 128], mt, ds_m[s])

    load(0)
    for t in range(NT):
        if t + 1 < NT:
            load(t + 1)
        compute(t)


NCORES = 8


def host_consts():
    c = {}
    c["ident"] = np.eye(128, dtype=np.float32)
    idx = np.arange(128)
    same = (idx[:, None] // 64) == (idx[None, :] // 64)
    lt = idx[:, None] < idx[None, :]
    le = idx[:, None] <= idx[None, :]
    gt = idx[:, None] > idx[None, :]
    EC = math.exp(-0.5)
    c["tri_incl"] = (-EC * (same & le)).astype(np.float32)
    c["tri_excl"] = (-EC * (same & lt)).astype(np.float32)
    c["tri_rev"] = (-EC * (same & gt)).astype(np.float32)
    c["trig_incl"] = (-(1.0 / 16) * (same & le)).astype(np.float32)
    c["trig_rev"] = (-(1.0 / 16) * (same & gt)).astype(np.float32)
    cb = np.zeros((128, 2), np.float32)
    cb[0:64, 0] = 1
    cb[64:128, 1] = 1
    c["cb"] = (-EC * cb).astype(np.float32)
    c["cbg"] = (-(1.0 / 16) * cb).astype(np.float32)
    c["ml_strict"] = (same & gt).astype(np.float32)
    c["mu_strict"] = (same & lt).astype(np.float32)
    c["mu_incl"] = (same & le).astype(np.float32)
    c["invf"] = (10000.0 ** (-np.arange(0, 32, 2, dtype=np.float32) / 32)).astype(np.float32)
    c["ones"] = np.ones((128, 64), np.float32)
    return c


CONST_SHAPES = dict(ident=[128, 128], tri_incl=[128, 128], tri_excl=[128, 128], tri_rev=[128, 128],
                    trig_incl=[128, 128], trig_rev=[128, 128], cb=[128, 2], cbg=[128, 2],
                    ml_strict=[128, 128], mu_strict=[128, 128], mu_incl=[128, 128], invf=[16], ones=[128, 64])

WEIGHT_SHAPES = dict(
    even_w_in=[1024, 3376], gla_gate_w2=[16, 256], gla_gate_b=[256], gla_norm_g=[128], rwkv_mu=[1824],
    rwkv_w0=[512], rwkv_w2=[64, 512], rwkv_a0=[512], rwkv_w2_=None, rwkv_a2=[64, 512], rwkv_g2=[160, 512],
    rwkv_k_k=[512], rwkv_k_a=[512], rwkv_r_k=[512], rwkv_ln_g=[512], rwkv_ln_b=[512], even_w_out=[1024, 1024],
    mla_w_in=[1024, 1056], mla_q_norm_g=[768], mla_w_q_b=[768, 1536], mla_kv_norm_g=[256], mla_w_kv_b=[256, 2048],
    mla_w_out=[1024, 1024], ffn_gu0=[1024, 5632], ffn_gu1=[1024, 5632], ffn_d0=[2816, 1024], ffn_d1=[2816, 1024],
    ln_g00=[1024], ln_g01=[1024], ln_g10=[1024], ln_g11=[1024], ln_b00=[1024], ln_b01=[1024], ln_b10=[1024], ln_b11=[1024])
del WEIGHT_SHAPES["rwkv_w2_"]


def host_weights(inp):
    w = {}
    f = lambda a: np.ascontiguousarray(np.asarray(a, dtype=np.float32))
    ew = np.asarray(inp["even_w_in"][0])
    o = 1552
    perm = np.concatenate([np.arange(0, 1536), o + np.arange(0, 1824), np.arange(1536, 1552)])
    w["even_w_in"] = f(ew[:, perm])
    for k in ("gla_gate_w2", "gla_gate_b", "gla_norm_g", "rwkv_mu", "rwkv_w0", "rwkv_w2", "rwkv_a0", "rwkv_a2",
              "rwkv_g2", "rwkv_k_k", "rwkv_k_a", "rwkv_ln_g", "rwkv_ln_b", "even_w_out", "mla_w_in",
              "mla_q_norm_g", "mla_kv_norm_g", "mla_w_kv_b", "mla_w_out"):
        w[k] = f(inp[k][0])
    w["rwkv_r_k"] = f(np.asarray(inp["rwkv_r_k"][0]).reshape(512))
    wq = np.asarray(inp["mla_w_q_b"][0]).reshape(768, 16, 96)
    w["mla_w_q_b"] = f(np.concatenate([wq[:, :, 0:64].reshape(768, 1024), wq[:, :, 64:96].reshape(768, 512)], 1))
    for i in range(2):
        gu = np.asarray(inp["ffn_w_gate_up"][i])
        w[f"ffn_gu{i}"] = f(gu.reshape(1024, 2, 22, 128).transpose(0, 2, 1, 3).reshape(1024, 5632))
        w[f"ffn_d{i}"] = f(inp["ffn_w_down"][i])
        for j in range(2):
            w[f"ln_g{i}{j}"] = f(inp["ln_g"][i, j])
            w[f"ln_b{i}{j}"] = f(inp["ln_b"][i, j])
    return w


def build(S, debug=False, stages=("ep", "ec", "eo", "f0", "mp", "at", "mo", "f1")):
    nc = bass.Bass("TRN2", target_bir_lowering=False)
    with ExitStack() as st:
        P = Prog(nc, st)
        P.init_mem()
        kind_dbg = "ExternalOutput" if debug else "Internal"
        x = P.dram("x", [S, 1024], F32, kind="ExternalInput")
        pos = P.dram("pos", [128, S // 128], I32, kind="ExternalInput")
        c = {k: P.dram(k, v, F32, kind="ExternalInput") for k, v in CONST_SHAPES.items()}
        w = {k: P.dram(k, v, F32, kind="ExternalInput") for k, v in WEIGHT_SHAPES.items()}
        c.update(w)
        prepB = P.dram("prepB", [S, WB], BF16)
        prepF = P.dram("prepF", [S, WF], F32)
        gam = P.dram("gam", [S // 128, 64, 24], F32)
        oT = P.dram("oT", [16, 64, S], BF16)
        oT2 = P.dram("oT2", [16, 64, S], BF16)
        qT = P.dram("qT", [16, 96, S], BF16)
        kT = P.dram("kT", [16, 96, S], BF16)
        vA = P.dram("vA", [S, 16 * 65], BF16)
        x1 = P.dram("x1", [S, 1024], F32, kind=kind_dbg if "eo" in stages else "ExternalInput")
        x2 = P.dram("x2", [S, 1024], F32, kind=kind_dbg if "f0" in stages else "ExternalInput")
        x3 = P.dram("x3", [S, 1024], F32, kind=kind_dbg if "mo" in stages else "ExternalInput")
        out = P.dram("out", [S, 1024], F32, kind="ExternalOutput")
        if "ep" in stages:
            stage_even_prep(P, S, x, c, prepB, prepF, gam)
        if "ec" in stages:
            stage_even_chunk(P, S, c, prepB, prepF, gam, oT)
        if "eo" in stages:
            stage_mla_out(P, S, oT, x, c["even_w_out"], c["ln_g00"], c["ln_b00"], x1)
        if "f0" in stages:
            stage_ffn(P, S, x1, c["ffn_gu0"], c["ffn_d0"], c["ln_g01"], c["ln_b01"], x2, c["ident"])
        if "mp" in stages:
            stage_mla_proj(P, S, x2, pos, c["invf"], c["mla_w_in"], c["mla_q_norm_g"], c["mla_w_q_b"],
                           c["mla_kv_norm_g"], c["mla_w_kv_b"], c["ident"], qT, kT, vA)
        if "at" in stages:
            stage_attn(P, S, qT, kT, vA, oT2, c["ones"])
        if "mo" in stages:
            stage_mla_out(P, S, oT2, x2, c["mla_w_out"], c["ln_g10"], c["ln_b10"], x3)
        if "f1" in stages:
            stage_ffn(P, S, x3, c["ffn_gu1"], c["ffn_d1"], c["ln_g11"], c["ln_b11"], out, c["ident"])
        P.barrier()
        P.emit()
        print({e: len(P.ins[e]) for e in ENGS}, flush=True)
    return nc


_NC_CACHE = {}


def kernel(**inp):
    x = np.asarray(inp["x"], dtype=np.float32)
    B, S, _ = x.shape
    pos = np.asarray(inp["positions"]).astype(np.int32)
    consts = host_consts()
    wts = host_weights(inp)
    if S not in _NC_CACHE:
        _NC_CACHE[S] = build(S)
    nc = _NC_CACHE[S]
    in_maps = []
    for b in range(B):
        m = dict(x=np.ascontiguousarray(x[b]),
                 pos=np.ascontiguousarray(pos[b].reshape(S // 128, 128).T))
        m.update(consts)
        m.update(wts)
        in_maps.append(m)
    res = run_bass_kernel_spmd(nc, in_maps, core_ids=list(range(B)))
    return np.stack([np.asarray(r["out"], dtype=np.float32) for r in res.results], 0)
```

```python
import math
import threading
from contextlib import ExitStack
import threading
import numpy as np
import concourse.bass as bass
import concourse.mybir as mybir
from concourse.bass_utils import run_bass_kernel_spmd

F32 = mybir.dt.float32
BF16 = mybir.dt.bfloat16
I32 = mybir.dt.int32
ALU = mybir.AluOpType
AF = mybir.ActivationFunctionType
AX = mybir.AxisListType

ENGS = ("pe", "act", "dve", "pool", "sp")


class Buf:
    __slots__ = ("name", "writes", "reads")

    def __init__(self, name=""):
        self.name = name
        self.writes = []
        self.reads = []


class V:
    __slots__ = ("ap", "buf")

    def __init__(self, ap, buf):
        self.ap = ap
        self.buf = buf

    def __getitem__(self, idx):
        return V(self.ap[idx], self.buf)

    def re(self, pattern, **kw):
        return V(self.ap.rearrange(pattern, **kw), self.buf)

    def sub(self, name=""):
        return V(self.ap, Buf(name))

    def bc(self, shape):
        return V(self.ap.to_broadcast(list(shape)), self.buf)

    def unsq(self, axis):
        return V(self.ap.unsqueeze(axis), self.buf)

    def pbc(self, n):
        return V(self.ap.partition_broadcast(n), self.buf)

    def bitcast(self, dt):
        return V(self.ap.bitcast(dt), self.buf)


class DmaPart:
    def __init__(self, sem, name):
        self.sem = sem
        self.total = 0
        self.name = name


class DmaSem:
    def __init__(self, prog, name):
        self.prog = prog
        self.name = name
        self.parts = {}

    def part(self, eng):
        k = "sw" if eng == "pool" else "hw"
        if k not in self.parts:
            p = self.prog
            sem = p.stack.enter_context(p.nc.semaphore(f"ds{len(p.dsems)}_{k}_{self.name}"))
            self.parts[k] = DmaPart(sem, self.name + k)
            p.dsems.append(self.parts[k])
        return self.parts[k]

    @property
    def total(self):
        return sum(x.total for x in self.parts.values())


class Prog:
    def __init__(self, nc, stack):
        self.nc = nc
        self.stack = stack
        self.ins = {e: [] for e in ENGS}
        self.waited = {e: {} for e in ENGS}
        self.esem = {e: stack.enter_context(nc.semaphore("es_" + e)) for e in ENGS}
        self.dsems = []
        self.ntile = 0
        self._il_step = None

    ARENA = 212800

    def init_mem(self):
        nc = self.nc
        self.arena = self.stack.enter_context(nc.sbuf_tensor("arena", [128, self.ARENA], mybir.dt.uint8))
        self.banks = [self.stack.enter_context(nc.psum_tensor(f"bank{i}", [128, 512], F32)) for i in range(8)]
        self.off = 0

    def stage_begin(self):
        self.barrier()
        self.off = 0

    def sb(self, shape, dtype, name=""):
        esz = {F32: 4, BF16: 2, I32: 4}[dtype]
        n = 1
        for d in shape[1:]:
            n *= d
        nb = (n * esz + 31) // 32 * 32
        assert self.off + nb <= self.ARENA, f"SBUF arena overflow: {self.off}+{nb} ({name})"
        ap = self.arena[0:shape[0], self.off:self.off + n * esz].bitcast(dtype)
        self.off += nb
        if len(shape) == 3:
            ap = ap.rearrange("p (a b) -> p a b", a=shape[1])
        elif len(shape) == 4:
            ap = ap.rearrange("p (a b c) -> p a b c", a=shape[1], b=shape[2])
        return V(ap, Buf(name))

    def bank(self, i, dtype=F32, name=""):
        ap = self.banks[i][:, :]
        if dtype != F32:
            ap = ap.bitcast(dtype)
        return V(ap, Buf(name or f"bank{i}"))

    def dram(self, name, shape, dtype, kind="Internal"):
        t = self.nc.dram_tensor(name, list(shape), dtype, kind=kind)
        return V(t.ap(), None)

    def dsem(self, name):
        return DmaSem(self, name)

    def _deps(self, eng, reads, writes):
        toks = []
        for v in reads:
            b = v.buf if isinstance(v, V) else v
            if b is None:
                continue
            toks.extend((t, True) for t in b.writes)
        for v in writes:
            b = v.buf if isinstance(v, V) else v
            if b is None:
                continue
            toks.extend((t, False) for t in b.writes)
            toks.extend((t, False) for t in b.reads)
        waits = []
        wd = self.waited[eng]
        for tk, raw in toks:
            if tk[0] == "e":
                _, f, idx = tk
                if f == eng and (eng == "pe" or not raw):
                    continue
                if wd.get(("e", f), -1) >= idx:
                    continue
                wd[("e", f)] = idx
                self.ins[f][idx]["signal"] = True
                waits.append(tk)
            else:
                _, ds = tk
                if wd.get(("d", id(ds)), -1) >= ds.total:
                    continue
                wd[("d", id(ds))] = ds.total
                waits.append(("d", ds, ds.total))
        return waits

    def _commit(self, tok, reads, writes):
        for v in reads:
            b = v.buf if isinstance(v, V) else v
            if b is None:
                continue
            if tok[0] == "e":
                b.reads = [t for t in b.reads if not (t[0] == "e" and t[1] == tok[1])]
            elif tok in b.reads:
                continue
            b.reads.append(tok)
        for v in writes:
            b = v.buf if isinstance(v, V) else v
            if b is None:
                continue
            b.writes = [tok]
            b.reads = []

    def op(self, eng, fn, reads=(), writes=()):
        waits = self._deps(eng, reads, writes)
        idx = len(self.ins[eng])
        self.ins[eng].append(dict(fn=fn, waits=waits, signal=False, dsem=None))
        self._commit(("e", eng, idx), reads, writes)
        if self._il_step is not None:
            self._il_step()
        return idx

    def dma(self, eng, out, in_, ds, serial=False, **kw):
        ds = ds.part(eng)
        waits = self._deps(eng, [in_], [out])
        if serial and ds.total and self.waited[eng].get(("d", id(ds)), -1) < ds.total:
            self.waited[eng][("d", id(ds))] = ds.total
            waits.append(("d", ds, ds.total))
        idx = len(self.ins[eng])
        ds.total += 16
        oa, ia = out.ap, in_.ap
        self.ins[eng].append(dict(fn=lambda e: e.dma_start(out=oa, in_=ia, **kw), waits=waits,
                                  signal=False, dsem=ds))
        self._commit(("d", ds), [in_], [out])
        if self._il_step is not None:
            self._il_step()

    def interleave(self, fns):
        n = len(fns)
        import os
        if n == 1 or os.environ.get('NO_IL'):
            for f in fns:
                f()
            return
        cv = threading.Condition()
        st = {"turn": 0, "alive": [True] * n, "err": None}
        loc = threading.local()

        def nxt(i):
            for d in range(1, n + 1):
                j = (i + d) % n
                if st["alive"][j]:
                    return j
            return None

        def step():
            i = loc.idx
            with cv:
                j = nxt(i)
                if j is None or j == i:
                    return
                st["turn"] = j
                cv.notify_all()
                while st["turn"] != i:
                    cv.wait()

        def worker(i):
            loc.idx = i
            with cv:
                while st["turn"] != i:
                    cv.wait()
            try:
                fns[i]()
            except BaseException as e:
                st["err"] = e
            with cv:
                st["alive"][i] = False
                st["turn"] = nxt(i)
                cv.notify_all()

        self._il_step = step
        ths = [threading.Thread(target=worker, args=(i,)) for i in range(n)]
        for t in ths:
            t.start()
        for t in ths:
            t.join()
        self._il_step = None
        if st["err"] is not None:
            raise st["err"]

    def wdma(self, out, in_, eng="pool"):
        if not hasattr(self, "wsems"):
            self.wsems = [self.dsem(f"w{i}") for i in range(6)]
            self.wcnt = 0
        ds = self.wsems[self.wcnt % len(self.wsems)]
        self.wcnt += 1
        self.dma(eng, out, in_, ds, serial=True)

    def barrier(self):
        for e in ENGS:
            waits = []
            wd = self.waited[e]
            for f in ENGS:
                if f == e or not self.ins[f]:
                    continue
                idx = None
                for k in range(len(self.ins[f]) - 1, -1, -1):
                    if self.ins[f][k]["dsem"] is None and self.ins[f][k]["fn"] is not None:
                        idx = k
                        break
                if idx is None or wd.get(("e", f), -1) >= idx:
                    continue
                wd[("e", f)] = idx
                self.ins[f][idx]["signal"] = True
                waits.append(("e", f, idx))
            for ds in self.dsems:
                if ds.total and wd.get(("d", id(ds)), -1) < ds.total:
                    wd[("d", id(ds))] = ds.total
                    waits.append(("d", ds, ds.total))
            if waits:
                self.ins[e].append(dict(fn=None, waits=waits, signal=False, dsem=None))

    def emit(self):
        nc = self.nc
        rank = {}
        for e in ENGS:
            c = 0
            r = []
            for rec in self.ins[e]:
                if rec["signal"]:
                    c += 1
                r.append(c)
            rank[e] = r
        print("signals", {e: (rank[e][-1] if rank[e] else 0) for e in ENGS}, "dma_max", max([d.total for d in self.dsems] + [0]), flush=True)
        handles = {"pe": "tensor", "act": "scalar", "dve": "vector", "pool": "gpsimd", "sp": "sync"}

        def run(e, eng):
            for rec in self.ins[e]:
                for w in rec["waits"]:
                    if w[0] == "e":
                        eng.wait_ge(self.esem[w[1]], rank[w[1]][w[2]])
                    else:
                        eng.wait_ge(w[1].sem, w[2])
                if rec["fn"] is None:
                    continue
                ins = rec["fn"](eng)
                if rec["dsem"] is not None:
                    ins.then_inc(rec["dsem"].sem, 16)
                elif rec["signal"]:
                    ins.then_inc(self.esem[e], 1)

        with nc.Block() as block:
            for e in ENGS:
                if not self.ins[e]:
                    continue
                getattr(block, handles[e])(lambda eng, e=e: run(e, eng))

    def mm(self, out, lhsT, rhs, start=True, stop=True):
        o, l, r = out.ap, lhsT.ap, rhs.ap
        return self.op("pe", lambda e: e.matmul(o, l, r, start=start, stop=stop), [lhsT, rhs], [out])

    def tr(self, out, in_, ident):
        o, i, d = out.ap, in_.ap, ident.ap
        return self.op("pe", lambda e: e.transpose(o, i, d), [in_, ident], [out])

    def act(self, out, in_, func, bias=None, scale=None, accum=None, eng="act"):
        o, i = out.ap, in_.ap
        kw = {}
        rd = [in_]
        wr = [out]
        if bias is not None:
            if isinstance(bias, V):
                kw["bias"] = bias.ap
                rd.append(bias)
            else:
                kw["bias"] = bias
        if scale is not None:
            if isinstance(scale, V):
                kw["scale"] = scale.ap
                rd.append(scale)
            else:
                kw["scale"] = scale
        if accum is not None:
            kw["accum_out"] = accum.ap
            wr.append(accum)
        return self.op("act", lambda e: e.activation(o, i, func, **kw), rd, wr)

    def tt(self, out, a, b, op, eng="dve"):
        o, x, y = out.ap, a.ap, b.ap
        return self.op(eng, lambda e: e.tensor_tensor(o, x, y, op), [a, b], [out])

    def ts(self, out, a, s1, op0, s2=None, op1=None, eng="dve", accum=None):
        o, x = out.ap, a.ap
        rd = [a]
        wr = [out]
        a1 = s1
        if isinstance(s1, V):
            rd.append(s1)
            a1 = s1.ap
        a2 = s2
        if isinstance(s2, V):
            rd.append(s2)
            a2 = s2.ap
        kw = {}
        if op1 is not None:
            kw["op1"] = op1
        if accum is not None:
            kw["accum_out"] = accum.ap
            wr.append(accum)
        return self.op(eng, lambda e: e.tensor_scalar(o, x, a1, a2, op0, **kw), rd, wr)

    def stt(self, out, a, s, b, op0, op1, eng="dve"):
        o, x, y = out.ap, a.ap, b.ap
        rd = [a, b]
        sc = s
        if isinstance(s, V):
            rd.append(s)
            sc = s.ap
        return self.op(eng, lambda e: e.scalar_tensor_tensor(o, x, sc, y, op0, op1), rd, [out])

    def copy(self, out, in_, eng="dve"):
        o, i = out.ap, in_.ap
        if eng == "act":
            return self.op("act", lambda e: e.copy(o, i), [in_], [out])
        return self.op(eng, lambda e: e.tensor_copy(o, i), [in_], [out])

    def memset(self, out, val, eng="dve"):
        o = out.ap
        return self.op(eng, lambda e: e.memset(o, val), [], [out])

    def reduce(self, out, in_, op=None, axis=None, eng="dve"):
        o, i = out.ap, in_.ap
        op = op or ALU.add
        axis = axis or AX.X
        return self.op(eng, lambda e: e.tensor_reduce(o, i, axis, op), [in_], [out])


DN_ALPHA = 4.0 ** 0.25
LN_EPS = 1e-5
D = 1024
FH = 2816
NF = 22


def ln_epilogue(P, z, xs, ods, g_bc, b_bc, ys, tmp):
    for n in range(2):
        sl = slice(n * 512, (n + 1) * 512)
        P.stt(z[:, sl], xs[:, sl], DN_ALPHA, ods[n], ALU.mult, ALU.add)
    ln_core(P, z, g_bc, b_bc, ys, tmp)


def ln_core(P, z, g_bc, b_bc, ys, tmp):
    stats, mv, rstd = tmp["stats"], tmp["mv"], tmp["rstd"]
    for n in range(2):
        sl = slice(n * 512, (n + 1) * 512)
        so, zi = stats[:, n, :].ap, z[:, sl].ap
        P.op("dve", lambda e, so=so, zi=zi: e.bn_stats(so, zi), [z], [stats])
    mo, si = mv.ap, stats.re("p a b -> p (a b)").ap
    P.op("dve", lambda e: e.bn_aggr(mo, si), [stats], [mv])
    P.ts(rstd, mv[:, 1:2], LN_EPS, ALU.add)
    P.act(rstd, rstd, AF.Sqrt)
    ro = rstd.ap
    P.op("dve", lambda e: e.reciprocal(ro, ro), [rstd], [rstd])
    P.ts(z, z, mv[:, 0:1], ALU.subtract, rstd[:, 0:1], ALU.mult)
    P.tt(z, z, g_bc, ALU.mult, eng="pool")
    P.tt(ys, z, b_bc, ALU.add, eng="pool")


def stage_ffn(P, S, xin, wgu_d, wd_d, lng_d, lnb_d, yout, ident_d):
    P.stage_begin()
    TT = 256
    NT = S // TT
    ident = P.sb([128, 128], BF16, "ident")
    wgu = P.sb([128, 8, NF * 256], BF16, "wgu")
    wd = P.sb([128, NF, D], BF16, "wd")
    g_bc = P.sb([128, D], F32, "g_bc")
    b_bc = P.sb([128, D], F32, "b_bc")
    ds_c = P.dsem("ffn_c")
    P.dma("pool", ident, ident_d, ds_c)
    P.dma("sp", g_bc, lng_d.pbc(128), ds_c)
    P.dma("sp", b_bc, lnb_d.pbc(128), ds_c)
    wgu_chunks = [wgu[:, :, f * 512:(f + 1) * 512].sub(f"wgu{f}") for f in range(NF // 2)]
    wd_chunks = [wd[:, f * 2:(f + 1) * 2, :].sub(f"wd{f}") for f in range(NF // 2)]
    wgu_src = wgu_d.re("(k p) n -> p k n", p=128)
    wd_src = wd_d.re("(f p) n -> p f n", p=128)

    xs = [[P.sb([128, D], F32, f"xs{s}{u}") for u in range(2)] for s in range(2)]
    xb = [[P.sb([128, D], BF16, f"xb{s}{u}") for u in range(2)] for s in range(2)]
    ds_x = [P.dsem(f"ffn_x{s}") for s in range(2)]
    xT = [P.sb([128, 8, TT], BF16, f"xT{s}") for s in range(2)]
    hT = P.sb([128, NF, TT], BF16, "hT")
    sg = [P.sb([128, TT], F32, f"sg{s}") for s in range(2)]
    z = [P.sb([128, D], F32, f"z{s}") for s in range(2)]
    ys = [P.sb([128, D], F32, f"ys{s}") for s in range(2)]
    ds_y = [P.dsem(f"ffn_y{s}") for s in range(2)]
    tmp = [dict(stats=P.sb([128, 2, 6], F32), mv=P.sb([128, 2], F32), rstd=P.sb([128, 1], F32)) for _ in range(2)]
    tp = P.bank(0, BF16, "tp")
    gu = [P.bank(1 + i, F32, f"gu{i}").re("p (a b) -> p a b", a=2) for i in range(2)]
    od = [[P.bank(3 + 2 * u + n, F32, f"od{u}{n}") for n in range(2)] for u in range(2)]

    def load(t):
        s = t % 2
        for u in range(2):
            rows = slice(t * TT + u * 128, t * TT + (u + 1) * 128)
            P.dma("sp", xs[s][u], xin[rows, :], ds_x[s])
            P.dma("pool", xb[s][u], xin[rows, :], ds_x[s])

    wl = {"gu": 0, "d": 0}

    def compute(t):
        s = t % 2
        for u in range(2):
            for k in range(8):
                P.tr(tp[:, k * 128:(k + 1) * 128], xb[s][u][:, k * 128:(k + 1) * 128], ident)
            P.copy(xT[s][:, :, u * 128:(u + 1) * 128], tp.re("p (k t) -> p k t", k=8), eng="act")
        for f in range(NF):
            if t == 0 and f % 2 == 0:
                c = f // 2
                P.wdma(wgu_chunks[c], wgu_src[:, :, c * 512:(c + 1) * 512])
            wch = wgu_chunks[f // 2]
            g = gu[f % 2]
            for half in range(2):
                for k in range(8):
                    c0 = (f % 2) * 256 + half * 128
                    P.mm(g[:, half, :], wch[:, k, c0:c0 + 128], xT[s][:, k, :], start=(k == 0), stop=(k == 7))
            sgt = sg[f % 2]
            P.act(sgt, g[:, 0, :], AF.Silu)
            P.tt(hT[:, f, :], sgt, g[:, 1, :], ALU.mult)
        for u in range(2):
            for n in range(2):
                for f in range(NF):
                    if t == 0 and u == 0 and n == 0 and f % 2 == 0:
                        c = f // 2
                        P.wdma(wd_chunks[c], wd_src[:, c * 2:(c + 1) * 2, :])
                    P.mm(od[u][n], hT[:, f, u * 128:(u + 1) * 128], wd_chunks[f // 2][:, f % 2, n * 512:(n + 1) * 512],
                         start=(f == 0), stop=(f == NF - 1))
        for u in range(2):
            ln_epilogue(P, z[u], xs[s][u], od[u], g_bc, b_bc, ys[u], tmp[u])
            rows = slice(t * TT + u * 128, t * TT + (u + 1) * 128)
            P.dma("sp", yout[rows, :], ys[u], ds_y[u])

    load(0)
    for t in range(NT):
        if t + 1 < NT:
            load(t + 1)
        compute(t)


RMS_EPS = 1e-6
NH = 16
QR = 768
KVR = 256
MAGIC = 12582912.0
TWO_PI = 2.0 * math.pi


def rms_stats(P, srcs, n, eps, stats, mv, rstd):
    k = len(srcs)
    for i, v in enumerate(srcs):
        so, zi = stats[:, i, :].ap, v.ap
        P.op("dve", lambda e, so=so, zi=zi: e.bn_stats(so, zi), [v], [stats])
    mo, si = mv.ap, stats[:, 0:k, :].re("p a b -> p (a b)").ap
    P.op("dve", lambda e: e.bn_aggr(mo, si), [stats], [mv])
    P.stt(rstd, mv[:, 0:1], mv[:, 0:1], mv[:, 1:2], ALU.mult, ALU.add)
    P.ts(rstd, rstd, eps, ALU.add)
    P.act(rstd, rstd, AF.Sqrt)
    ro = rstd.ap
    P.op("dve", lambda e: e.reciprocal(ro, ro), [rstd], [rstd])


def rope(P, dst1, dst2, x1, x2, cos, sin, t):
    P.tt(t[0], x1, cos, ALU.mult)
    P.tt(t[1], x2, sin, ALU.mult)
    P.tt(dst1, t[0], t[1], ALU.subtract)
    P.tt(t[2], x1, sin, ALU.mult)
    P.tt(t[3], x2, cos, ALU.mult)
    P.tt(dst2, t[2], t[3], ALU.add)


def stage_mla_proj(P, S, xin, pos_d, invf_d, win_d, qng_d, wqb_d, kvng_d, wkvb_d, ident_d, qT_d, kT_d, vA_d):
    P.stage_begin()
    NT = S // 128
    ident = P.sb([128, 128], BF16, "ident")
    win = P.sb([128, 8, 1056], BF16, "win")
    wqb = P.sb([128, 6, 1536], BF16, "wqb")
    wkvb = P.sb([128, 2, 2048], BF16, "wkvb")
    qng = P.sb([128, QR], F32, "qng")
    kvng = P.sb([128, KVR], F32, "kvng")
    invf = P.sb([128, 16], F32, "invf")
    posi = P.sb([128, NT], I32, "posi")
    posf = P.sb([128, NT], F32, "posf")
    ang = P.sb([128, NT, 16], F32, "ang")
    tn = P.sb([128, NT, 16], F32, "tn")
    cosT = P.sb([128, NT, 16], F32, "cos")
    sinT = P.sb([128, NT, 16], F32, "sin")
    ds_c = P.dsem("mp_c")
    P.dma("pool", ident, ident_d, ds_c)
    P.dma("sp", qng, qng_d.pbc(128), ds_c)
    P.dma("sp", kvng, kvng_d.pbc(128), ds_c)
    P.dma("sp", invf, invf_d.pbc(128), ds_c)
    P.dma("sp", posi, pos_d, ds_c)
    win_src = win_d.re("(k p) n -> p k n", p=128)
    for k in range(0, 8, 2):
        P.wdma(win[:, k:k + 2, :].sub(f"win{k}"), win_src[:, k:k + 2, :])
    wqb_src = wqb_d.re("(k p) n -> p k n", p=128)
    for k in range(0, 6, 2):
        P.wdma(wqb[:, k:k + 2, :].sub(f"wqb{k}"), wqb_src[:, k:k + 2, :])
    P.wdma(wkvb.sub("wkvb"), wkvb_d.re("(k p) n -> p k n", p=128))
    P.barrier()
    P.copy(posf, posi)
    P.tt(ang, posf.unsq(2).bc([128, NT, 16]), invf.unsq(1).bc([128, NT, 16]), ALU.mult)
    for (dst, shift) in ((sinT, 0.0), (cosT, math.pi / 2)):
        a2 = ang
        if shift:
            P.ts(dst, ang, shift, ALU.add)
            a2 = dst
        P.ts(tn, a2, 1.0 / TWO_PI, ALU.mult, MAGIC, ALU.add)
        P.ts(tn, tn, MAGIC, ALU.subtract)
        P.stt(dst, tn, -TWO_PI, a2, ALU.mult, ALU.add)
        P.ts(dst, dst, 3.14159, ALU.min, -3.14159, ALU.max)
        P.act(dst, dst, AF.Sin)

    xb = [P.sb([128, D], BF16, f"xb{s}") for s in range(2)]
    ds_x = [P.dsem(f"mp_x{s}") for s in range(2)]
    xT = P.sb([128, 8, 128], BF16, "xT")
    cqn = P.sb([128, QR], BF16, "cqn")
    ckvn = P.sb([128, KVR], BF16, "ckvn")
    cqT = P.sb([128, 6, 128], BF16, "cqT")
    ckvT = P.sb([128, 2, 128], BF16, "ckvT")
    Qf = P.sb([128, NH, 96], BF16, "Qf")
    Kf = P.sb([128, NH, 96], BF16, "Kf")
    Va = [P.sb([128, NH, 65], BF16, f"Va{s}") for s in range(2)]
    ds_v = [P.dsem(f"mp_v{s}") for s in range(2)]
    kr = P.sb([128, 32], BF16, "kr")
    rt = [P.sb([128, NH, 16], F32, f"rt{i}") for i in range(4)]
    QTs = [P.sb([96, NH, 512], BF16, f"QTs{s}") for s in range(2)]
    KTs = [P.sb([96, NH, 512], BF16, f"KTs{s}") for s in range(2)]
    ds_q = [P.dsem(f"mp_q{s}") for s in range(2)]
    stats = P.sb([128, 2, 6], F32, "stats")
    mv = P.sb([128, 2], F32, "mv")
    rstd = P.sb([128, 1], F32, "rstd")
    for s in range(2):
        P.memset(Va[s][:, :, 64:65], 1.0)
    tp = P.bank(0, BF16, "tp")
    bk = [None] + [P.bank(i, F32, f"b{i}") for i in range(1, 8)]
    qT_v = qT_d.re("h d s -> d h s")
    kT_v = kT_d.re("h d s -> d h s")

    def load(t):
        P.dma("pool", xb[t % 2], xin[t * 128:(t + 1) * 128, :], ds_x[t % 2])

    def compute(t):
        s = t % 2
        for k in range(8):
            P.tr(tp[:, k * 128:(k + 1) * 128], xb[s][:, k * 128:(k + 1) * 128], ident)
        P.copy(xT, tp.re("p (k t) -> p k t", k=8), eng="act")
        for n, (c0, c1) in enumerate(((0, 512), (512, 1024), (1024, 1056))):
            for k in range(8):
                P.mm(bk[1 + n][:, 0:c1 - c0], xT[:, k, :], win[:, k, c0:c1], start=(k == 0), stop=(k == 7))
        rms_stats(P, [bk[1], bk[2][:, 0:256]], QR, RMS_EPS, stats, mv, rstd)
        P.stt(cqn[:, 0:512], bk[1], rstd[:, 0:1], qng[:, 0:512], ALU.mult, ALU.mult)
        P.stt(cqn[:, 512:768], bk[2][:, 0:256], rstd[:, 0:1], qng[:, 512:768], ALU.mult, ALU.mult)
        rms_stats(P, [bk[2][:, 256:512]], KVR, RMS_EPS, stats, mv, rstd)
        P.stt(ckvn, bk[2][:, 256:512], rstd[:, 0:1], kvng, ALU.mult, ALU.mult)
        cs = cosT[:, t, :]
        sn = sinT[:, t, :]
        kp = bk[3][:, 0:32].re("p (i two) -> p i two", two=2)
        krv = kr.re("p (i two) -> p i two", two=2)
        rope(P, krv[:, :, 0], krv[:, :, 1], kp[:, :, 0], kp[:, :, 1], cs, sn, [r[:, 0, :] for r in rt])
        for k in range(6):
            P.tr(tp[:, k * 128:(k + 1) * 128], cqn[:, k * 128:(k + 1) * 128], ident)
        P.copy(cqT, tp[:, 0:768].re("p (k t) -> p k t", k=6), eng="act")
        for k in range(2):
            P.tr(tp[:, k * 128:(k + 1) * 128], ckvn[:, k * 128:(k + 1) * 128], ident)
        P.copy(ckvT, tp[:, 0:256].re("p (k t) -> p k t", k=2), eng="act")
        for n in range(3):
            for k in range(6):
                P.mm(bk[4 + n], cqT[:, k, :], wqb[:, k, n * 512:(n + 1) * 512], start=(k == 0), stop=(k == 5))
        kvb = [bk[1], bk[2], bk[3], bk[7]]
        for n in range(4):
            for k in range(2):
                P.mm(kvb[n], ckvT[:, k, :], wkvb[:, k, n * 512:(n + 1) * 512], start=(k == 0), stop=(k == 1))
        for n in range(2):
            P.copy(Qf[:, n * 8:(n + 1) * 8, 0:64], bk[4 + n].re("p (h d) -> p h d", h=8), eng="act")
        qp = bk[6].re("p (h i two) -> p h i two", h=NH, two=2)
        qd = Qf[:, :, 64:96].re("p h (i two) -> p h i two", two=2)
        csb = cs.unsq(1).bc([128, NH, 16])
        snb = sn.unsq(1).bc([128, NH, 16])
        rope(P, qd[:, :, :, 0], qd[:, :, :, 1], qp[:, :, :, 0], qp[:, :, :, 1], csb, snb, rt)
        for n in range(4):
            kv4 = kvb[n].re("p (h d) -> p h d", h=4)
            P.copy(Kf[:, n * 4:(n + 1) * 4, 0:64], kv4[:, :, 0:64], eng="act")
            P.copy(Va[s][:, n * 4:(n + 1) * 4, 0:64], kv4[:, :, 64:128])
        P.copy(Kf[:, :, 64:96], kr.unsq(1).bc([128, NH, 32]))
        w = (t % 4) * 128
        slot = (t // 4) % 2
        for (src, dst) in ((Qf, QTs[slot]), (Kf, KTs[slot])):
            for half in range(2):
                for hh in range(8):
                    h = half * 8 + hh
                    P.tr(tp[0:96, hh * 128:(hh + 1) * 128], src[:, h, :], ident)
                P.copy(dst[:, half * 8:(half + 1) * 8, w:w + 128], tp[0:96, :].re("p (h t) -> p h t", h=8), eng="act")
        P.dma("sp", vA_d[t * 128:(t + 1) * 128, :], Va[s].re("p h d -> p (h d)"), ds_v[s])
        if t % 4 == 3:
            win0 = (t // 4) * 512
            P.dma("sp", qT_v[:, :, win0:win0 + 512], QTs[slot], ds_q[slot])
            P.dma("sp", kT_v[:, :, win0:win0 + 512], KTs[slot], ds_q[slot])

    load(0)
    for t in range(NT):
        if t + 1 < NT:
            load(t + 1)
        compute(t)


def stage_attn(P, S, qT_d, kT_d, vA_d, oT_d, ones_d):
    P.stage_begin()
    NJ = S // 128
    NG = S // 512
    scale = 96.0 ** -0.5
    Vall = P.sb([128, NJ, NH * 65], BF16, "Vall")
    ones = P.sb([128, 64], F32, "ones")
    ds_c = P.dsem("at_c")
    P.dma("sp", ones, ones_d, ds_c)
    vsrc = vA_d.re("(j p) c -> p j c", p=128)
    step = max(1, NJ // 4)
    for j0 in range(0, NJ, step):
        P.dma("sp", Vall[:, j0:j0 + step, :], vsrc[:, j0:j0 + step, :], ds_c)
    KT = [P.sb([96, S], BF16, f"KT{s}") for s in range(2)]
    QT = [P.sb([96, S], BF16, f"QT{s}") for s in range(2)]
    ds_h = [P.dsem(f"at_h{s}") for s in range(2)]
    OT = [P.sb([64, S], BF16, f"OT{s}") for s in range(2)]
    ds_o = [P.dsem(f"at_o{s}") for s in range(2)]
    PT = [P.sb([128, 512], BF16, f"PT{i}") for i in range(4)]
    osb = [P.sb([65, 512], F32, f"osb{i}") for i in range(2)]
    sc = [P.bank(i, F32, f"sc{i}") for i in range(4)]
    oacc = [P.bank(4 + i, F32, f"oacc{i}") for i in range(2)]
    bcp = P.bank(6, F32, "bcp")

    def load(h):
        P.dma("sp", KT[h % 2], kT_d[h], ds_h[h % 2])
        P.dma("sp", QT[h % 2], qT_d[h], ds_h[h % 2])

    items = [(h, g, j) for h in range(NH) for g in range(NG) for j in range(4 * g + 4)]
    LOOK = 2
    pending = []

    def qk(idx):
        h, g, j = items[idx]
        if g == 0 and j == 0 and h + 1 < NH:
            load(h + 1)
        m = j - 4 * g
        c0 = max(0, m) * 128
        n = 512 - c0
        P.mm(sc[idx % 4][:, 0:n], KT[h % 2][:, j * 128:(j + 1) * 128], QT[h % 2][:, g * 512 + c0:(g + 1) * 512])

    def pv(idx):
        h, g, j = items[idx]
        nj = 4 * g + 4
        m = j - 4 * g
        c0 = max(0, m) * 128
        n = 512 - c0
        scb, pt, oa = sc[idx % 4], PT[idx % 4], oacc[g % 2]
        P.act(pt[:, 0:n], scb[:, 0:n], AF.Exp, scale=scale)
        if m >= 0:
            P.memset(pt[64:128, 0:64], 0.0, eng="pool")
        P.mm(oa[0:65, c0:512], Vall[:, j, h * 65:(h + 1) * 65], pt[:, 0:n], start=(j == 0), stop=(j == nj - 1))
        if j == nj - 1:
            ob = osb[g % 2]
            ot = OT[h % 2]
            P.copy(ob, oa[0:65, :], eng="act")
            ro = ob[64:65, :].ap
            P.op("dve", lambda e, ro=ro: e.reciprocal(ro, ro), [ob], [ob])

            def fin(h=h, g=g, ob=ob, ot=ot):
                P.mm(bcp[0:64, :], ones[64:65, 0:64], ob[64:65, :])
                P.tt(ot[:, g * 512:(g + 1) * 512], ob[0:64, :], bcp[0:64, :], ALU.mult)
                if g == NG - 1:
                    P.dma("sp", oT_d[h], ot, ds_o[h % 2])
            pending.append([idx + 8, fin])

    load(0)
    n_it = len(items)
    for idx in range(n_it + LOOK):
        if idx < n_it:
            qk(idx)
        if idx - LOOK >= 0:
            pv(idx - LOOK)
        while pending and pending[0][0] <= idx - LOOK:
            pending.pop(0)[1]()
    while pending:
        pending.pop(0)[1]()


def stage_mla_out(P, S, oT_d, xin, wout_d, lng_d, lnb_d, yout):
    P.stage_begin()
    NW = S // 512
    wout = P.sb([64, NH, D], BF16, "wout")
    g_bc = P.sb([128, D], F32, "g_bc")
    b_bc = P.sb([128, D], F32, "b_bc")
    ds_c = P.dsem("mo_c")
    P.dma("sp", g_bc, lng_d.pbc(128), ds_c)
    P.dma("sp", b_bc, lnb_d.pbc(128), ds_c)
    P.wdma(wout, wout_d.re("(h d) n -> d h n", d=64))
    OTw = [P.sb([64, NH, 512], BF16, f"OTw{s}") for s in range(2)]
    ds_w = [P.dsem(f"mo_w{s}") for s in range(2)]
    xs = [P.sb([128, D], F32, f"xs{s}") for s in range(2)]
    ds_x = [P.dsem(f"mo_x{s}") for s in range(2)]
    z = [P.sb([128, D], F32, f"z{s}") for s in range(2)]
    ys = [P.sb([128, D], F32, f"ys{s}") for s in range(2)]
    ds_y = [P.dsem(f"mo_y{s}") for s in range(2)]
    tmp = [dict(stats=P.sb([128, 2, 6], F32), mv=P.sb([128, 2], F32), rstd=P.sb([128, 1], F32)) for _ in range(2)]
    od = [[P.bank(2 * u + n, F32, f"od{u}{n}") for n in range(2)] for u in range(2)]
    oT_v = oT_d.re("h d s -> d h s")

    def loadw(w):
        P.dma("sp", OTw[w % 2], oT_v[:, :, w * 512:(w + 1) * 512], ds_w[w % 2])

    def loadx(t):
        P.dma("sp", xs[t % 2], xin[t * 128:(t + 1) * 128, :], ds_x[t % 2])

    loadw(0)
    loadx(0)
    for w in range(NW):
        if w + 1 < NW:
            loadw(w + 1)
        for u4 in range(4):
            t = w * 4 + u4
            if t + 1 < S // 128:
                loadx(t + 1)
            u = t % 2
            for n in range(2):
                for h in range(NH):
                    P.mm(od[u][n], OTw[w % 2][:, h, u4 * 128:(u4 + 1) * 128], wout[:, h, n * 512:(n + 1) * 512],
                         start=(h == 0), stop=(h == NH - 1))
            ln_epilogue(P, z[u], xs[t % 2], od[u], g_bc, b_bc, ys[u], tmp[u])
            P.dma("sp", yout[t * 128:(t + 1) * 128, :], ys[u], ds_y[u])


EC_STOP = 99
EC_VAR = ''

EC = math.exp(-0.5)
CB_R, CB_A, CB_B, CB_K, CB_BH, CB_KH, CB_V = [i * 512 for i in range(7)]
CB_GQ, CB_GK, CB_GKH, CB_GV = 3584, 3840, 4096, 4352
WB = 4864
CF_BONUS, CF_G, CF_GS = 0, 512, 1024
WF = 1536


def rsqrt_lnexp(P, out, in_, add=None, mx=None, scale_in=None):
    if scale_in is not None:
        P.ts(out, in_, scale_in, ALU.mult, add if add is not None else 0.0, ALU.add)
    elif mx is not None:
        P.ts(out, in_, mx, ALU.max)
    else:
        P.ts(out, in_, add, ALU.add)
    P.act(out, out, AF.Ln)
    P.act(out, out, AF.Exp, scale=-0.5)


def sigmoid_(P, out, in_, tmp=None):
    P.act(out, in_, AF.Tanh, scale=0.5)
    P.ts(out, out, 0.5, ALU.mult, 0.5, ALU.add)


def stage_even_prep(P, S, xin, c, prepB_d, prepF_d, gam_d):
    P.stage_begin()
    NT = S // 128
    ds_c = P.dsem("ep_c")
    ident = P.sb([128, 128], BF16, "ident")
    P.dma("pool", ident, c["ident"], ds_c)
    win = P.sb([128, 8, 3376], BF16, "win")
    src = c["even_w_in"].re("(k p) n -> p k n", p=128)
    for k in range(8):
        P.wdma(win[:, k:k + 1, :].sub(f"win{k}"), src[:, k:k + 1, :])
    Wwa = P.sb([128, 512], BF16, "Wwa")
    P.wdma(Wwa[0:64, :].sub("w2"), c["rwkv_w2"])
    P.wdma(Wwa[64:128, :].sub("a2"), c["rwkv_a2"])
    G2 = P.sb([128, 2, 512], BF16, "G2")
    P.wdma(G2[:, 0, :].sub("g2a"), c["rwkv_g2"][0:128, :])
    P.wdma(G2[0:32, 1, :].sub("g2b"), c["rwkv_g2"][128:160, :])
    GW = P.sb([48, 256], BF16, "GW")
    P.wdma(GW[32:48, :].sub("gw"), c["gla_gate_w2"])

    def bc(name, n):
        t = P.sb([128, n], F32, name)
        P.dma("sp", t, c[name].pbc(128), ds_c)
        return t
    mu = bc("rwkv_mu", 1824)
    w0 = bc("rwkv_w0", 512)
    a0 = bc("rwkv_a0", 512)
    k_k = bc("rwkv_k_k", 512)
    k_a = bc("rwkv_k_a", 512)
    r_k = bc("rwkv_r_k", 512)
    gate_b = bc("gla_gate_b", 256)

    def cst(name, shape):
        t = P.sb(shape, F32, name)
        P.dma("sp", t, c[name], ds_c)
        return t
    TRI_i = cst("tri_incl", [128, 128])
    TRI_x = cst("tri_excl", [128, 128])
    TRI_r = cst("tri_rev", [128, 128])
    TRIg_i = cst("trig_incl", [128, 128])
    TRIg_r = cst("trig_rev", [128, 128])
    CBk = cst("cb", [128, 2])
    CBg = cst("cbg", [128, 2])
    P.barrier()

    xb = [P.sb([128, 1024], BF16, f"xb{s}") for s in range(2)]
    ds_x = [P.dsem(f"ep_x{s}") for s in range(2)]
    xT = P.sb([128, 8, 128], BF16, "xT")
    Psb = P.sb([128, 3376], F32, "Psb")
    Sh = [P.sb([128, 1824], F32, f"Sh{s}") for s in range(2)]
    ds_sh = [P.dsem(f"ep_sh{s}") for s in range(2)]
    Lin = P.sb([128, 304], BF16, "Lin")
    LT0 = P.sb([128, 128], BF16, "LT0")
    LT1 = P.sb([128, 128], BF16, "LT1")
    LT2 = P.sb([48, 128], BF16, "LT2")
    sigw = P.sb([128, 512], F32, "sigw")
    av = P.sb([128, 512], F32, "a")
    kkn = P.sb([128, 512], F32, "kkn")
    sq = P.sb([128, 512], F32, "sq")
    k2 = P.sb([128, 512], F32, "k2")
    bvec = P.sb([128, 512], F32, "bvec")
    E = P.sb([128, 512], F32, "E")
    ss = P.sb([128, 8], F32, "ss")
    rks = P.sb([128, 8], F32, "rks")
    sp = P.sb([128, 256], F32, "sp")
    Eg = P.sb([128, 256], F32, "Eg")
    OB = [P.sb([128, WB], BF16, f"OB{s}") for s in range(2)]
    OF = [P.sb([128, WF], F32, f"OF{s}") for s in range(2)]
    GM = [P.sb([64, 24], F32, f"GM{s}") for s in range(2)]
    ds_o = [P.dsem(f"ep_o{s}") for s in range(2)]
    ShA = [Sh[i][:, 0:1536].sub(f"ShA{i}") for i in range(2)]
    ShB = [Sh[i][:, 1536:1824].sub(f"ShB{i}") for i in range(2)]
    ds_shb = [P.dsem(f"ep_shb{s}") for s in range(2)]
    xTs = P.sb([128, 8, 128], BF16, "xTs")
    lastc = P.sb([128, 8, 1], BF16, "lastc")
    P.memset(lastc, 0.0)
    tp = P.bank(7, BF16, "tp")
    rb = [P.bank(i, F32, f"rb{i}") for i in range(7)]
    rbi = [0]

    def nb():
        b = rb[rbi[0] % 7]
        rbi[0] += 1
        return b

    def load(t):
        P.dma("pool", xb[t % 2], xin[t * 128:(t + 1) * 128, :], ds_x[t % 2])

    def compute(t):
        s = t % 2
        ob, of, gm = OB[s], OF[s], GM[s]
        for k in range(8):
            P.tr(tp[:, k * 128:(k + 1) * 128], xb[s][:, k * 128:(k + 1) * 128], ident)
        P.copy(xT, tp.re("p (k t) -> p k t", k=8), eng="act")
        P.copy(xTs[:, :, 1:128], xT[:, :, 0:127], eng="pool")
        P.copy(xTs[:, :, 0:1], lastc, eng="pool")
        P.copy(lastc, xT[:, :, 127:128], eng="pool")
        for n in range(7):
            c0, c1 = n * 512, min(3376, (n + 1) * 512)
            b = nb()
            for k in range(8):
                P.mm(b[:, 0:c1 - c0], xT[:, k, :], win[:, k, c0:c1], start=(k == 0), stop=(k == 7))
            P.copy(Psb[:, c0:c1], b[:, 0:c1 - c0], eng=("act" if n % 2 == 0 else "dve"))
        sha, shb = ShA[s], ShB[s]
        for n in range(4):
            c0, c1 = 1536 + n * 512, min(3360, 1536 + (n + 1) * 512)
            b = nb()
            for k in range(8):
                P.mm(b[:, 0:c1 - c0], xTs[:, k, :], win[:, k, c0:c1], start=(k == 0), stop=(k == 7))
            dst = sha[:, n * 512:(n + 1) * 512] if n < 3 else shb
            P.copy(dst, b[:, 0:c1 - c0], eng=("act" if n % 2 == 1 else "dve"))
        for (v_, c0, c1, m0, eng_) in ((shb, 3072, 3360, 1536, "pool"), (sha, 1536, 3072, 0, "dve")):
            pr_ = Psb[:, c0:c1]
            mu_ = mu[:, m0:m0 + (c1 - c0)]
            P.tt(v_, v_, pr_, ALU.subtract, eng=eng_)
            P.tt(v_, v_, mu_, ALU.mult, eng=eng_)
            P.tt(v_, v_, pr_, ALU.add, eng=eng_)
        r, kx, vx = sha[:, 0:512], sha[:, 512:1024], sha[:, 1024:1536]
        wl, al, gl = shb[:, 0:64], shb[:, 64:128], shb[:, 128:288]
        P.act(Lin[:, 0:64], wl, AF.Tanh)
        P.copy(Lin[:, 64:128], al)
        P.act(sq[:, 0:160], gl, AF.Tanh, scale=0.5)
        P.ts(Lin[:, 128:288], sq[:, 0:160], 0.5, ALU.mult, 0.5, ALU.add)
        P.copy(Lin[:, 288:304], Psb[:, 3360:3376])
        for i, (lt, c0, c1) in enumerate(((LT0, 0, 128), (LT1, 128, 256), (LT2, 256, 304))):
            P.tr(tp[0:c1 - c0, i * 128:(i + 1) * 128], Lin[:, c0:c1], ident)
            P.copy(lt, tp[0:c1 - c0, i * 128:(i + 1) * 128], eng="act")
        wpre, apre, lapre, gpre = rb[4], rb[5], rb[6], rb[3]
        cum, cumx, rev, gmb = rb[0], rb[1], rb[2], rb[3]
        cumg, revg = rb[4], rb[5]
        P.mm(wpre, LT0[0:64, :], Wwa[0:64, :])
        P.mm(apre, LT0[64:128, :], Wwa[64:128, :])
        P.mm(lapre[:, 0:256], LT2[32:48, :], GW[32:48, :])
        P.mm(gpre, LT1, G2[:, 0, :], start=True, stop=False)
        P.mm(gpre, LT2[0:32, :], G2[0:32, 1, :], start=False, stop=True)
        P.copy(of[:, CF_G:CF_G + 512], gpre, eng="act")
        gq, gk = Psb[:, 0:256], Psb[:, 256:512]
        gv, gg = Psb[:, 512:1024], Psb[:, 1024:1536]
        gs = of[:, CF_GS:CF_GS + 512]

        def chainA():
            P.tt(sigw, wpre, w0, ALU.add)
            sigmoid_(P, sigw, sigw)
            P.mm(cum, TRI_i, sigw)
            P.mm(cumx, TRI_x, sigw)
            P.mm(rev, TRI_r, sigw)
            for h in range(8):
                P.mm(gmb[0:64, h * 2:(h + 1) * 2], sigw[:, h * 64:(h + 1) * 64], CBk)

        def chainB():
            P.tt(av, apre, a0, ALU.add)
            sigmoid_(P, av, av)
            P.tt(kkn, kx, k_k, ALU.mult)
            P.tt(sq, kkn, kkn, ALU.mult)
            P.reduce(ss, sq.re("p (h k) -> p h k", h=8))
            rsqrt_lnexp(P, ss, ss, mx=1e-24)
            P.tt(kkn.re("p (h k) -> p h k", h=8), kkn.re("p (h k) -> p h k", h=8), ss.unsq(2).bc([128, 8, 64]), ALU.mult)
            P.stt(k2, av, -1.0, k_a, ALU.add, ALU.mult)
            P.stt(k2, k2, 1.0, kx, ALU.add, ALU.mult)
            P.tt(bvec, kkn, av, ALU.mult)
            P.copy(ob[:, CB_V:CB_V + 512], vx, eng="act")

        def chainC():
            P.tt(sp, lapre[:, 0:256], gate_b, ALU.add)
            P.act(sp, sp, AF.Exp, scale=-1.0)
            P.act(sp, sp, AF.Ln, bias=1.0)
            P.mm(cumg[:, 0:256], TRIg_i, sp)
            P.mm(revg[:, 0:256], TRIg_r, sp)
            P.act(Eg, cumg[:, 0:256], AF.Exp)
            P.stt(ob[:, CB_GQ:CB_GQ + 256], gq, 0.125, Eg, ALU.mult, ALU.mult)
            P.act(Eg, cumg[:, 0:256], AF.Exp, scale=-1.0)
            P.tt(ob[:, CB_GK:CB_GK + 256], gk, Eg, ALU.mult)
            P.act(Eg, revg[:, 0:256], AF.Exp)
            P.tt(ob[:, CB_GKH:CB_GKH + 256], gk, Eg, ALU.mult)
            P.copy(ob[:, CB_GV:CB_GV + 512], gv, eng="act")
            sigmoid_(P, gs, gg)
            P.tt(gs, gs, gg, ALU.mult)

        P.interleave([chainA, chainB, chainC])
        for h in range(4):
            P.mm(gmb[0:64, 16 + h * 2:16 + (h + 1) * 2], sp[:, h * 64:(h + 1) * 64], CBg)
        P.act(gm, gmb[0:64, 0:24], AF.Exp)
        P.dma("sp", gam_d[t], gm, ds_o[s])
        P.act(E, cum, AF.Exp)
        P.tt(ob[:, CB_R:CB_R + 512], r, E, ALU.mult)
        P.act(E, cumx, AF.Exp)
        P.stt(ob[:, CB_A:CB_A + 512], kkn, -1.0, E, ALU.mult, ALU.mult)
        P.act(E, cum, AF.Exp, scale=-1.0)
        P.tt(ob[:, CB_B:CB_B + 512], bvec, E, ALU.mult)
        P.tt(ob[:, CB_K:CB_K + 512], k2, E, ALU.mult)
        P.act(E, rev, AF.Exp)
        P.tt(ob[:, CB_BH:CB_BH + 512], bvec, E, ALU.mult)
        P.tt(ob[:, CB_KH:CB_KH + 512], k2, E, ALU.mult)
        P.tt(sq, r, k2, ALU.mult)
        P.tt(sq, sq, r_k, ALU.mult)
        P.reduce(rks, sq.re("p (h k) -> p h k", h=8))
        P.tt(of[:, CF_BONUS:CF_BONUS + 512].re("p (h k) -> p h k", h=8), vx.re("p (h k) -> p h k", h=8),
             rks.unsq(2).bc([128, 8, 64]), ALU.mult)
        rows = slice(t * 128, (t + 1) * 128)
        P.dma("sp", prepB_d[rows, :], ob, ds_o[s])
        P.dma("sp", prepF_d[rows, :], of, ds_o[s])

    load(0)
    for t in range(NT):
        if t + 1 < NT:
            load(t + 1)
        compute(t)


def stage_even_chunk(P, S, c, prepB_d, prepF_d, gam_d, oT_d):
    P.stage_begin()
    NT = S // 128
    ds_c = P.dsem("ec_c")
    ident = P.sb([128, 128], BF16, "ident")
    P.dma("pool", ident, c["ident"], ds_c)

    def cst(name, shape):
        t = P.sb(shape, F32, name)
        P.dma("sp", t, c[name], ds_c)
        return t
    ML_s = cst("ml_strict", [128, 128])
    MU_s = cst("mu_strict", [128, 128])
    MU_i = cst("mu_incl", [128, 128])

    def bc(name, n):
        t = P.sb([128, n], F32, name)
        P.dma("sp", t, c[name].pbc(128), ds_c)
        return t
    ln_g = bc("rwkv_ln_g", 512)
    ln_b = bc("rwkv_ln_b", 512)
    norm_g = bc("gla_norm_g", 128)

    IB = [P.sb([128, WB], BF16, f"IB{s}") for s in range(2)]
    IF = [P.sb([128, WF], F32, f"IF{s}") for s in range(2)]
    GM = [P.sb([64, 24], F32, f"GMi{s}") for s in range(2)]
    ds_i = [P.dsem(f"ec_i{s}") for s in range(2)]

    def fm(name, nh=8):
        return P.sb([64, nh, 128], BF16, name)
    RT, AT, BT, KT = fm("RT"), fm("AT"), fm("BT"), fm("KT")
    RTm = [fm("RTm0"), fm("RTm1")]
    qtT, ktT = fm("qtT", 4), fm("ktT", 4)
    qtTm = [fm("qtTm0", 4), fm("qtTm1", 4)]

    def mat(name, nh=8):
        return P.sb([128, nh, 128], BF16, name)
    Pm = [mat("Pm0"), mat("Pm1")]
    PTm = [mat("PTm0"), mat("PTm1")]
    Qm = [mat("Qm0"), mat("Qm1")]
    AakT, ArbT, ArkT = mat("AakT"), mat("ArbT"), mat("ArkT")
    attnT = mat("attnT", 4)
    WtT = fm("WtT")
    AV = P.sb([128, 8, 64], BF16, "AV")
    Ut = P.sb([128, 8, 64], F32, "Ut")
    Usb = P.sb([128, 8, 64], BF16, "Usb")
    H = P.sb([64, 8, 64], F32, "H")
    Hb = [P.sb([64, 8, 64], BF16, f"Hb{i}") for i in range(3)]
    Sg = P.sb([64, 4, 128], F32, "Sg")
    Sgb = [P.sb([64, 4, 128], BF16, f"Sgb{i}") for i in range(3)]
    Ysb = P.sb([128, 8, 64], F32, "Ysb")
    ysq = P.sb([128, 8, 64], F32, "ysq")
    Osb = P.sb([128, 4, 128], F32, "Osb")
    s1 = P.sb([128, 8], F32, "s1")
    s2 = P.sb([128, 8], F32, "s2")
    MIX = P.sb([128, 1024], BF16, "MIX")
    mixT = [P.sb([128, 8, 128], BF16, f"mixT{s}") for s in range(2)]
    ds_m = [P.dsem(f"ec_m{s}") for s in range(2)]
    BHm = [P.sb([128, 512], BF16, f"BHm{i}") for i in range(2)]
    KHm = [P.sb([128, 512], BF16, f"KHm{i}") for i in range(2)]
    GKHm = [P.sb([128, 256], BF16, f"GKHm{i}") for i in range(2)]
    for tl in RTm + qtTm + BHm + KHm + GKHm + [Usb]:
        P.memset(tl, 0.0)
    P.memset(H, 0.0)
    P.memset(Hb[0], 0.0)
    P.memset(Sg, 0.0)
    P.memset(Sgb[0], 0.0)
    tp = P.bank(7, BF16, "tp")
    Yb = P.bank(6, F32, "Yb")
    Ob = P.bank(5, F32, "Ob")
    rb = [P.bank(i, F32, f"rb{i}") for i in range(5)]
    rbi = [0]

    def nb():
        b = rb[rbi[0] % 5]
        rbi[0] += 1
        return b
    oT_v = oT_d.re("(k two) d s -> (two d) k s", two=2)

    def load(t):
        s = t % 2
        rows = slice(t * 128, (t + 1) * 128)
        P.dma("sp", IB[s], prepB_d[rows, :], ds_i[s])
        P.dma("sp", IF[s], prepF_d[rows, :], ds_i[s])
        P.dma("sp", GM[s], gam_d[t], ds_i[s])

    def transpose_heads(dst, src, c0, nh, dstm=None):
        for h in range(nh):
            P.tr(tp[0:64, h * 128:(h + 1) * 128], src[:, c0 + h * 64:c0 + (h + 1) * 64], ident)
        tv = tp[0:64, 0:nh * 128].re("p (h t) -> p h t", h=nh)
        P.copy(dst, tv, eng="act")
        if dstm is not None and EC_VAR != 'b':
            P.copy(dstm[0][:, :, 0:64], dst[:, :, 0:64], eng="pool")
            P.copy(dstm[1][:, :, 64:128], dst[:, :, 64:128], eng="pool")

    def headmm(dst, lhs, rhs, mask, nh=8):
        for g in range(nh // 4):
            b = nb()
            for hh in range(4):
                h = g * 4 + hh
                P.mm(b[:, hh * 128:(hh + 1) * 128], lhs[:, h, :], rhs[:, h, :])
            bv = b.re("p (h t) -> p h t", h=4)
            if mask is None:
                P.copy(dst[:, g * 4:(g + 1) * 4, :], bv, eng="act")
            else:
                P.tt(dst[:, g * 4:(g + 1) * 4, :], bv, mask.unsq(1).bc([128, 4, 128]), ALU.mult)

    def compute(t):
        s = t % 2
        ib, iff, gm = IB[s], IF[s], GM[s]
        if EC_STOP <= 0:
            return
        transpose_heads(RT, ib, CB_R, 8, RTm)
        transpose_heads(AT, ib, CB_A, 8)
        transpose_heads(BT, ib, CB_B, 8)
        transpose_heads(KT, ib, CB_K, 8)
        if EC_VAR == 'a':
            return
        L0, M0 = PTm[0], Pm[0]
        headmm(L0, AT, BT, ML_s)
        headmm(M0, BT, AT, MU_s)
        headmm(AakT, KT, AT, MU_s)
        headmm(ArbT, BT, RT, MU_i)
        headmm(ArkT, KT, RT, MU_i)
        if EC_STOP <= 1:
            return
        P.tt(Qm[0], M0, ident.unsq(1).bc([128, 8, 128]), ALU.add)
        cur = 0
        for lvl in range(5):
            nxt = 1 - cur
            if lvl < 4:
                headmm(Pm[nxt], PTm[cur], Pm[cur], None)
            headmm(PTm[nxt], Pm[cur], PTm[cur], None)
            for g in range(2):
                b = nb()
                for hh in range(4):
                    h = g * 4 + hh
                    P.mm(b[:, hh * 128:(hh + 1) * 128], PTm[nxt][:, h, :], Qm[cur][:, h, :])
                P.tt(Qm[nxt][:, g * 4:(g + 1) * 4, :], b.re("p (h t) -> p h t", h=4), Qm[cur][:, g * 4:(g + 1) * 4, :], ALU.add)
            cur = nxt
        TT = Qm[cur]
        if EC_STOP <= 2:
            return
        for g in range(2):
            b = nb()
            for hh in range(4):
                h = g * 4 + hh
                P.mm(b[0:64, hh * 128:(hh + 1) * 128], ib[:, CB_A + h * 64:CB_A + (h + 1) * 64], TT[:, h, :])
            P.copy(WtT[:, g * 4:(g + 1) * 4, :], b[0:64, :].re("p (h t) -> p h t", h=4), eng="act")
        b = nb()
        for h in range(8):
            P.mm(b[:, h * 64:(h + 1) * 64], AakT[:, h, :], ib[:, CB_V + h * 64:CB_V + (h + 1) * 64])
        P.copy(AV, b.re("p (h v) -> p h v", h=8), eng="act")
        b = nb()
        for h in range(8):
            P.mm(b[:, h * 64:(h + 1) * 64], TT[:, h, :], AV[:, h, :])
        P.copy(Ut, b.re("p (h v) -> p h v", h=8), eng="act")
        if EC_STOP <= 3:
            return
        transpose_heads(qtT, ib, CB_GQ, 4, qtTm)
        transpose_heads(ktT, ib, CB_GK, 4)
        headmm(attnT, ktT, qtT, MU_i, nh=4)
        if EC_STOP <= 4:
            return
        for cc in range(2):
            rs = slice(cc * 64, (cc + 1) * 64)
            P.copy(BHm[cc][rs, :], ib[rs, CB_BH:CB_BH + 512], eng="pool")
            P.copy(KHm[cc][rs, :], ib[rs, CB_KH:CB_KH + 512], eng="pool")
            P.copy(GKHm[cc][rs, :], ib[rs, CB_GKH:CB_GKH + 256], eng="pool")
        for cc in range(2):
            rs = slice(cc * 64, (cc + 1) * 64)
            hb_in, hb_out = Hb[(2 * t + cc) % 3], Hb[(2 * t + cc + 1) % 3]
            sb_in, sb_out = Sgb[(2 * t + cc) % 3], Sgb[(2 * t + cc + 1) % 3]
            up = nb()
            for h in range(8):
                P.mm(up[:, h * 64:(h + 1) * 64], WtT[:, h, :], hb_in[:, h, :])
            P.tt(Usb[rs, :, :], up[rs, :].re("p (h v) -> p h v", h=8), Ut[rs, :, :], ALU.add)
            dh = nb()
            for h in range(8):
                P.mm(dh[0:64, h * 64:(h + 1) * 64], BHm[cc][:, h * 64:(h + 1) * 64], Usb[:, h, :],
                     start=True, stop=False)
                P.mm(dh[0:64, h * 64:(h + 1) * 64], KHm[cc][:, h * 64:(h + 1) * 64],
                     ib[:, CB_V + h * 64:CB_V + (h + 1) * 64], start=False, stop=True)
            gsel = gm[:, 0:16].re("p (h c) -> p h c", c=2)[:, :, cc]
            P.tt(H, H, gsel.unsq(2).bc([64, 8, 64]), ALU.mult)
            P.tt(H, H, dh[0:64, :].re("p (h v) -> p h v", h=8), ALU.add)
            P.copy(hb_out, H, eng="act")
            ds_ = nb()
            for h in range(4):
                P.mm(ds_[0:64, h * 128:(h + 1) * 128], GKHm[cc][:, h * 64:(h + 1) * 64],
                     ib[:, CB_GV + h * 128:CB_GV + (h + 1) * 128])
            gsel2 = gm[:, 16:24].re("p (h c) -> p h c", c=2)[:, :, cc]
            P.tt(Sg, Sg, gsel2.unsq(2).bc([64, 4, 128]), ALU.mult)
            P.tt(Sg, Sg, ds_[0:64, :].re("p (h v) -> p h v", h=4), ALU.add)
            P.copy(sb_out, Sg, eng="act")
        for h in range(8):
            ys_ = Yb[:, h * 64:(h + 1) * 64]
            P.mm(ys_, RTm[0][:, h, :], Hb[(2 * t) % 3][:, h, :], start=True, stop=False)
            P.mm(ys_, RTm[1][:, h, :], Hb[(2 * t + 1) % 3][:, h, :], start=False, stop=False)
            P.mm(ys_, ArbT[:, h, :], Usb[:, h, :], start=False, stop=False)
            P.mm(ys_, ArkT[:, h, :], ib[:, CB_V + h * 64:CB_V + (h + 1) * 64], start=False, stop=True)
        for h in range(4):
            os_ = Ob[:, h * 128:(h + 1) * 128]
            P.mm(os_, qtTm[0][:, h, :], Sgb[(2 * t) % 3][:, h, :], start=True, stop=False)
            P.mm(os_, qtTm[1][:, h, :], Sgb[(2 * t + 1) % 3][:, h, :], start=False, stop=False)
            P.mm(os_, attnT[:, h, :], ib[:, CB_GV + h * 128:CB_GV + (h + 1) * 128], start=False, stop=True)
        if EC_STOP <= 5:
            return
        P.copy(Ysb, Yb.re("p (h v) -> p h v", h=8), eng="act")
        P.reduce(s1, Ysb)
        P.tt(ysq, Ysb, Ysb, ALU.mult)
        P.reduce(s2, ysq)
        P.ts(s1, s1, 1.0 / 64, ALU.mult)
        P.stt(ysq[:, :, 0], s1, -1.0, s1, ALU.mult, ALU.mult)
        P.stt(s2, s2, 1.0 / 64, ysq[:, :, 0], ALU.mult, ALU.add)
        _r = rsqrt_lnexp
        _r(P, s2, s2, add=64e-5)
        P.tt(Ysb, Ysb, s1.unsq(2).bc([128, 8, 64]), ALU.subtract)
        P.tt(Ysb, Ysb, s2.unsq(2).bc([128, 8, 64]), ALU.mult)
        yf = Ysb.re("p h v -> p (h v)")
        P.tt(yf, yf, ln_g, ALU.mult)
        P.tt(yf, yf, ln_b, ALU.add)
        P.tt(yf, yf, iff[:, CF_BONUS:CF_BONUS + 512], ALU.add)
        P.tt(MIX[:, 512:1024], yf, iff[:, CF_G:CF_G + 512], ALU.mult)
        if EC_STOP <= 6:
            return
        P.copy(Osb, Ob.re("p (h v) -> p h v", h=4), eng="act")
        osq = ysq.re("p h v -> p (h v)").re("p (h v) -> p h v", h=4)
        P.tt(osq, Osb, Osb, ALU.mult)
        P.reduce(s1[:, 0:4], osq)
        _r(P, s1[:, 0:4], s1[:, 0:4], scale_in=1.0 / 128, add=1e-6)
        P.tt(Osb, Osb, s1[:, 0:4].unsq(2).bc([128, 4, 128]), ALU.mult)
        P.tt(Osb, Osb, norm_g.unsq(1).bc([128, 4, 128]), ALU.mult)
        P.tt(MIX[:, 0:512], Osb.re("p h v -> p (h v)"), iff[:, CF_GS:CF_GS + 512], ALU.mult)
        if EC_STOP <= 7:
            return
        mt = mixT[s]
        for k in range(8):
            P.tr(tp[:, k * 128:(k + 1) * 128], MIX[:, k * 128:(k + 1) * 128], ident)
        P.copy(mt, tp.re("p (k t) -> p k t", k=8), eng="act")
        P.dma("sp", oT_v[:, :, t * 128:(t + 1) * 128], mt, ds_m[s])

    load(0)
    for t in range(NT):
        if t + 1 < NT:
            load(t + 1)
        compute(t)


NCORES = 8


def host_consts():
    c = {}
    c["ident"] = np.eye(128, dtype=np.float32)
    idx = np.arange(128)
    same = (idx[:, None] // 64) == (idx[None, :] // 64)
    lt = idx[:, None] < idx[None, :]
    le = idx[:, None] <= idx[None, :]
    gt = idx[:, None] > idx[None, :]
    EC = math.exp(-0.5)
    c["tri_incl"] = (-EC * (same & le)).astype(np.float32)
    c["tri_excl"] = (-EC * (same & lt)).astype(np.float32)
    c["tri_rev"] = (-EC * (same & gt)).astype(np.float32)
    c["trig_incl"] = (-(1.0 / 16) * (same & le)).astype(np.float32)
    c["trig_rev"] = (-(1.0 / 16) * (same & gt)).astype(np.float32)
    cb = np.zeros((128, 2), np.float32)
    cb[0:64, 0] = 1
    cb[64:128, 1] = 1
    c["cb"] = (-EC * cb).astype(np.float32)
    c["cbg"] = (-(1.0 / 16) * cb).astype(np.float32)
    c["ml_strict"] = (same & gt).astype(np.float32)
    c["mu_strict"] = (same & lt).astype(np.float32)
    c["mu_incl"] = (same & le).astype(np.float32)
    c["invf"] = (10000.0 ** (-np.arange(0, 32, 2, dtype=np.float32) / 32)).astype(np.float32)
    c["ones"] = np.ones((128, 64), np.float32)
    return c


CONST_SHAPES = dict(ident=[128, 128], tri_incl=[128, 128], tri_excl=[128, 128], tri_rev=[128, 128],
                    trig_incl=[128, 128], trig_rev=[128, 128], cb=[128, 2], cbg=[128, 2],
                    ml_strict=[128, 128], mu_strict=[128, 128], mu_incl=[128, 128], invf=[16], ones=[128, 64])

WEIGHT_SHAPES = dict(
    even_w_in=[1024, 3376], gla_gate_w2=[16, 256], gla_gate_b=[256], gla_norm_g=[128], rwkv_mu=[1824],
    rwkv_w0=[512], rwkv_w2=[64, 512], rwkv_a0=[512], rwkv_w2_=None, rwkv_a2=[64, 512], rwkv_g2=[160, 512],
    rwkv_k_k=[512], rwkv_k_a=[512], rwkv_r_k=[512], rwkv_ln_g=[512], rwkv_ln_b=[512], even_w_out=[1024, 1024],
    mla_w_in=[1024, 1056], mla_q_norm_g=[768], mla_w_q_b=[768, 1536], mla_kv_norm_g=[256], mla_w_kv_b=[256, 2048],
    mla_w_out=[1024, 1024], ffn_gu0=[1024, 5632], ffn_gu1=[1024, 5632], ffn_d0=[2816, 1024], ffn_d1=[2816, 1024],
    ln_g00=[1024], ln_g01=[1024], ln_g10=[1024], ln_g11=[1024], ln_b00=[1024], ln_b01=[1024], ln_b10=[1024], ln_b11=[1024])
del WEIGHT_SHAPES["rwkv_w2_"]


def host_weights(inp):
    w = {}
    f = lambda a: np.ascontiguousarray(np.asarray(a, dtype=np.float32))
    ew = np.asarray(inp["even_w_in"][0])
    o = 1552
    perm = np.concatenate([np.arange(0, 1536), o + np.arange(0, 1824), np.arange(1536, 1552)])
    w["even_w_in"] = f(ew[:, perm])
    for k in ("gla_gate_w2", "gla_gate_b", "gla_norm_g", "rwkv_mu", "rwkv_w0", "rwkv_w2", "rwkv_a0", "rwkv_a2",
              "rwkv_g2", "rwkv_k_k", "rwkv_k_a", "rwkv_ln_g", "rwkv_ln_b", "even_w_out", "mla_w_in",
              "mla_q_norm_g", "mla_kv_norm_g", "mla_w_kv_b", "mla_w_out"):
        w[k] = f(inp[k][0])
    w["rwkv_r_k"] = f(np.asarray(inp["rwkv_r_k"][0]).reshape(512))
    wq = np.asarray(inp["mla_w_q_b"][0]).reshape(768, 16, 96)
    w["mla_w_q_b"] = f(np.concatenate([wq[:, :, 0:64].reshape(768, 1024), wq[:, :, 64:96].reshape(768, 512)], 1))
    for i in range(2):
        gu = np.asarray(inp["ffn_w_gate_up"][i])
        w[f"ffn_gu{i}"] = f(gu.reshape(1024, 2, 22, 128).transpose(0, 2, 1, 3).reshape(1024, 5632))
        w[f"ffn_d{i}"] = f(inp["ffn_w_down"][i])
        for j in range(2):
            w[f"ln_g{i}{j}"] = f(inp["ln_g"][i, j])
            w[f"ln_b{i}{j}"] = f(inp["ln_b"][i, j])
    return w


def build(S, debug=False, stages=("ep", "ec", "eo", "f0", "mp", "at", "mo", "f1")):
    nc = bass.Bass("TRN2", target_bir_lowering=False)
    with ExitStack() as st:
        P = Prog(nc, st)
        P.init_mem()
        kind_dbg = "ExternalOutput" if debug else "Internal"
        x = P.dram("x", [S, 1024], F32, kind="ExternalInput")
        pos = P.dram("pos", [128, S // 128], I32, kind="ExternalInput")
        c = {k: P.dram(k, v, F32, kind="ExternalInput") for k, v in CONST_SHAPES.items()}
        w = {k: P.dram(k, v, F32, kind="ExternalInput") for k, v in WEIGHT_SHAPES.items()}
        c.update(w)
        prepB = P.dram("prepB", [S, WB], BF16)
        prepF = P.dram("prepF", [S, WF], F32)
        gam = P.dram("gam", [S // 128, 64, 24], F32)
        oT = P.dram("oT", [16, 64, S], BF16)
        oT2 = P.dram("oT2", [16, 64, S], BF16)
        qT = P.dram("qT", [16, 96, S], BF16)
        kT = P.dram("kT", [16, 96, S], BF16)
        vA = P.dram("vA", [S, 16 * 65], BF16)
        x1 = P.dram("x1", [S, 1024], F32, kind=kind_dbg if "eo" in stages else "ExternalInput")
        x2 = P.dram("x2", [S, 1024], F32, kind=kind_dbg if "f0" in stages else "ExternalInput")
        x3 = P.dram("x3", [S, 1024], F32, kind=kind_dbg if "mo" in stages else "ExternalInput")
        out = P.dram("out", [S, 1024], F32, kind="ExternalOutput")
        if "ep" in stages:
            stage_even_prep(P, S, x, c, prepB, prepF, gam)
        if "ec" in stages:
            stage_even_chunk(P, S, c, prepB, prepF, gam, oT)
        if "eo" in stages:
            stage_mla_out(P, S, oT, x, c["even_w_out"], c["ln_g00"], c["ln_b00"], x1)
        if "f0" in stages:
            stage_ffn(P, S, x1, c["ffn_gu0"], c["ffn_d0"], c["ln_g01"], c["ln_b01"], x2, c["ident"])
        if "mp" in stages:
            stage_mla_proj(P, S, x2, pos, c["invf"], c["mla_w_in"], c["mla_q_norm_g"], c["mla_w_q_b"],
                           c["mla_kv_norm_g"], c["mla_w_kv_b"], c["ident"], qT, kT, vA)
        if "at" in stages:
            stage_attn(P, S, qT, kT, vA, oT2, c["ones"])
        if "mo" in stages:
            stage_mla_out(P, S, oT2, x2, c["mla_w_out"], c["ln_g10"], c["ln_b10"], x3)
        if "f1" in stages:
            stage_ffn(P, S, x3, c["ffn_gu1"], c["ffn_d1"], c["ln_g11"], c["ln_b11"], out, c["ident"])
        P.barrier()
        P.emit()
        print({e: len(P.ins[e]) for e in ENGS}, flush=True)
    return nc


_NC_CACHE = {}


def kernel(**inp):
    x = np.asarray(inp["x"], dtype=np.float32)
    B, S, _ = x.shape
    pos = np.asarray(inp["positions"]).astype(np.int32)
    consts = host_consts()
    wts = host_weights(inp)
    if S not in _NC_CACHE:
        _NC_CACHE[S] = build(S)
    nc = _NC_CACHE[S]
    in_maps = []
    for b in range(B):
        m = dict(x=np.ascontiguousarray(x[b]),
                 pos=np.ascontiguousarray(pos[b].reshape(S // 128, 128).T))
        m.update(consts)
        m.update(wts)
        in_maps.append(m)
    res = run_bass_kernel_spmd(nc, in_maps, core_ids=list(range(B)))
    return np.stack([np.asarray(r["out"], dtype=np.float32) for r in res.results], 0)
```

```python
import math
import threading
from contextlib import ExitStack
import threading
import numpy as np
import concourse.bass as bass
import concourse.mybir as mybir
from concourse.bass_utils import run_bass_kernel_spmd

F32 = mybir.dt.float32
BF16 = mybir.dt.bfloat16
I32 = mybir.dt.int32
ALU = mybir.AluOpType
AF = mybir.ActivationFunctionType
AX = mybir.AxisListType

ENGS = ("pe", "act", "dve", "pool", "sp")


class Buf:
    __slots__ = ("name", "writes", "reads")

    def __init__(self, name=""):
        self.name = name
        self.writes = []
        self.reads = []


class V:
    __slots__ = ("ap", "buf")

    def __init__(self, ap, buf):
        self.ap = ap
        self.buf = buf

    def __getitem__(self, idx):
        return V(self.ap[idx], self.buf)

    def re(self, pattern, **kw):
        return V(self.ap.rearrange(pattern, **kw), self.buf)

    def sub(self, name=""):
        return V(self.ap, Buf(name))

    def bc(self, shape):
        return V(self.ap.to_broadcast(list(shape)), self.buf)

    def unsq(self, axis):
        return V(self.ap.unsqueeze(axis), self.buf)

    def pbc(self, n):
        return V(self.ap.partition_broadcast(n), self.buf)

    def bitcast(self, dt):
        return V(self.ap.bitcast(dt), self.buf)


class DmaPart:
    def __init__(self, sem, name):
        self.sem = sem
        self.total = 0
        self.name = name


class DmaSem:
    def __init__(self, prog, name):
        self.prog = prog
        self.name = name
        self.parts = {}

    def part(self, eng):
        k = "sw" if eng == "pool" else "hw"
        if k not in self.parts:
            p = self.prog
            sem = p.stack.enter_context(p.nc.semaphore(f"ds{len(p.dsems)}_{k}_{self.name}"))
            self.parts[k] = DmaPart(sem, self.name + k)
            p.dsems.append(self.parts[k])
        return self.parts[k]

    @property
    def total(self):
        return sum(x.total for x in self.parts.values())


class Prog:
    def __init__(self, nc, stack):
        self.nc = nc
        self.stack = stack
        self.ins = {e: [] for e in ENGS}
        self.waited = {e: {} for e in ENGS}
        self.esem = {e: stack.enter_context(nc.semaphore("es_" + e)) for e in ENGS}
        self.dsems = []
        self.ntile = 0
        self._il_step = None

    ARENA = 212800

    def init_mem(self):
        nc = self.nc
        self.arena = self.stack.enter_context(nc.sbuf_tensor("arena", [128, self.ARENA], mybir.dt.uint8))
        self.banks = [self.stack.enter_context(nc.psum_tensor(f"bank{i}", [128, 512], F32)) for i in range(8)]
        self.off = 0

    def stage_begin(self):
        self.barrier()
        self.off = 0

    def sb(self, shape, dtype, name=""):
        esz = {F32: 4, BF16: 2, I32: 4}[dtype]
        n = 1
        for d in shape[1:]:
            n *= d
        nb = (n * esz + 31) // 32 * 32
        assert self.off + nb <= self.ARENA, f"SBUF arena overflow: {self.off}+{nb} ({name})"
        ap = self.arena[0:shape[0], self.off:self.off + n * esz].bitcast(dtype)
        self.off += nb
        if len(shape) == 3:
            ap = ap.rearrange("p (a b) -> p a b", a=shape[1])
        elif len(shape) == 4:
            ap = ap.rearrange("p (a b c) -> p a b c", a=shape[1], b=shape[2])
        return V(ap, Buf(name))

    def bank(self, i, dtype=F32, name=""):
        ap = self.banks[i][:, :]
        if dtype != F32:
            ap = ap.bitcast(dtype)
        return V(ap, Buf(name or f"bank{i}"))

    def dram(self, name, shape, dtype, kind="Internal"):
        t = self.nc.dram_tensor(name, list(shape), dtype, kind=kind)
        return V(t.ap(), None)

    def dsem(self, name):
        return DmaSem(self, name)

    def _deps(self, eng, reads, writes):
        toks = []
        for v in reads:
            b = v.buf if isinstance(v, V) else v
            if b is None:
                continue
            toks.extend((t, True) for t in b.writes)
        for v in writes:
            b = v.buf if isinstance(v, V) else v
            if b is None:
                continue
            toks.extend((t, False) for t in b.writes)
            toks.extend((t, False) for t in b.reads)
        waits = []
        wd = self.waited[eng]
        for tk, raw in toks:
            if tk[0] == "e":
                _, f, idx = tk
                if f == eng and (eng == "pe" or not raw):
                    continue
                if wd.get(("e", f), -1) >= idx:
                    continue
                wd[("e", f)] = idx
                self.ins[f][idx]["signal"] = True
                waits.append(tk)
            else:
                _, ds = tk
                if wd.get(("d", id(ds)), -1) >= ds.total:
                    continue
                wd[("d", id(ds))] = ds.total
                waits.append(("d", ds, ds.total))
        return waits

    def _commit(self, tok, reads, writes):
        for v in reads:
            b = v.buf if isinstance(v, V) else v
            if b is None:
                continue
            if tok[0] == "e":
                b.reads = [t for t in b.reads if not (t[0] == "e" and t[1] == tok[1])]
            elif tok in b.reads:
                continue
            b.reads.append(tok)
        for v in writes:
            b = v.buf if isinstance(v, V) else v
            if b is None:
                continue
            b.writes = [tok]
            b.reads = []

    def op(self, eng, fn, reads=(), writes=()):
        waits = self._deps(eng, reads, writes)
        idx = len(self.ins[eng])
        self.ins[eng].append(dict(fn=fn, waits=waits, signal=False, dsem=None))
        self._commit(("e", eng, idx), reads, writes)
        if self._il_step is not None:
            self._il_step()
        return idx

    def dma(self, eng, out, in_, ds, serial=False, **kw):
        ds = ds.part(eng)
        waits = self._deps(eng, [in_], [out])
        if serial and ds.total and self.waited[eng].get(("d", id(ds)), -1) < ds.total:
            self.waited[eng][("d", id(ds))] = ds.total
            waits.append(("d", ds, ds.total))
        idx = len(self.ins[eng])
        ds.total += 16
        oa, ia = out.ap, in_.ap
        self.ins[eng].append(dict(fn=lambda e: e.dma_start(out=oa, in_=ia, **kw), waits=waits,
                                  signal=False, dsem=ds))
        self._commit(("d", ds), [in_], [out])
        if self._il_step is not None:
            self._il_step()

    def interleave(self, fns):
        n = len(fns)
        import os
        if n == 1 or os.environ.get('NO_IL'):
            for f in fns:
                f()
            return
        cv = threading.Condition()
        st = {"turn": 0, "alive": [True] * n, "err": None}
        loc = threading.local()

        def nxt(i):
            for d in range(1, n + 1):
                j = (i + d) % n
                if st["alive"][j]:
                    return j
            return None

        def step():
            i = loc.idx
            with cv:
                j = nxt(i)
                if j is None or j == i:
                    return
                st["turn"] = j
                cv.notify_all()
                while st["turn"] != i:
                    cv.wait()

        def worker(i):
            loc.idx = i
            with cv:
                while st["turn"] != i:
                    cv.wait()
            try:
                fns[i]()
            except BaseException as e:
                st["err"] = e
            with cv:
                st["alive"][i] = False
                st["turn"] = nxt(i)
                cv.notify_all()

        self._il_step = step
        ths = [threading.Thread(target=worker, args=(i,)) for i in range(n)]
        for t in ths:
            t.start()
        for t in ths:
            t.join()
        self._il_step = None
        if st["err"] is not None:
            raise st["err"]

    def wdma(self, out, in_, eng="pool"):
        if not hasattr(self, "wsems"):
            self.wsems = [self.dsem(f"w{i}") for i in range(6)]
            self.wcnt = 0
        ds = self.wsems[self.wcnt % len(self.wsems)]
        self.wcnt += 1
        self.dma(eng, out, in_, ds, serial=True)

    def barrier(self):
        for e in ENGS:
            waits = []
            wd = self.waited[e]
            for f in ENGS:
                if f == e or not self.ins[f]:
                    continue
                idx = None
                for k in range(len(self.ins[f]) - 1, -1, -1):
                    if self.ins[f][k]["dsem"] is None and self.ins[f][k]["fn"] is not None:
                        idx = k
                        break
                if idx is None or wd.get(("e", f), -1) >= idx:
                    continue
                wd[("e", f)] = idx
                self.ins[f][idx]["signal"] = True
                waits.append(("e", f, idx))
            for ds in self.dsems:
                if ds.total and wd.get(("d", id(ds)), -1) < ds.total:
                    wd[("d", id(ds))] = ds.total
                    waits.append(("d", ds, ds.total))
            if waits:
                self.ins[e].append(dict(fn=None, waits=waits, signal=False, dsem=None))

    def emit(self):
        nc = self.nc
        rank = {}
        for e in ENGS:
            c = 0
            r = []
            for rec in self.ins[e]:
                if rec["signal"]:
                    c += 1
                r.append(c)
            rank[e] = r
        print("signals", {e: (rank[e][-1] if rank[e] else 0) for e in ENGS}, "dma_max", max([d.total for d in self.dsems] + [0]), flush=True)
        handles = {"pe": "tensor", "act": "scalar", "dve": "vector", "pool": "gpsimd", "sp": "sync"}

        def run(e, eng):
            for rec in self.ins[e]:
                for w in rec["waits"]:
                    if w[0] == "e":
                        eng.wait_ge(self.esem[w[1]], rank[w[1]][w[2]])
                    else:
                        eng.wait_ge(w[1].sem, w[2])
                if rec["fn"] is None:
                    continue
                ins = rec["fn"](eng)
                if rec["dsem"] is not None:
                    ins.then_inc(rec["dsem"].sem, 16)
                elif rec["signal"]:
                    ins.then_inc(self.esem[e], 1)

        with nc.Block() as block:
            for e in ENGS:
                if not self.ins[e]:
                    continue
                getattr(block, handles[e])(lambda eng, e=e: run(e, eng))

    def mm(self, out, lhsT, rhs, start=True, stop=True):
        o, l, r = out.ap, lhsT.ap, rhs.ap
        return self.op("pe", lambda e: e.matmul(o, l, r, start=start, stop=stop), [lhsT, rhs], [out])

    def tr(self, out, in_, ident):
        o, i, d = out.ap, in_.ap, ident.ap
        return self.op("pe", lambda e: e.transpose(o, i, d), [in_, ident], [out])

    def act(self, out, in_, func, bias=None, scale=None, accum=None, eng="act"):
        o, i = out.ap, in_.ap
        kw = {}
        rd = [in_]
        wr = [out]
        if bias is not None:
            if isinstance(bias, V):
                kw["bias"] = bias.ap
                rd.append(bias)
            else:
                kw["bias"] = bias
        if scale is not None:
            if isinstance(scale, V):
                kw["scale"] = scale.ap
                rd.append(scale)
            else:
                kw["scale"] = scale
        if accum is not None:
            kw["accum_out"] = accum.ap
            wr.append(accum)
        return self.op("act", lambda e: e.activation(o, i, func, **kw), rd, wr)

    def tt(self, out, a, b, op, eng="dve"):
        o, x, y = out.ap, a.ap, b.ap
        return self.op(eng, lambda e: e.tensor_tensor(o, x, y, op), [a, b], [out])

    def ts(self, out, a, s1, op0, s2=None, op1=None, eng="dve", accum=None):
        o, x = out.ap, a.ap
        rd = [a]
        wr = [out]
        a1 = s1
        if isinstance(s1, V):
            rd.append(s1)
            a1 = s1.ap
        a2 = s2
        if isinstance(s2, V):
            rd.append(s2)
            a2 = s2.ap
        kw = {}
        if op1 is not None:
            kw["op1"] = op1
        if accum is not None:
            kw["accum_out"] = accum.ap
            wr.append(accum)
        return self.op(eng, lambda e: e.tensor_scalar(o, x, a1, a2, op0, **kw), rd, wr)

    def stt(self, out, a, s, b, op0, op1, eng="dve"):
        o, x, y = out.ap, a.ap, b.ap
        rd = [a, b]
        sc = s
        if isinstance(s, V):
            rd.append(s)
            sc = s.ap
        return self.op(eng, lambda e: e.scalar_tensor_tensor(o, x, sc, y, op0, op1), rd, [out])

    def copy(self, out, in_, eng="dve"):
        o, i = out.ap, in_.ap
        if eng == "act":
            return self.op("act", lambda e: e.copy(o, i), [in_], [out])
        return self.op(eng, lambda e: e.tensor_copy(o, i), [in_], [out])

    def memset(self, out, val, eng="dve"):
        o = out.ap
        return self.op(eng, lambda e: e.memset(o, val), [], [out])

    def reduce(self, out, in_, op=None, axis=None, eng="dve"):
        o, i = out.ap, in_.ap
        op = op or ALU.add
        axis = axis or AX.X
        return self.op(eng, lambda e: e.tensor_reduce(o, i, axis, op), [in_], [out])


DN_ALPHA = 4.0 ** 0.25
LN_EPS = 1e-5
D = 1024
FH = 2816
NF = 22


def ln_epilogue(P, z, xs, ods, g_bc, b_bc, ys, tmp):
    for n in range(2):
        sl = slice(n * 512, (n + 1) * 512)
        P.stt(z[:, sl], xs[:, sl], DN_ALPHA, ods[n], ALU.mult, ALU.add)
    ln_core(P, z, g_bc, b_bc, ys, tmp)


def ln_core(P, z, g_bc, b_bc, ys, tmp):
    stats, mv, rstd = tmp["stats"], tmp["mv"], tmp["rstd"]
    for n in range(2):
        sl = slice(n * 512, (n + 1) * 512)
        so, zi = stats[:, n, :].ap, z[:, sl].ap
        P.op("dve", lambda e, so=so, zi=zi: e.bn_stats(so, zi), [z], [stats])
    mo, si = mv.ap, stats.re("p a b -> p (a b)").ap
    P.op("dve", lambda e: e.bn_aggr(mo, si), [stats], [mv])
    P.ts(rstd, mv[:, 1:2], LN_EPS, ALU.add)
    P.act(rstd, rstd, AF.Sqrt)
    ro = rstd.ap
    P.op("dve", lambda e: e.reciprocal(ro, ro), [rstd], [rstd])
    P.ts(z, z, mv[:, 0:1], ALU.subtract, rstd[:, 0:1], ALU.mult)
    P.tt(z, z, g_bc, ALU.mult, eng="pool")
    P.tt(ys, z, b_bc, ALU.add, eng="pool")


def stage_ffn(P, S, xin, wgu_d, wd_d, lng_d, lnb_d, yout, ident_d):
    P.stage_begin()
    TT = 256
    NT = S // TT
    ident = P.sb([128, 128], BF16, "ident")
    wgu = P.sb([128, 8, NF * 256], BF16, "wgu")
    wd = P.sb([128, NF, D], BF16, "wd")
    g_bc = P.sb([128, D], F32, "g_bc")
    b_bc = P.sb([128, D], F32, "b_bc")
    ds_c = P.dsem("ffn_c")
    P.dma("pool", ident, ident_d, ds_c)
    P.dma("sp", g_bc, lng_d.pbc(128), ds_c)
    P.dma("sp", b_bc, lnb_d.pbc(128), ds_c)
    wgu_chunks = [wgu[:, :, f * 512:(f + 1) * 512].sub(f"wgu{f}") for f in range(NF // 2)]
    wd_chunks = [wd[:, f * 2:(f + 1) * 2, :].sub(f"wd{f}") for f in range(NF // 2)]
    wgu_src = wgu_d.re("(k p) n -> p k n", p=128)
    wd_src = wd_d.re("(f p) n -> p f n", p=128)

    xs = [[P.sb([128, D], F32, f"xs{s}{u}") for u in range(2)] for s in range(2)]
    xb = [[P.sb([128, D], BF16, f"xb{s}{u}") for u in range(2)] for s in range(2)]
    ds_x = [P.dsem(f"ffn_x{s}") for s in range(2)]
    xT = [P.sb([128, 8, TT], BF16, f"xT{s}") for s in range(2)]
    hT = P.sb([128, NF, TT], BF16, "hT")
    sg = [P.sb([128, TT], F32, f"sg{s}") for s in range(2)]
    z = [P.sb([128, D], F32, f"z{s}") for s in range(2)]
    ys = [P.sb([128, D], F32, f"ys{s}") for s in range(2)]
    ds_y = [P.dsem(f"ffn_y{s}") for s in range(2)]
    tmp = [dict(stats=P.sb([128, 2, 6], F32), mv=P.sb([128, 2], F32), rstd=P.sb([128, 1], F32)) for _ in range(2)]
    tp = P.bank(0, BF16, "tp")
    gu = [P.bank(1 + i, F32, f"gu{i}").re("p (a b) -> p a b", a=2) for i in range(2)]
    od = [[P.bank(3 + 2 * u + n, F32, f"od{u}{n}") for n in range(2)] for u in range(2)]

    def load(t):
        s = t % 2
        for u in range(2):
            rows = slice(t * TT + u * 128, t * TT + (u + 1) * 128)
            P.dma("sp", xs[s][u], xin[rows, :], ds_x[s])
            P.dma("pool", xb[s][u], xin[rows, :], ds_x[s])

    wl = {"gu": 0, "d": 0}

    def compute(t):
        s = t % 2
        for u in range(2):
            for k in range(8):
                P.tr(tp[:, k * 128:(k + 1) * 128], xb[s][u][:, k * 128:(k + 1) * 128], ident)
            P.copy(xT[s][:, :, u * 128:(u + 1) * 128], tp.re("p (k t) -> p k t", k=8), eng="act")
        for f in range(NF):
            if t == 0 and f % 2 == 0:
                c = f // 2
                P.wdma(wgu_chunks[c], wgu_src[:, :, c * 512:(c + 1) * 512])
            wch = wgu_chunks[f // 2]
            g = gu[f % 2]
            for half in range(2):
                for k in range(8):
                    c0 = (f % 2) * 256 + half * 128
                    P.mm(g[:, half, :], wch[:, k, c0:c0 + 128], xT[s][:, k, :], start=(k == 0), stop=(k == 7))
            sgt = sg[f % 2]
            P.act(sgt, g[:, 0, :], AF.Silu)
            P.tt(hT[:, f, :], sgt, g[:, 1, :], ALU.mult)
        for u in range(2):
            for n in range(2):
                for f in range(NF):
                    if t == 0 and u == 0 and n == 0 and f % 2 == 0:
                        c = f // 2
                        P.wdma(wd_chunks[c], wd_src[:, c * 2:(c + 1) * 2, :])
                    P.mm(od[u][n], hT[:, f, u * 128:(u + 1) * 128], wd_chunks[f // 2][:, f % 2, n * 512:(n + 1) * 512],
                         start=(f == 0), stop=(f == NF - 1))
        for u in range(2):
            ln_epilogue(P, z[u], xs[s][u], od[u], g_bc, b_bc, ys[u], tmp[u])
            rows = slice(t * TT + u * 128, t * TT + (u + 1) * 128)
            P.dma("sp", yout[rows, :], ys[u], ds_y[u])

    load(0)
    for t in range(NT):
        if t + 1 < NT:
            load(t + 1)
        compute(t)


RMS_EPS = 1e-6
NH = 16
QR = 768
KVR = 256
MAGIC = 12582912.0
TWO_PI = 2.0 * math.pi


def rms_stats(P, srcs, n, eps, stats, mv, rstd):
    k = len(srcs)
    for i, v in enumerate(srcs):
        so, zi = stats[:, i, :].ap, v.ap
        P.op("dve", lambda e, so=so, zi=zi: e.bn_stats(so, zi), [v], [stats])
    mo, si = mv.ap, stats[:, 0:k, :].re("p a b -> p (a b)").ap
    P.op("dve", lambda e: e.bn_aggr(mo, si), [stats], [mv])
    P.stt(rstd, mv[:, 0:1], mv[:, 0:1], mv[:, 1:2], ALU.mult, ALU.add)
    P.ts(rstd, rstd, eps, ALU.add)
    P.act(rstd, rstd, AF.Sqrt)
    ro = rstd.ap
    P.op("dve", lambda e: e.reciprocal(ro, ro), [rstd], [rstd])


def rope(P, dst1, dst2, x1, x2, cos, sin, t):
    P.tt(t[0], x1, cos, ALU.mult)
    P.tt(t[1], x2, sin, ALU.mult)
    P.tt(dst1, t[0], t[1], ALU.subtract)
    P.tt(t[2], x1, sin, ALU.mult)
    P.tt(t[3], x2, cos, ALU.mult)
    P.tt(dst2, t[2], t[3], ALU.add)


def stage_mla_proj(P, S, xin, pos_d, invf_d, win_d, qng_d, wqb_d, kvng_d, wkvb_d, ident_d, qT_d, kT_d, vA_d):
    P.stage_begin()
    NT = S // 128
    ident = P.sb([128, 128], BF16, "ident")
    win = P.sb([128, 8, 1056], BF16, "win")
    wqb = P.sb([128, 6, 1536], BF16, "wqb")
    wkvb = P.sb([128, 2, 2048], BF16, "wkvb")
    qng = P.sb([128, QR], F32, "qng")
    kvng = P.sb([128, KVR], F32, "kvng")
    invf = P.sb([128, 16], F32, "invf")
    posi = P.sb([128, NT], I32, "posi")
    posf = P.sb([128, NT], F32, "posf")
    ang = P.sb([128, NT, 16], F32, "ang")
    tn = P.sb([128, NT, 16], F32, "tn")
    cosT = P.sb([128, NT, 16], F32, "cos")
    sinT = P.sb([128, NT, 16], F32, "sin")
    ds_c = P.dsem("mp_c")
    P.dma("pool", ident, ident_d, ds_c)
    P.dma("sp", qng, qng_d.pbc(128), ds_c)
    P.dma("sp", kvng, kvng_d.pbc(128), ds_c)
    P.dma("sp", invf, invf_d.pbc(128), ds_c)
    P.dma("sp", posi, pos_d, ds_c)
    win_src = win_d.re("(k p) n -> p k n", p=128)
    wqb_src = wqb_d.re("(k p) n -> p k n", p=128)
    winc = [win[:, k:k + 2, :].sub(f"win{k}") for k in range(0, 8, 2)]
    wqbc = [wqb[:, k:k + 2, :].sub(f"wqb{k}") for k in range(0, 6, 2)]
    wkvbv = wkvb.sub("wkvb")
    P.barrier()
    for i, k in enumerate(range(0, 8, 2)):
        P.wdma(winc[i], win_src[:, k:k + 2, :])
    for i, k in enumerate(range(0, 6, 2)):
        P.wdma(wqbc[i], wqb_src[:, k:k + 2, :])
    P.wdma(wkvbv, wkvb_d.re("(k p) n -> p k n", p=128))
    P.copy(posf, posi)
    P.tt(ang, posf.unsq(2).bc([128, NT, 16]), invf.unsq(1).bc([128, NT, 16]), ALU.mult)
    for (dst, shift) in ((sinT, 0.0), (cosT, math.pi / 2)):
        a2 = ang
        if shift:
            P.ts(dst, ang, shift, ALU.add)
            a2 = dst
        P.ts(tn, a2, 1.0 / TWO_PI, ALU.mult, MAGIC, ALU.add)
        P.ts(tn, tn, MAGIC, ALU.subtract)
        P.stt(dst, tn, -TWO_PI, a2, ALU.mult, ALU.add)
        P.ts(dst, dst, 3.14159, ALU.min, -3.14159, ALU.max)
        P.act(dst, dst, AF.Sin)

    xb = [P.sb([128, D], BF16, f"xb{s}") for s in range(2)]
    ds_x = [P.dsem(f"mp_x{s}") for s in range(2)]
    xT = P.sb([128, 8, 128], BF16, "xT")
    cqn = P.sb([128, QR], BF16, "cqn")
    ckvn = P.sb([128, KVR], BF16, "ckvn")
    cqT = P.sb([128, 6, 128], BF16, "cqT")
    ckvT = P.sb([128, 2, 128], BF16, "ckvT")
    Qf = P.sb([128, NH, 96], BF16, "Qf")
    Kf = P.sb([128, NH, 96], BF16, "Kf")
    Va = [P.sb([128, NH, 65], BF16, f"Va{s}") for s in range(2)]
    ds_v = [P.dsem(f"mp_v{s}") for s in range(2)]
    kr = P.sb([128, 32], BF16, "kr")
    rt = [P.sb([128, NH, 16], F32, f"rt{i}") for i in range(4)]
    QTs = [P.sb([96, NH, 512], BF16, f"QTs{s}") for s in range(2)]
    KTs = [P.sb([96, NH, 512], BF16, f"KTs{s}") for s in range(2)]
    ds_q = [P.dsem(f"mp_q{s}") for s in range(2)]
    stats = P.sb([128, 2, 6], F32, "stats")
    mv = P.sb([128, 2], F32, "mv")
    rstd = P.sb([128, 1], F32, "rstd")
    for s in range(2):
        P.memset(Va[s][:, :, 64:65], 1.0)
    tp = P.bank(0, BF16, "tp")
    bk = [None] + [P.bank(i, F32, f"b{i}") for i in range(1, 8)]
    qT_v = qT_d.re("h d s -> d h s")
    kT_v = kT_d.re("h d s -> d h s")

    def load(t):
        P.dma("pool", xb[t % 2], xin[t * 128:(t + 1) * 128, :], ds_x[t % 2])

    def compute(t):
        s = t % 2
        for k in range(8):
            P.tr(tp[:, k * 128:(k + 1) * 128], xb[s][:, k * 128:(k + 1) * 128], ident)
        P.copy(xT, tp.re("p (k t) -> p k t", k=8), eng="act")
        for n, (c0, c1) in enumerate(((0, 512), (512, 1024), (1024, 1056))):
            for k in range(8):
                P.mm(bk[1 + n][:, 0:c1 - c0], xT[:, k, :], winc[k // 2][:, k % 2, c0:c1], start=(k == 0), stop=(k == 7))
        rms_stats(P, [bk[1], bk[2][:, 0:256]], QR, RMS_EPS, stats, mv, rstd)
        P.stt(cqn[:, 0:512], bk[1], rstd[:, 0:1], qng[:, 0:512], ALU.mult, ALU.mult)
        P.stt(cqn[:, 512:768], bk[2][:, 0:256], rstd[:, 0:1], qng[:, 512:768], ALU.mult, ALU.mult)
        rms_stats(P, [bk[2][:, 256:512]], KVR, RMS_EPS, stats, mv, rstd)
        P.stt(ckvn, bk[2][:, 256:512], rstd[:, 0:1], kvng, ALU.mult, ALU.mult)
        cs = cosT[:, t, :]
        sn = sinT[:, t, :]
        kp = bk[3][:, 0:32].re("p (i two) -> p i two", two=2)
        krv = kr.re("p (i two) -> p i two", two=2)
        rope(P, krv[:, :, 0], krv[:, :, 1], kp[:, :, 0], kp[:, :, 1], cs, sn, [r[:, 0, :] for r in rt])
        for k in range(6):
            P.tr(tp[:, k * 128:(k + 1) * 128], cqn[:, k * 128:(k + 1) * 128], ident)
        P.copy(cqT, tp[:, 0:768].re("p (k t) -> p k t", k=6), eng="act")
        for k in range(2):
            P.tr(tp[:, k * 128:(k + 1) * 128], ckvn[:, k * 128:(k + 1) * 128], ident)
        P.copy(ckvT, tp[:, 0:256].re("p (k t) -> p k t", k=2), eng="act")
        for n in range(3):
            for k in range(6):
                P.mm(bk[4 + n], cqT[:, k, :], wqbc[k // 2][:, k % 2, n * 512:(n + 1) * 512], start=(k == 0), stop=(k == 5))
        kvb = [bk[1], bk[2], bk[3], bk[7]]
        for n in range(4):
            for k in range(2):
                P.mm(kvb[n], ckvT[:, k, :], wkvbv[:, k, n * 512:(n + 1) * 512], start=(k == 0), stop=(k == 1))
        for n in range(2):
            P.copy(Qf[:, n * 8:(n + 1) * 8, 0:64], bk[4 + n].re("p (h d) -> p h d", h=8), eng="act")
        qp = bk[6].re("p (h i two) -> p h i two", h=NH, two=2)
        qd = Qf[:, :, 64:96].re("p h (i two) -> p h i two", two=2)
        csb = cs.unsq(1).bc([128, NH, 16])
        snb = sn.unsq(1).bc([128, NH, 16])
        rope(P, qd[:, :, :, 0], qd[:, :, :, 1], qp[:, :, :, 0], qp[:, :, :, 1], csb, snb, rt)
        for n in range(4):
            kv4 = kvb[n].re("p (h d) -> p h d", h=4)
            P.copy(Kf[:, n * 4:(n + 1) * 4, 0:64], kv4[:, :, 0:64], eng="act")
            P.copy(Va[s][:, n * 4:(n + 1) * 4, 0:64], kv4[:, :, 64:128])
        P.copy(Kf[:, :, 64:96], kr.unsq(1).bc([128, NH, 32]))
        w = (t % 4) * 128
        slot = (t // 4) % 2
        for (src, dst) in ((Qf, QTs[slot]), (Kf, KTs[slot])):
            for half in range(2):
                for hh in range(8):
                    h = half * 8 + hh
                    P.tr(tp[0:96, hh * 128:(hh + 1) * 128], src[:, h, :], ident)
                P.copy(dst[:, half * 8:(half + 1) * 8, w:w + 128], tp[0:96, :].re("p (h t) -> p h t", h=8), eng="act")
        P.dma("sp", vA_d[t * 128:(t + 1) * 128, :], Va[s].re("p h d -> p (h d)"), ds_v[s])
        if t % 4 == 3:
            win0 = (t // 4) * 512
            P.dma("sp", qT_v[:, :, win0:win0 + 512], QTs[slot], ds_q[slot])
            P.dma("sp", kT_v[:, :, win0:win0 + 512], KTs[slot], ds_q[slot])

    load(0)
    for t in range(NT):
        if t + 1 < NT:
            load(t + 1)
        compute(t)


def stage_attn(P, S, qT_d, kT_d, vA_d, oT_d, ones_d):
    P.stage_begin()
    NJ = S // 128
    NG = S // 512
    scale = 96.0 ** -0.5
    Vall = P.sb([128, NJ, NH * 65], BF16, "Vall")
    ones = P.sb([128, 64], F32, "ones")
    ds_c = P.dsem("at_c")
    P.dma("sp", ones, ones_d, ds_c)
    vsrc = vA_d.re("(j p) c -> p j c", p=128)
    step = max(1, NJ // 4)
    for j0 in range(0, NJ, step):
        P.dma("sp", Vall[:, j0:j0 + step, :], vsrc[:, j0:j0 + step, :], ds_c)
    KT = [P.sb([96, S], BF16, f"KT{s}") for s in range(2)]
    QT = [P.sb([96, S], BF16, f"QT{s}") for s in range(2)]
    ds_h = [P.dsem(f"at_h{s}") for s in range(2)]
    OT = [P.sb([64, S], BF16, f"OT{s}") for s in range(2)]
    ds_o = [P.dsem(f"at_o{s}") for s in range(2)]
    PT = [P.sb([128, 512], BF16, f"PT{i}") for i in range(4)]
    osb = [P.sb([65, 512], F32, f"osb{i}") for i in range(2)]
    sc = [P.bank(i, F32, f"sc{i}") for i in range(4)]
    oacc = [P.bank(4 + i, F32, f"oacc{i}") for i in range(2)]
    bcp = P.bank(6, F32, "bcp")

    def load(h):
        P.dma("sp", KT[h % 2], kT_d[h], ds_h[h % 2])
        P.dma("sp", QT[h % 2], qT_d[h], ds_h[h % 2])

    items = [(h, g, j) for h in range(NH) for g in range(NG) for j in range(4 * g + 4)]
    LOOK = 2
    pending = []

    def qk(idx):
        h, g, j = items[idx]
        if g == 0 and j == 0 and h + 1 < NH:
            load(h + 1)
        m = j - 4 * g
        c0 = max(0, m) * 128
        n = 512 - c0
        P.mm(sc[idx % 4][:, 0:n], KT[h % 2][:, j * 128:(j + 1) * 128], QT[h % 2][:, g * 512 + c0:(g + 1) * 512])

    def pv(idx):
        h, g, j = items[idx]
        nj = 4 * g + 4
        m = j - 4 * g
        c0 = max(0, m) * 128
        n = 512 - c0
        scb, pt, oa = sc[idx % 4], PT[idx % 4], oacc[g % 2]
        P.act(pt[:, 0:n], scb[:, 0:n], AF.Exp, scale=scale)
        if m >= 0:
            P.memset(pt[64:128, 0:64], 0.0, eng="pool")
        P.mm(oa[0:65, c0:512], Vall[:, j, h * 65:(h + 1) * 65], pt[:, 0:n], start=(j == 0), stop=(j == nj - 1))
        if j == nj - 1:
            ob = osb[g % 2]
            ot = OT[h % 2]
            P.copy(ob, oa[0:65, :])
            ro = ob[64:65, :].ap
            P.op("dve", lambda e, ro=ro: e.reciprocal(ro, ro), [ob], [ob])

            def fin(h=h, g=g, ob=ob, ot=ot):
                P.mm(bcp[0:64, :], ones[64:65, 0:64], ob[64:65, :])
                P.tt(ot[:, g * 512:(g + 1) * 512], ob[0:64, :], bcp[0:64, :], ALU.mult)
                if g == NG - 1:
                    P.dma("sp", oT_d[h], ot, ds_o[h % 2])
            pending.append([idx + 8, fin])

    load(0)
    n_it = len(items)
    for idx in range(n_it + LOOK):
        if idx < n_it:
            qk(idx)
        if idx - LOOK >= 0:
            pv(idx - LOOK)
        while pending and pending[0][0] <= idx - LOOK:
            pending.pop(0)[1]()
    while pending:
        pending.pop(0)[1]()


def stage_mla_out(P, S, oT_d, xin, wout_d, lng_d, lnb_d, yout):
    P.stage_begin()
    NW = S // 512
    wout = P.sb([64, NH, D], BF16, "wout")
    g_bc = P.sb([128, D], F32, "g_bc")
    b_bc = P.sb([128, D], F32, "b_bc")
    ds_c = P.dsem("mo_c")
    P.dma("sp", g_bc, lng_d.pbc(128), ds_c)
    P.dma("sp", b_bc, lnb_d.pbc(128), ds_c)
    P.wdma(wout, wout_d.re("(h d) n -> d h n", d=64))
    OTw = [P.sb([64, NH, 512], BF16, f"OTw{s}") for s in range(2)]
    ds_w = [P.dsem(f"mo_w{s}") for s in range(2)]
    xs = [P.sb([128, D], F32, f"xs{s}") for s in range(2)]
    ds_x = [P.dsem(f"mo_x{s}") for s in range(2)]
    z = [P.sb([128, D], F32, f"z{s}") for s in range(2)]
    ys = [P.sb([128, D], F32, f"ys{s}") for s in range(2)]
    ds_y = [P.dsem(f"mo_y{s}") for s in range(2)]
    tmp = [dict(stats=P.sb([128, 2, 6], F32), mv=P.sb([128, 2], F32), rstd=P.sb([128, 1], F32)) for _ in range(2)]
    od = [[P.bank(2 * u + n, F32, f"od{u}{n}") for n in range(2)] for u in range(2)]
    oT_v = oT_d.re("h d s -> d h s")

    def loadw(w):
        P.dma("sp", OTw[w % 2], oT_v[:, :, w * 512:(w + 1) * 512], ds_w[w % 2])

    def loadx(t):
        P.dma("sp", xs[t % 2], xin[t * 128:(t + 1) * 128, :], ds_x[t % 2])

    loadw(0)
    loadx(0)
    for w in range(NW):
        if w + 1 < NW:
            loadw(w + 1)
        for u4 in range(4):
            t = w * 4 + u4
            if t + 1 < S // 128:
                loadx(t + 1)
            u = t % 2
            for n in range(2):
                for h in range(NH):
                    P.mm(od[u][n], OTw[w % 2][:, h, u4 * 128:(u4 + 1) * 128], wout[:, h, n * 512:(n + 1) * 512],
                         start=(h == 0), stop=(h == NH - 1))
            ln_epilogue(P, z[u], xs[t % 2], od[u], g_bc, b_bc, ys[u], tmp[u])
            P.dma("sp", yout[t * 128:(t + 1) * 128, :], ys[u], ds_y[u])


EC_STOP = 99
EC_VAR = ''

EC = math.exp(-0.5)
CB_R, CB_A, CB_B, CB_K, CB_BH, CB_KH, CB_V = [i * 512 for i in range(7)]
CB_GQ, CB_GK, CB_GKH, CB_GV = 3584, 3840, 4096, 4352
WB = 4864
CF_BONUS, CF_G, CF_GS = 0, 512, 1024
WF = 1536


def rsqrt_lnexp(P, out, in_, add=None, mx=None, scale_in=None):
    if scale_in is not None:
        P.ts(out, in_, scale_in, ALU.mult, add if add is not None else 0.0, ALU.add)
    elif mx is not None:
        P.ts(out, in_, mx, ALU.max)
    else:
        P.ts(out, in_, add, ALU.add)
    P.act(out, out, AF.Ln)
    P.act(out, out, AF.Exp, scale=-0.5)


def sigmoid_(P, out, in_, tmp=None):
    P.act(out, in_, AF.Tanh, scale=0.5)
    P.ts(out, out, 0.5, ALU.mult, 0.5, ALU.add)


def stage_even_prep(P, S, xin, c, prepB_d, prepF_d, gam_d):
    P.stage_begin()
    NT = S // 128
    ds_c = P.dsem("ep_c")
    ident = P.sb([128, 128], BF16, "ident")
    P.dma("pool", ident, c["ident"], ds_c)
    win_all = P.sb([128, 8, 3376], BF16, "win")
    src = c["even_w_in"].re("(k p) n -> p k n", p=128)
    wink = [win_all[:, k, :].sub(f"win{k}") for k in range(8)]
    Wwa = P.sb([128, 512], BF16, "Wwa")
    P.wdma(Wwa[0:64, :].sub("w2"), c["rwkv_w2"])
    P.wdma(Wwa[64:128, :].sub("a2"), c["rwkv_a2"])
    G2 = P.sb([128, 2, 512], BF16, "G2")
    P.wdma(G2[:, 0, :].sub("g2a"), c["rwkv_g2"][0:128, :])
    P.wdma(G2[0:32, 1, :].sub("g2b"), c["rwkv_g2"][128:160, :])
    GW = P.sb([48, 256], BF16, "GW")
    P.wdma(GW[32:48, :].sub("gw"), c["gla_gate_w2"])

    def bc(name, n):
        t = P.sb([128, n], F32, name)
        P.dma("sp", t, c[name].pbc(128), ds_c)
        return t
    mu = bc("rwkv_mu", 1824)
    w0 = bc("rwkv_w0", 512)
    a0 = bc("rwkv_a0", 512)
    k_k = bc("rwkv_k_k", 512)
    k_a = bc("rwkv_k_a", 512)
    r_k = bc("rwkv_r_k", 512)
    gate_b = bc("gla_gate_b", 256)

    def cst(name, shape):
        t = P.sb(shape, F32, name)
        P.dma("sp", t, c[name], ds_c)
        return t
    TRI_i = cst("tri_incl", [128, 128])
    TRI_x = cst("tri_excl", [128, 128])
    TRI_r = cst("tri_rev", [128, 128])
    TRIg_i = cst("trig_incl", [128, 128])
    TRIg_r = cst("trig_rev", [128, 128])
    CBk = cst("cb", [128, 2])
    CBg = cst("cbg", [128, 2])
    P.barrier()
    for k in range(8):
        P.wdma(wink[k], src[:, k, :])

    xb = [P.sb([128, 1024], BF16, f"xb{s}") for s in range(2)]
    ds_x = [P.dsem(f"ep_x{s}") for s in range(2)]
    xT = P.sb([128, 8, 128], BF16, "xT")
    Psb = P.sb([128, 3376], F32, "Psb")
    Sh = [P.sb([128, 1824], F32, f"Sh{s}") for s in range(2)]
    ds_sh = [P.dsem(f"ep_sh{s}") for s in range(2)]
    Lin = P.sb([128, 304], BF16, "Lin")
    LT0 = P.sb([128, 128], BF16, "LT0")
    LT1 = P.sb([128, 128], BF16, "LT1")
    LT2 = P.sb([48, 128], BF16, "LT2")
    sigw = P.sb([128, 512], F32, "sigw")
    av = P.sb([128, 512], F32, "a")
    kkn = P.sb([128, 512], F32, "kkn")
    sq = P.sb([128, 512], F32, "sq")
    k2 = P.sb([128, 512], F32, "k2")
    bvec = P.sb([128, 512], F32, "bvec")
    E = P.sb([128, 512], F32, "E")
    ss = P.sb([128, 8], F32, "ss")
    rks = P.sb([128, 8], F32, "rks")
    sp = P.sb([128, 256], F32, "sp")
    Eg = P.sb([128, 256], F32, "Eg")
    OB = [P.sb([128, WB], BF16, f"OB{s}") for s in range(2)]
    OF = [P.sb([128, WF], F32, f"OF{s}") for s in range(2)]
    GM = [P.sb([64, 24], F32, f"GM{s}") for s in range(2)]
    ds_o = [P.dsem(f"ep_o{s}") for s in range(2)]
    ShA = [Sh[i][:, 0:1536].sub(f"ShA{i}") for i in range(2)]
    ShB = [Sh[i][:, 1536:1824].sub(f"ShB{i}") for i in range(2)]
    ds_shb = [P.dsem(f"ep_shb{s}") for s in range(2)]
    xTs = P.sb([128, 8, 128], BF16, "xTs")
    lastc = P.sb([128, 8, 1], BF16, "lastc")
    P.memset(lastc, 0.0)
    tp = P.bank(7, BF16, "tp")
    rb = [P.bank(i, F32, f"rb{i}") for i in range(7)]
    rbi = [0]

    def nb():
        b = rb[rbi[0] % 7]
        rbi[0] += 1
        return b

    def load(t):
        P.dma("pool", xb[t % 2], xin[t * 128:(t + 1) * 128, :], ds_x[t % 2])

    def compute(t):
        s = t % 2
        ob, of, gm = OB[s], OF[s], GM[s]
        for k in range(8):
            P.tr(tp[:, k * 128:(k + 1) * 128], xb[s][:, k * 128:(k + 1) * 128], ident)
        P.copy(xT, tp.re("p (k t) -> p k t", k=8), eng="act")
        P.copy(xTs[:, :, 1:128], xT[:, :, 0:127], eng="pool")
        P.copy(xTs[:, :, 0:1], lastc, eng="pool")
        P.copy(lastc, xT[:, :, 127:128], eng="pool")
        for n in range(7):
            c0, c1 = n * 512, min(3376, (n + 1) * 512)
            b = nb()
            for k in range(8):
                P.mm(b[:, 0:c1 - c0], xT[:, k, :], wink[k][:, c0:c1], start=(k == 0), stop=(k == 7))
            P.copy(Psb[:, c0:c1], b[:, 0:c1 - c0], eng=("act" if n % 2 == 0 else "dve"))
        sha, shb = ShA[s], ShB[s]
        for n in range(4):
            c0, c1 = 1536 + n * 512, min(3360, 1536 + (n + 1) * 512)
            b = nb()
            for k in range(8):
                P.mm(b[:, 0:c1 - c0], xTs[:, k, :], wink[k][:, c0:c1], start=(k == 0), stop=(k == 7))
            dst = sha[:, n * 512:(n + 1) * 512] if n < 3 else shb
            P.copy(dst, b[:, 0:c1 - c0], eng=("act" if n % 2 == 1 else "dve"))
        for (v_, c0, c1, m0, eng_) in ((shb, 3072, 3360, 1536, "pool"), (sha, 1536, 3072, 0, "dve")):
            pr_ = Psb[:, c0:c1]
            mu_ = mu[:, m0:m0 + (c1 - c0)]
            P.tt(v_, v_, pr_, ALU.subtract, eng=eng_)
            P.tt(v_, v_, mu_, ALU.mult, eng=eng_)
            P.tt(v_, v_, pr_, ALU.add, eng=eng_)
        r, kx, vx = sha[:, 0:512], sha[:, 512:1024], sha[:, 1024:1536]
        wl, al, gl = shb[:, 0:64], shb[:, 64:128], shb[:, 128:288]
        P.act(Lin[:, 0:64], wl, AF.Tanh)
        P.copy(Lin[:, 64:128], al)
        P.act(sq[:, 0:160], gl, AF.Tanh, scale=0.5)
        P.ts(Lin[:, 128:288], sq[:, 0:160], 0.5, ALU.mult, 0.5, ALU.add)
        P.copy(Lin[:, 288:304], Psb[:, 3360:3376])
        for i, (lt, c0, c1) in enumerate(((LT0, 0, 128), (LT1, 128, 256), (LT2, 256, 304))):
            P.tr(tp[0:c1 - c0, i * 128:(i + 1) * 128], Lin[:, c0:c1], ident)
            P.copy(lt, tp[0:c1 - c0, i * 128:(i + 1) * 128], eng="act")
        wpre, apre, lapre, gpre = rb[4], rb[5], rb[6], rb[3]
        cum, cumx, rev, gmb = rb[0], rb[1], rb[2], rb[3]
        cumg, revg = rb[4], rb[5]
        P.mm(wpre, LT0[0:64, :], Wwa[0:64, :])
        P.mm(apre, LT0[64:128, :], Wwa[64:128, :])
        P.mm(lapre[:, 0:256], LT2[32:48, :], GW[32:48, :])
        P.mm(gpre, LT1, G2[:, 0, :], start=True, stop=False)
        P.mm(gpre, LT2[0:32, :], G2[0:32, 1, :], start=False, stop=True)
        P.copy(of[:, CF_G:CF_G + 512], gpre, eng="act")
        gq, gk = Psb[:, 0:256], Psb[:, 256:512]
        gv, gg = Psb[:, 512:1024], Psb[:, 1024:1536]
        gs = of[:, CF_GS:CF_GS + 512]

        def chainA():
            P.tt(sigw, wpre, w0, ALU.add)
            sigmoid_(P, sigw, sigw)
            P.mm(cum, TRI_i, sigw)
            P.mm(cumx, TRI_x, sigw)
            P.mm(rev, TRI_r, sigw)
            for h in range(8):
                P.mm(gmb[0:64, h * 2:(h + 1) * 2], sigw[:, h * 64:(h + 1) * 64], CBk)

        def chainB():
            P.tt(av, apre, a0, ALU.add)
            sigmoid_(P, av, av)
            P.tt(kkn, kx, k_k, ALU.mult)
            P.tt(sq, kkn, kkn, ALU.mult)
            P.reduce(ss, sq.re("p (h k) -> p h k", h=8))
            rsqrt_lnexp(P, ss, ss, mx=1e-24)
            P.tt(kkn.re("p (h k) -> p h k", h=8), kkn.re("p (h k) -> p h k", h=8), ss.unsq(2).bc([128, 8, 64]), ALU.mult)
            P.stt(k2, av, -1.0, k_a, ALU.add, ALU.mult)
            P.stt(k2, k2, 1.0, kx, ALU.add, ALU.mult)
            P.tt(bvec, kkn, av, ALU.mult)
            P.copy(ob[:, CB_V:CB_V + 512], vx, eng="act")

        def chainC():
            P.tt(sp, lapre[:, 0:256], gate_b, ALU.add)
            P.act(sp, sp, AF.Exp, scale=-1.0)
            P.act(sp, sp, AF.Ln, bias=1.0)
            P.mm(cumg[:, 0:256], TRIg_i, sp)
            P.mm(revg[:, 0:256], TRIg_r, sp)
            P.act(Eg, cumg[:, 0:256], AF.Exp)
            P.stt(ob[:, CB_GQ:CB_GQ + 256], gq, 0.125, Eg, ALU.mult, ALU.mult)
            P.act(Eg, cumg[:, 0:256], AF.Exp, scale=-1.0)
            P.tt(ob[:, CB_GK:CB_GK + 256], gk, Eg, ALU.mult)
            P.act(Eg, revg[:, 0:256], AF.Exp)
            P.tt(ob[:, CB_GKH:CB_GKH + 256], gk, Eg, ALU.mult)
            P.copy(ob[:, CB_GV:CB_GV + 512], gv, eng="act")
            sigmoid_(P, gs, gg)
            P.tt(gs, gs, gg, ALU.mult)

        P.interleave([chainA, chainB, chainC])
        for h in range(4):
            P.mm(gmb[0:64, 16 + h * 2:16 + (h + 1) * 2], sp[:, h * 64:(h + 1) * 64], CBg)
        P.act(gm, gmb[0:64, 0:24], AF.Exp)
        P.dma("sp", gam_d[t], gm, ds_o[s])
        P.act(E, cum, AF.Exp)
        P.tt(ob[:, CB_R:CB_R + 512], r, E, ALU.mult)
        P.act(E, cumx, AF.Exp)
        P.stt(ob[:, CB_A:CB_A + 512], kkn, -1.0, E, ALU.mult, ALU.mult)
        P.act(E, cum, AF.Exp, scale=-1.0)
        P.tt(ob[:, CB_B:CB_B + 512], bvec, E, ALU.mult)
        P.tt(ob[:, CB_K:CB_K + 512], k2, E, ALU.mult)
        P.act(E, rev, AF.Exp)
        P.tt(ob[:, CB_BH:CB_BH + 512], bvec, E, ALU.mult)
        P.tt(ob[:, CB_KH:CB_KH + 512], k2, E, ALU.mult)
        P.tt(sq, r, k2, ALU.mult)
        P.tt(sq, sq, r_k, ALU.mult)
        P.reduce(rks, sq.re("p (h k) -> p h k", h=8))
        P.tt(of[:, CF_BONUS:CF_BONUS + 512].re("p (h k) -> p h k", h=8), vx.re("p (h k) -> p h k", h=8),
             rks.unsq(2).bc([128, 8, 64]), ALU.mult)
        rows = slice(t * 128, (t + 1) * 128)
        P.dma("sp", prepB_d[rows, :], ob, ds_o[s])
        P.dma("sp", prepF_d[rows, :], of, ds_o[s])

    load(0)
    for t in range(NT):
        if t + 1 < NT:
            load(t + 1)
        compute(t)


def stage_even_chunk(P, S, c, prepB_d, prepF_d, gam_d, oT_d):
    P.stage_begin()
    NT = S // 128
    ds_c = P.dsem("ec_c")
    ident = P.sb([128, 128], BF16, "ident")
    P.dma("pool", ident, c["ident"], ds_c)

    def cst(name, shape):
        t = P.sb(shape, F32, name)
        P.dma("sp", t, c[name], ds_c)
        return t
    ML_s = cst("ml_strict", [128, 128])
    MU_s = cst("mu_strict", [128, 128])
    MU_i = cst("mu_incl", [128, 128])

    def bc(name, n):
        t = P.sb([128, n], F32, name)
        P.dma("sp", t, c[name].pbc(128), ds_c)
        return t
    ln_g = bc("rwkv_ln_g", 512)
    ln_b = bc("rwkv_ln_b", 512)
    norm_g = bc("gla_norm_g", 128)

    IB = [P.sb([128, WB], BF16, f"IB{s}") for s in range(2)]
    IF = [P.sb([128, WF], F32, f"IF{s}") for s in range(2)]
    GM = [P.sb([64, 24], F32, f"GMi{s}") for s in range(2)]
    ds_i = [P.dsem(f"ec_i{s}") for s in range(2)]

    def fm(name, nh=8):
        return P.sb([64, nh, 128], BF16, name)
    RT, AT, BT, KT = fm("RT"), fm("AT"), fm("BT"), fm("KT")
    RTm = [fm("RTm0"), fm("RTm1")]
    qtT, ktT = fm("qtT", 4), fm("ktT", 4)
    qtTm = [fm("qtTm0", 4), fm("qtTm1", 4)]

    def mat(name, nh=8):
        return P.sb([128, nh, 128], BF16, name)
    Pm = [mat("Pm0"), mat("Pm1")]
    PTm = [mat("PTm0"), mat("PTm1")]
    Qm = [mat("Qm0"), mat("Qm1")]
    AakT, ArbT, ArkT = mat("AakT"), mat("ArbT"), mat("ArkT")
    attnT = mat("attnT", 4)
    WtT = fm("WtT")
    AV = P.sb([128, 8, 64], BF16, "AV")
    Ut = P.sb([128, 8, 64], F32, "Ut")
    Usb = P.sb([128, 8, 64], BF16, "Usb")
    H = P.sb([64, 8, 64], F32, "H")
    Hb = [P.sb([64, 8, 64], BF16, f"Hb{i}") for i in range(3)]
    Sg = P.sb([64, 4, 128], F32, "Sg")
    Sgb = [P.sb([64, 4, 128], BF16, f"Sgb{i}") for i in range(3)]
    Ysb = P.sb([128, 8, 64], F32, "Ysb")
    ysq = P.sb([128, 8, 64], F32, "ysq")
    Osb = P.sb([128, 4, 128], F32, "Osb")
    s1 = P.sb([128, 8], F32, "s1")
    s2 = P.sb([128, 8], F32, "s2")
    MIX = P.sb([128, 1024], BF16, "MIX")
    mixT = [P.sb([128, 8, 128], BF16, f"mixT{s}") for s in range(2)]
    ds_m = [P.dsem(f"ec_m{s}") for s in range(2)]
    BHm = [P.sb([128, 512], BF16, f"BHm{i}") for i in range(2)]
    KHm = [P.sb([128, 512], BF16, f"KHm{i}") for i in range(2)]
    GKHm = [P.sb([128, 256], BF16, f"GKHm{i}") for i in range(2)]
    for tl in RTm + qtTm + BHm + KHm + GKHm + [Usb]:
        P.memset(tl, 0.0)
    P.memset(H, 0.0)
    P.memset(Hb[0], 0.0)
    P.memset(Sg, 0.0)
    P.memset(Sgb[0], 0.0)
    tp = P.bank(7, BF16, "tp")
    Yb = P.bank(6, F32, "Yb")
    Ob = P.bank(5, F32, "Ob")
    rb = [P.bank(i, F32, f"rb{i}") for i in range(5)]
    rbi = [0]

    def nb():
        b = rb[rbi[0] % 5]
        rbi[0] += 1
        return b
    oT_v = oT_d.re("(k two) d s -> (two d) k s", two=2)

    def load(t):
        s = t % 2
        rows = slice(t * 128, (t + 1) * 128)
        P.dma("sp", IB[s], prepB_d[rows, :], ds_i[s])
        P.dma("sp", IF[s], prepF_d[rows, :], ds_i[s])
        P.dma("sp", GM[s], gam_d[t], ds_i[s])

    def transpose_heads(dst, src, c0, nh, dstm=None):
        for h in range(nh):
            P.tr(tp[0:64, h * 128:(h + 1) * 128], src[:, c0 + h * 64:c0 + (h + 1) * 64], ident)
        tv = tp[0:64, 0:nh * 128].re("p (h t) -> p h t", h=nh)
        P.copy(dst, tv, eng="act")
        if dstm is not None and EC_VAR != 'b':
            P.copy(dstm[0][:, :, 0:64], dst[:, :, 0:64], eng="pool")
            P.copy(dstm[1][:, :, 64:128], dst[:, :, 64:128], eng="pool")

    def headmm(dst, lhs, rhs, mask, nh=8):
        for g in range(nh // 4):
            b = nb()
            for hh in range(4):
                h = g * 4 + hh
                P.mm(b[:, hh * 128:(hh + 1) * 128], lhs[:, h, :], rhs[:, h, :])
            bv = b.re("p (h t) -> p h t", h=4)
            if mask is None:
                P.copy(dst[:, g * 4:(g + 1) * 4, :], bv, eng="act")
            else:
                P.tt(dst[:, g * 4:(g + 1) * 4, :], bv, mask.unsq(1).bc([128, 4, 128]), ALU.mult)

    def compute(t):
        s = t % 2
        ib, iff, gm = IB[s], IF[s], GM[s]
        if EC_STOP <= 0:
            return
        transpose_heads(RT, ib, CB_R, 8, RTm)
        transpose_heads(AT, ib, CB_A, 8)
        transpose_heads(BT, ib, CB_B, 8)
        transpose_heads(KT, ib, CB_K, 8)
        if EC_VAR == 'a':
            return
        L0, M0 = PTm[0], Pm[0]
        headmm(L0, AT, BT, ML_s)
        headmm(M0, BT, AT, MU_s)
        headmm(AakT, KT, AT, MU_s)
        headmm(ArbT, BT, RT, MU_i)
        headmm(ArkT, KT, RT, MU_i)
        if EC_STOP <= 1:
            return
        P.tt(Qm[0], M0, ident.unsq(1).bc([128, 8, 128]), ALU.add)
        cur = 0
        for lvl in range(5):
            nxt = 1 - cur
            if lvl < 4:
                headmm(Pm[nxt], PTm[cur], Pm[cur], None)
            headmm(PTm[nxt], Pm[cur], PTm[cur], None)
            for g in range(2):
                b = nb()
                for hh in range(4):
                    h = g * 4 + hh
                    P.mm(b[:, hh * 128:(hh + 1) * 128], PTm[nxt][:, h, :], Qm[cur][:, h, :])
                P.tt(Qm[nxt][:, g * 4:(g + 1) * 4, :], b.re("p (h t) -> p h t", h=4), Qm[cur][:, g * 4:(g + 1) * 4, :], ALU.add)
            cur = nxt
        TT = Qm[cur]
        if EC_STOP <= 2:
            return
        for g in range(2):
            b = nb()
            for hh in range(4):
                h = g * 4 + hh
                P.mm(b[0:64, hh * 128:(hh + 1) * 128], ib[:, CB_A + h * 64:CB_A + (h + 1) * 64], TT[:, h, :])
            P.copy(WtT[:, g * 4:(g + 1) * 4, :], b[0:64, :].re("p (h t) -> p h t", h=4), eng="act")
        b = nb()
        for h in range(8):
            P.mm(b[:, h * 64:(h + 1) * 64], AakT[:, h, :], ib[:, CB_V + h * 64:CB_V + (h + 1) * 64])
        P.copy(AV, b.re("p (h v) -> p h v", h=8), eng="act")
        b = nb()
        for h in range(8):
            P.mm(b[:, h * 64:(h + 1) * 64], TT[:, h, :], AV[:, h, :])
        P.copy(Ut, b.re("p (h v) -> p h v", h=8), eng="act")
        if EC_STOP <= 3:
            return
        transpose_heads(qtT, ib, CB_GQ, 4, qtTm)
        transpose_heads(ktT, ib, CB_GK, 4)
        headmm(attnT, ktT, qtT, MU_i, nh=4)
        if EC_STOP <= 4:
            return
        for cc in range(2):
            rs = slice(cc * 64, (cc + 1) * 64)
            P.copy(BHm[cc][rs, :], ib[rs, CB_BH:CB_BH + 512], eng="pool")
            P.copy(KHm[cc][rs, :], ib[rs, CB_KH:CB_KH + 512], eng="pool")
            P.copy(GKHm[cc][rs, :], ib[rs, CB_GKH:CB_GKH + 256], eng="pool")
        for cc in range(2):
            rs = slice(cc * 64, (cc + 1) * 64)
            hb_in, hb_out = Hb[(2 * t + cc) % 3], Hb[(2 * t + cc + 1) % 3]
            sb_in, sb_out = Sgb[(2 * t + cc) % 3], Sgb[(2 * t + cc + 1) % 3]
            up = nb()
            for h in range(8):
                P.mm(up[:, h * 64:(h + 1) * 64], WtT[:, h, :], hb_in[:, h, :])
            P.tt(Usb[rs, :, :], up[rs, :].re("p (h v) -> p h v", h=8), Ut[rs, :, :], ALU.add)
            dh = nb()
            for h in range(8):
                P.mm(dh[0:64, h * 64:(h + 1) * 64], BHm[cc][:, h * 64:(h + 1) * 64], Usb[:, h, :],
                     start=True, stop=False)
                P.mm(dh[0:64, h * 64:(h + 1) * 64], KHm[cc][:, h * 64:(h + 1) * 64],
                     ib[:, CB_V + h * 64:CB_V + (h + 1) * 64], start=False, stop=True)
            gsel = gm[:, 0:16].re("p (h c) -> p h c", c=2)[:, :, cc]
            P.tt(H, H, gsel.unsq(2).bc([64, 8, 64]), ALU.mult)
            P.tt(H, H, dh[0:64, :].re("p (h v) -> p h v", h=8), ALU.add)
            P.copy(hb_out, H, eng="act")
            ds_ = nb()
            for h in range(4):
                P.mm(ds_[0:64, h * 128:(h + 1) * 128], GKHm[cc][:, h * 64:(h + 1) * 64],
                     ib[:, CB_GV + h * 128:CB_GV + (h + 1) * 128])
            gsel2 = gm[:, 16:24].re("p (h c) -> p h c", c=2)[:, :, cc]
            P.tt(Sg, Sg, gsel2.unsq(2).bc([64, 4, 128]), ALU.mult)
            P.tt(Sg, Sg, ds_[0:64, :].re("p (h v) -> p h v", h=4), ALU.add)
            P.copy(sb_out, Sg, eng="act")
        for h in range(8):
            ys_ = Yb[:, h * 64:(h + 1) * 64]
            P.mm(ys_, RTm[0][:, h, :], Hb[(2 * t) % 3][:, h, :], start=True, stop=False)
            P.mm(ys_, RTm[1][:, h, :], Hb[(2 * t + 1) % 3][:, h, :], start=False, stop=False)
            P.mm(ys_, ArbT[:, h, :], Usb[:, h, :], start=False, stop=False)
            P.mm(ys_, ArkT[:, h, :], ib[:, CB_V + h * 64:CB_V + (h + 1) * 64], start=False, stop=True)
        for h in range(4):
            os_ = Ob[:, h * 128:(h + 1) * 128]
            P.mm(os_, qtTm[0][:, h, :], Sgb[(2 * t) % 3][:, h, :], start=True, stop=False)
            P.mm(os_, qtTm[1][:, h, :], Sgb[(2 * t + 1) % 3][:, h, :], start=False, stop=False)
            P.mm(os_, attnT[:, h, :], ib[:, CB_GV + h * 128:CB_GV + (h + 1) * 128], start=False, stop=True)
        if EC_STOP <= 5:
            return
        P.copy(Ysb, Yb.re("p (h v) -> p h v", h=8), eng="act")
        P.reduce(s1, Ysb)
        P.tt(ysq, Ysb, Ysb, ALU.mult)
        P.reduce(s2, ysq)
        P.ts(s1, s1, 1.0 / 64, ALU.mult)
        P.stt(ysq[:, :, 0], s1, -1.0, s1, ALU.mult, ALU.mult)
        P.stt(s2, s2, 1.0 / 64, ysq[:, :, 0], ALU.mult, ALU.add)
        _r = rsqrt_lnexp
        _r(P, s2, s2, add=64e-5)
        P.tt(Ysb, Ysb, s1.unsq(2).bc([128, 8, 64]), ALU.subtract)
        P.tt(Ysb, Ysb, s2.unsq(2).bc([128, 8, 64]), ALU.mult)
        yf = Ysb.re("p h v -> p (h v)")
        P.tt(yf, yf, ln_g, ALU.mult)
        P.tt(yf, yf, ln_b, ALU.add)
        P.tt(yf, yf, iff[:, CF_BONUS:CF_BONUS + 512], ALU.add)
        P.tt(MIX[:, 512:1024], yf, iff[:, CF_G:CF_G + 512], ALU.mult)
        if EC_STOP <= 6:
            return
        P.copy(Osb, Ob.re("p (h v) -> p h v", h=4), eng="act")
        osq = ysq.re("p h v -> p (h v)").re("p (h v) -> p h v", h=4)
        P.tt(osq, Osb, Osb, ALU.mult)
        P.reduce(s1[:, 0:4], osq)
        _r(P, s1[:, 0:4], s1[:, 0:4], scale_in=1.0 / 128, add=1e-6)
        P.tt(Osb, Osb, s1[:, 0:4].unsq(2).bc([128, 4, 128]), ALU.mult)
        P.tt(Osb, Osb, norm_g.unsq(1).bc([128, 4, 128]), ALU.mult)
        P.tt(MIX[:, 0:512], Osb.re("p h v -> p (h v)"), iff[:, CF_GS:CF_GS + 512], ALU.mult)
        if EC_STOP <= 7:
            return
        mt = mixT[s]
        for k in range(8):
            P.tr(tp[:, k * 128:(k + 1) * 128], MIX[:, k * 128:(k + 1) * 128], ident)
        P.copy(mt, tp.re("p (k t) -> p k t", k=8), eng="act")
        P.dma("sp", oT_v[:, :, t * 128:(t + 1) * 128], mt, ds_m[s])

    load(0)
    for t in range(NT):
        if t + 1 < NT:
            load(t + 1)
        compute(t)


NCORES = 8


def host_consts():
    c = {}
    c["ident"] = np.eye(128, dtype=np.float32)
    idx = np.arange(128)
    same = (idx[:, None] // 64) == (idx[None, :] // 64)
    lt = idx[:, None] < idx[None, :]
    le = idx[:, None] <= idx[None, :]
    gt = idx[:, None] > idx[None, :]
    EC = math.exp(-0.5)
    c["tri_incl"] = (-EC * (same & le)).astype(np.float32)
    c["tri_excl"] = (-EC * (same & lt)).astype(np.float32)
    c["tri_rev"] = (-EC * (same & gt)).astype(np.float32)
    c["trig_incl"] = (-(1.0 / 16) * (same & le)).astype(np.float32)
    c["trig_rev"] = (-(1.0 / 16) * (same & gt)).astype(np.float32)
    cb = np.zeros((128, 2), np.float32)
    cb[0:64, 0] = 1
    cb[64:128, 1] = 1
    c["cb"] = (-EC * cb).astype(np.float32)
    c["cbg"] = (-(1.0 / 16) * cb).astype(np.float32)
    c["ml_strict"] = (same & gt).astype(np.float32)
    c["mu_strict"] = (same & lt).astype(np.float32)
    c["mu_incl"] = (same & le).astype(np.float32)
    c["invf"] = (10000.0 ** (-np.arange(0, 32, 2, dtype=np.float32) / 32)).astype(np.float32)
    c["ones"] = np.ones((128, 64), np.float32)
    return c


CONST_SHAPES = dict(ident=[128, 128], tri_incl=[128, 128], tri_excl=[128, 128], tri_rev=[128, 128],
                    trig_incl=[128, 128], trig_rev=[128, 128], cb=[128, 2], cbg=[128, 2],
                    ml_strict=[128, 128], mu_strict=[128, 128], mu_incl=[128, 128], invf=[16], ones=[128, 64])

WEIGHT_SHAPES = dict(
    even_w_in=[1024, 3376], gla_gate_w2=[16, 256], gla_gate_b=[256], gla_norm_g=[128], rwkv_mu=[1824],
    rwkv_w0=[512], rwkv_w2=[64, 512], rwkv_a0=[512], rwkv_w2_=None, rwkv_a2=[64, 512], rwkv_g2=[160, 512],
    rwkv_k_k=[512], rwkv_k_a=[512], rwkv_r_k=[512], rwkv_ln_g=[512], rwkv_ln_b=[512], even_w_out=[1024, 1024],
    mla_w_in=[1024, 1056], mla_q_norm_g=[768], mla_w_q_b=[768, 1536], mla_kv_norm_g=[256], mla_w_kv_b=[256, 2048],
    mla_w_out=[1024, 1024], ffn_gu0=[1024, 5632], ffn_gu1=[1024, 5632], ffn_d0=[2816, 1024], ffn_d1=[2816, 1024],
    ln_g00=[1024], ln_g01=[1024], ln_g10=[1024], ln_g11=[1024], ln_b00=[1024], ln_b01=[1024], ln_b10=[1024], ln_b11=[1024])
del WEIGHT_SHAPES["rwkv_w2_"]


def host_weights(inp):
    w = {}
    f = lambda a: np.ascontiguousarray(np.asarray(a, dtype=np.float32))
    ew = np.asarray(inp["even_w_in"][0])
    o = 1552
    perm = np.concatenate([np.arange(0, 1536), o + np.arange(0, 1824), np.arange(1536, 1552)])
    w["even_w_in"] = f(ew[:, perm])
    for k in ("gla_gate_w2", "gla_gate_b", "gla_norm_g", "rwkv_mu", "rwkv_w0", "rwkv_w2", "rwkv_a0", "rwkv_a2",
              "rwkv_g2", "rwkv_k_k", "rwkv_k_a", "rwkv_ln_g", "rwkv_ln_b", "even_w_out", "mla_w_in",
              "mla_q_norm_g", "mla_kv_norm_g", "mla_w_kv_b", "mla_w_out"):
        w[k] = f(inp[k][0])
    w["rwkv_r_k"] = f(np.asarray(inp["rwkv_r_k"][0]).reshape(512))
    wq = np.asarray(inp["mla_w_q_b"][0]).reshape(768, 16, 96)
    w["mla_w_q_b"] = f(np.concatenate([wq[:, :, 0:64].reshape(768, 1024), wq[:, :, 64:96].reshape(768, 512)], 1))
    for i in range(2):
        gu = np.asarray(inp["ffn_w_gate_up"][i])
        w[f"ffn_gu{i}"] = f(gu.reshape(1024, 2, 22, 128).transpose(0, 2, 1, 3).reshape(1024, 5632))
        w[f"ffn_d{i}"] = f(inp["ffn_w_down"][i])
        for j in range(2):
            w[f"ln_g{i}{j}"] = f(inp["ln_g"][i, j])
            w[f"ln_b{i}{j}"] = f(inp["ln_b"][i, j])
    return w


def build(S, debug=False, stages=("ep", "ec", "eo", "f0", "mp", "at", "mo", "f1")):
    nc = bass.Bass("TRN2", target_bir_lowering=False)
    with ExitStack() as st:
        P = Prog(nc, st)
        P.init_mem()
        kind_dbg = "ExternalOutput" if debug else "Internal"
        x = P.dram("x", [S, 1024], F32, kind="ExternalInput")
        pos = P.dram("pos", [128, S // 128], I32, kind="ExternalInput")
        c = {k: P.dram(k, v, F32, kind="ExternalInput") for k, v in CONST_SHAPES.items()}
        w = {k: P.dram(k, v, F32, kind="ExternalInput") for k, v in WEIGHT_SHAPES.items()}
        c.update(w)
        prepB = P.dram("prepB", [S, WB], BF16)
        prepF = P.dram("prepF", [S, WF], F32)
        gam = P.dram("gam", [S // 128, 64, 24], F32)
        oT = P.dram("oT", [16, 64, S], BF16)
        oT2 = P.dram("oT2", [16, 64, S], BF16)
        qT = P.dram("qT", [16, 96, S], BF16)
        kT = P.dram("kT", [16, 96, S], BF16)
        vA = P.dram("vA", [S, 16 * 65], BF16)
        x1 = P.dram("x1", [S, 1024], F32, kind=kind_dbg if "eo" in stages else "ExternalInput")
        x2 = P.dram("x2", [S, 1024], F32, kind=kind_dbg if "f0" in stages else "ExternalInput")
        x3 = P.dram("x3", [S, 1024], F32, kind=kind_dbg if "mo" in stages else "ExternalInput")
        out = P.dram("out", [S, 1024], F32, kind="ExternalOutput")
        if "ep" in stages:
            stage_even_prep(P, S, x, c, prepB, prepF, gam)
        if "ec" in stages:
            stage_even_chunk(P, S, c, prepB, prepF, gam, oT)
        if "eo" in stages:
            stage_mla_out(P, S, oT, x, c["even_w_out"], c["ln_g00"], c["ln_b00"], x1)
        if "f0" in stages:
            stage_ffn(P, S, x1, c["ffn_gu0"], c["ffn_d0"], c["ln_g01"], c["ln_b01"], x2, c["ident"])
        if "mp" in stages:
            stage_mla_proj(P, S, x2, pos, c["invf"], c["mla_w_in"], c["mla_q_norm_g"], c["mla_w_q_b"],
                           c["mla_kv_norm_g"], c["mla_w_kv_b"], c["ident"], qT, kT, vA)
        if "at" in stages:
            stage_attn(P, S, qT, kT, vA, oT2, c["ones"])
        if "mo" in stages:
            stage_mla_out(P, S, oT2, x2, c["mla_w_out"], c["ln_g10"], c["ln_b10"], x3)
        if "f1" in stages:
            stage_ffn(P, S, x3, c["ffn_gu1"], c["ffn_d1"], c["ln_g11"], c["ln_b11"], out, c["ident"])
        P.barrier()
        P.emit()
        print({e: len(P.ins[e]) for e in ENGS}, flush=True)
    return nc


_NC_CACHE = {}


def kernel(**inp):
    x = np.asarray(inp["x"], dtype=np.float32)
    B, S, _ = x.shape
    pos = np.asarray(inp["positions"]).astype(np.int32)
    consts = host_consts()
    wts = host_weights(inp)
    if S not in _NC_CACHE:
        _NC_CACHE[S] = build(S)
    nc = _NC_CACHE[S]
    in_maps = []
    for b in range(B):
        m = dict(x=np.ascontiguousarray(x[b]),
                 pos=np.ascontiguousarray(pos[b].reshape(S // 128, 128).T))
        m.update(consts)
        m.update(wts)
        in_maps.append(m)
    res = run_bass_kernel_spmd(nc, in_maps, core_ids=list(range(B)))
    return np.stack([np.asarray(r["out"], dtype=np.float32) for r in res.results], 0)
```
